# Optimizing a Trainium2 kernel written in Bass

```python
import jax, jax.numpy as jnp
from jax import lax
import numpy as np

D_MODEL = 1024
BATCH = 8
SEQ = 2048
DEPTH = 1
DEC_BATCH = 128
DEC_SEQ = 8
PAST_LEN = 16384
PAGE_SIZE = 128

RW_HEADS = 8
RW_HEAD_DIM = 64
RW_WIDTH = RW_HEADS * RW_HEAD_DIM
RW_DECAY_RANK = 64
RW_AAA_RANK = 64
RW_GATE_RANK = 128
RW_COLS = 3 * RW_WIDTH + RW_DECAY_RANK + RW_AAA_RANK + RW_GATE_RANK
RW_GN_EPS = 64e-5
HG_HEADS = 4
HG_DK = 128
HG_DV = 128
HG_FDIM = HG_HEADS * HG_DK
HG_VDIM = HG_HEADS * HG_DV
HG_COLS = 2 * HG_FDIM + 2 * HG_VDIM
HG_CHUNK = 32
GATE_COLS = 2 * D_MODEL
IN_COLS = RW_COLS + HG_COLS + GATE_COLS
PEER_KEYS = 128
PEER_EXPERTS = PEER_KEYS * PEER_KEYS
PEER_HEADS = 8
PEER_TOPK = 16
PEER_QDIM = 256
PEER_HALF = PEER_QDIM // 2
PEER_BLOCK = 128
NORM_EPS = 1e-6

kernel_name = "rwkv7_hgrn2_peer_hybrid_step"


def _rmsnorm(x, g):
    x32 = x.astype(jnp.float32)
    y = x32 * lax.rsqrt(jnp.mean(x32 * x32, axis=-1, keepdims=True) + NORM_EPS)
    return (y * g.astype(jnp.float32)).astype(x.dtype)


def _rwkv7(zs, wkv0, w0, w2, a0, a2, g2, k_k, k_a, r_k, ln_w, ln_b):
    B, T, _ = zs.shape
    f32 = jnp.float32
    zs = zs.astype(f32)
    r, k, v, d_low, a_low, g_low = jnp.split(
        zs, [RW_WIDTH, 2 * RW_WIDTH, 3 * RW_WIDTH, 3 * RW_WIDTH + RW_DECAY_RANK,
             3 * RW_WIDTH + RW_DECAY_RANK + RW_AAA_RANK], axis=-1)
    w_log = -jax.nn.softplus(-(w0.astype(f32) + jnp.tanh(d_low) @ w2.astype(f32))) - 0.5
    decay = jnp.exp(-jnp.exp(w_log))
    a = jax.nn.sigmoid(a0.astype(f32) + a_low @ a2.astype(f32))
    g = jax.nn.sigmoid(g_low) @ g2.astype(f32)

    def heads(t):
        return t.reshape(B, T, RW_HEADS, RW_HEAD_DIM)

    kk = heads(k * k_k.astype(f32))
    kk = kk / jnp.maximum(jnp.linalg.norm(kk, axis=-1, keepdims=True), 1e-12)
    k = k * (1.0 + (a - 1.0) * k_a.astype(f32))
    r_h, k_h, v_h, w_h, a_h = heads(r), heads(k), heads(v), heads(decay), heads(a)

    def step(S, inp):
        r_t, w_t, k_t, v_t, kk_t, a_t = inp
        sa = jnp.einsum('bhvk,bhk->bhv', S, -kk_t)
        S = (S * w_t[:, :, None, :] + sa[..., None] * (kk_t * a_t)[:, :, None, :]
             + v_t[..., None] * k_t[:, :, None, :])
        return S, jnp.einsum('bhvk,bhk->bhv', S, r_t)

    def tm(t):
        return jnp.moveaxis(t, 1, 0)

    S_fin, o = lax.scan(step, wkv0.astype(f32),
                        (tm(r_h), tm(w_h), tm(k_h), tm(v_h), tm(kk), tm(a_h)))
    o = jnp.moveaxis(o, 0, 1)
    mu = jnp.mean(o, axis=-1, keepdims=True)
    var = jnp.mean(jnp.square(o - mu), axis=-1, keepdims=True)
    o = ((o - mu) * lax.rsqrt(var + RW_GN_EPS)).reshape(B, T, RW_WIDTH)
    o = o * ln_w.astype(f32) + ln_b.astype(f32)
    bonus = jnp.sum(r_h * k_h * r_k.astype(f32), axis=-1, keepdims=True) * v_h
    o = (o + bonus.reshape(B, T, RW_WIDTH)) * g
    return o, S_fin


def _hgrn2(q, f_logit, i_in, g_out, hg0, lb, norm_g):
    B, T, _ = q.shape
    f32 = jnp.float32
    f = lb + (1.0 - lb) * jax.nn.sigmoid(f_logit.astype(f32))
    logf = jnp.log(f)
    k = 1.0 - f
    L = min(HG_CHUNK, T)
    n_ch = -(-T // L)
    pad = n_ch * L - T

    def chunks(t, d):
        t = jnp.pad(t.astype(f32), ((0, 0), (0, pad), (0, 0)))
        return t.reshape(B, n_ch, L, HG_HEADS, d).transpose(1, 0, 3, 2, 4)

    qc, kc, lc, vc = chunks(q, HG_DK), chunks(k, HG_DK), chunks(logf, HG_DK), chunks(i_in, HG_DV)
    b = jnp.cumsum(lc, axis=3)
    b_last = b[:, :, :, -1:, :]
    q_in = qc * jnp.exp(b)
    k_in = kc * jnp.exp(-b)
    k_out = kc * jnp.exp(b_last - b)
    causal = jnp.tril(jnp.ones((L, L), dtype=bool))
    A = jnp.where(causal, jnp.einsum('nbhld,nbhsd->nbhls', q_in, k_in), 0.0)
    o_intra = jnp.einsum('nbhls,nbhsv->nbhlv', A, vc)

    def step(S, inp):
        q_t, k_t, v_t, d_t = inp
        o_t = jnp.einsum('bhld,bhdv->bhlv', q_t, S)
        S = S * d_t[..., None] + jnp.einsum('bhsd,bhsv->bhdv', k_t, v_t)
        return S, o_t

    S_fin, o_inter = lax.scan(step, hg0.astype(f32),
                              (q_in, k_out, vc, jnp.exp(b_last[:, :, :, 0, :])))
    o = (o_intra + o_inter).transpose(1, 0, 3, 2, 4).reshape(B, n_ch * L, HG_HEADS, HG_DV)[:, :T]
    o = o * lax.rsqrt(jnp.mean(o * o, axis=-1, keepdims=True) + NORM_EPS) * norm_g.astype(f32)
    o = o.reshape(B, T, HG_VDIM) * jax.nn.silu(g_out.astype(f32))
    return o, S_fin


def _peer(xn, w_q, keys, u_tab, v_tab):
    B, T, D = xn.shape
    n_tok = B * T
    blk = min(PEER_BLOCK, n_tok)
    n_blk = -(-n_tok // blk)
    xf = jnp.pad(xn.reshape(n_tok, D), ((0, n_blk * blk - n_tok), (0, 0))).reshape(n_blk, blk, D)

    def one_block(xt):
        q = jnp.einsum('td,dc->tc', xt, w_q).reshape(blk, PEER_HEADS, 2, PEER_HALF)
        s = jnp.einsum('thcd,hckd->thck', q, keys).astype(jnp.float32)
        v1, i1 = lax.top_k(s[:, :, 0], PEER_TOPK)
        v2, i2 = lax.top_k(s[:, :, 1], PEER_TOPK)
        cand = (v1[..., :, None] + v2[..., None, :]).reshape(blk, PEER_HEADS, PEER_TOPK * PEER_TOPK)
        cv, ci = lax.top_k(cand, PEER_TOPK)
        idx = (jnp.take_along_axis(i1, ci // PEER_TOPK, axis=-1) * PEER_KEYS
               + jnp.take_along_axis(i2, ci % PEER_TOPK, axis=-1))
        gates = jax.nn.softmax(cv, axis=-1)
        act = jax.nn.gelu(jnp.einsum('thkd,td->thk', u_tab[idx], xt).astype(jnp.float32),
                          approximate=False)
        return jnp.einsum('thk,thkd->td', (gates * act).astype(xt.dtype), v_tab[idx])

    out = lax.map(one_block, xf)
    return out.reshape(n_blk * blk, D)[:n_tok].reshape(B, T, D)


def _layer(x, shift0, wkv0, hg0, lb, norm_mix_g, w_in, rw_mu, rw_w0, rw_w2, rw_a0, rw_a2, rw_g2,
           rw_k_k, rw_k_a, rw_r_k, rw_ln_w, rw_ln_b, hg_norm_g, w_up_a, w_up_b, w_out,
           norm_ffn_g, peer_w_q, peer_keys, peer_u, peer_v):
    xn = _rmsnorm(x, norm_mix_g)
    z = jnp.einsum('btd,dc->btc', xn, w_in)
    z_rw = z[..., :RW_COLS]
    z_first_prev = jnp.einsum('bd,dc->bc', shift0, w_in[:, :RW_COLS])
    z_rw_prev = jnp.concatenate([z_first_prev[:, None], z_rw[:, :-1]], axis=1)
    z_rw_shift = z_rw + (z_rw_prev - z_rw) * rw_mu
    o_a, wkv_new = _rwkv7(z_rw_shift, wkv0, rw_w0, rw_w2, rw_a0, rw_a2, rw_g2,
                          rw_k_k, rw_k_a, rw_r_k, rw_ln_w, rw_ln_b)
    z_hg = z[..., RW_COLS:RW_COLS + HG_COLS]
    hq, hf, hi, hgate = jnp.split(z_hg, [HG_FDIM, 2 * HG_FDIM, 2 * HG_FDIM + HG_VDIM], axis=-1)
    o_b, hg_new = _hgrn2(hq, hf, hi, hgate, hg0, lb, hg_norm_g)
    gate_a, gate_b = jnp.split(z[..., RW_COLS + HG_COLS:], 2, axis=-1)
    merged = (jax.nn.sigmoid(gate_a.astype(jnp.float32)) * (o_a @ w_up_a.astype(jnp.float32))
              + jax.nn.sigmoid(gate_b.astype(jnp.float32)) * (o_b @ w_up_b.astype(jnp.float32)))
    x = x + jnp.einsum('btd,de->bte', merged.astype(x.dtype), w_out)
    x = x + _peer(_rmsnorm(x, norm_ffn_g), peer_w_q, peer_keys, peer_u, peer_v)
    return x, xn[:, -1], wkv_new.astype(x.dtype), hg_new.astype(x.dtype)


def setup_inputs(seed: int = 0) -> dict:
    key = jax.random.key(seed)
    ks = jax.random.split(key, 32)
    f32 = jnp.float32

    def nrm(k, shape, scale):
        return jax.random.normal(k, shape, f32) * scale

    return {
        "x_prompt": nrm(ks[0], (BATCH, SEQ, D_MODEL), 1.0),
        "x_sample": nrm(ks[1], (DEC_BATCH, DEC_SEQ, D_MODEL), 1.0),
        "state_rwkv_shift": nrm(ks[2], (DEPTH, DEC_BATCH, D_MODEL), 1.0),
        "state_rwkv_wkv": nrm(ks[3], (DEPTH, DEC_BATCH, RW_HEADS, RW_HEAD_DIM, RW_HEAD_DIM), 0.3),
        "state_hgrn": nrm(ks[4], (DEPTH, DEC_BATCH, HG_HEADS, HG_DK, HG_DV), 0.3),
        "norm_mix_g": 1.0 + nrm(ks[5], (DEPTH, D_MODEL), 0.02),
        "w_in": nrm(ks[6], (DEPTH, D_MODEL, IN_COLS), D_MODEL ** -0.5),
        "rw_mu": jax.random.uniform(ks[7], (DEPTH, RW_COLS), f32),
        "rw_w0": nrm(ks[8], (DEPTH, RW_WIDTH), 0.5) - 0.5,
        "rw_w2": nrm(ks[9], (DEPTH, RW_DECAY_RANK, RW_WIDTH), 0.1),
        "rw_a0": nrm(ks[10], (DEPTH, RW_WIDTH), 0.1),
        "rw_a2": nrm(ks[11], (DEPTH, RW_AAA_RANK, RW_WIDTH), 0.5 * RW_AAA_RANK ** -0.5),
        "rw_g2": nrm(ks[12], (DEPTH, RW_GATE_RANK, RW_WIDTH), RW_GATE_RANK ** -0.5),
        "rw_k_k": 0.85 + nrm(ks[13], (DEPTH, RW_WIDTH), 0.05),
        "rw_k_a": 1.0 + nrm(ks[14], (DEPTH, RW_WIDTH), 0.05),
        "rw_r_k": nrm(ks[15], (DEPTH, RW_HEADS, RW_HEAD_DIM), 0.1),
        "rw_ln_w": 1.0 + nrm(ks[16], (DEPTH, RW_WIDTH), 0.02),
        "rw_ln_b": nrm(ks[17], (DEPTH, RW_WIDTH), 0.02),
        "hg_lb_logits": 1.0 + nrm(ks[18], (DEPTH + 1, HG_FDIM), 0.1),
        "hg_norm_g": 1.0 + nrm(ks[19], (DEPTH, HG_DV), 0.02),
        "w_up_a": nrm(ks[20], (DEPTH, RW_WIDTH, D_MODEL), RW_WIDTH ** -0.5),
        "w_up_b": nrm(ks[21], (DEPTH, HG_VDIM, D_MODEL), HG_VDIM ** -0.5),
        "w_out": nrm(ks[22], (DEPTH, D_MODEL, D_MODEL), D_MODEL ** -0.5),
        "norm_ffn_g": 1.0 + nrm(ks[23], (DEPTH, D_MODEL), 0.02),
        "peer_w_q": nrm(ks[24], (DEPTH, D_MODEL, PEER_HEADS * PEER_QDIM), D_MODEL ** -0.5),
        "peer_keys": nrm(ks[25], (DEPTH, PEER_HEADS, 2, PEER_KEYS, PEER_HALF), PEER_HALF ** -0.5),
        "peer_u": nrm(ks[26], (DEPTH, PEER_EXPERTS, D_MODEL), D_MODEL ** -0.5),
        "peer_v": nrm(ks[27], (DEPTH, PEER_EXPERTS, D_MODEL), 0.1),
        "norm_final_g": 1.0 + nrm(ks[28], (D_MODEL,), 0.02),
    }


def reference(x_prompt, x_sample, state_rwkv_shift, state_rwkv_wkv, state_hgrn,
              norm_mix_g, w_in, rw_mu, rw_w0, rw_w2, rw_a0, rw_a2, rw_g2, rw_k_k, rw_k_a,
              rw_r_k, rw_ln_w, rw_ln_b, hg_lb_logits, hg_norm_g, w_up_a, w_up_b, w_out,
              norm_ffn_g, peer_w_q, peer_keys, peer_u, peer_v, norm_final_g):
    b_p = x_prompt.shape[0]
    dt = x_prompt.dtype
    lbs = jnp.cumsum(jax.nn.softmax(hg_lb_logits.astype(jnp.float32), axis=0), axis=0)
    xp, xs = x_prompt, x_sample
    p_shift, p_wkv, p_hg, s_shift, s_wkv, s_hg = [], [], [], [], [], []
    for l in range(DEPTH):
        w_l = (norm_mix_g[l], w_in[l], rw_mu[l], rw_w0[l], rw_w2[l], rw_a0[l], rw_a2[l], rw_g2[l],
               rw_k_k[l], rw_k_a[l], rw_r_k[l], rw_ln_w[l], rw_ln_b[l], hg_norm_g[l],
               w_up_a[l], w_up_b[l], w_out[l], norm_ffn_g[l], peer_w_q[l], peer_keys[l],
               peer_u[l], peer_v[l])
        xp, sh, wk, hg = _layer(
            xp, jnp.zeros((b_p, D_MODEL), dt),
            jnp.zeros((b_p, RW_HEADS, RW_HEAD_DIM, RW_HEAD_DIM), dt),
            jnp.zeros((b_p, HG_HEADS, HG_DK, HG_DV), dt), lbs[l], *w_l)
        p_shift.append(sh)
        p_wkv.append(wk)
        p_hg.append(hg)
        xs, sh, wk, hg = _layer(xs, state_rwkv_shift[l], state_rwkv_wkv[l], state_hgrn[l], lbs[l], *w_l)
        s_shift.append(sh)
        s_wkv.append(wk)
        s_hg.append(hg)
    y_prompt = _rmsnorm(xp, norm_final_g)
    y_sample = _rmsnorm(xs, norm_final_g)
    return (y_prompt, y_sample, jnp.stack(p_shift), jnp.stack(p_wkv), jnp.stack(p_hg),
            jnp.stack(s_shift), jnp.stack(s_wkv), jnp.stack(s_hg))
```

```python
import math
from contextlib import ExitStack

import numpy as np
import concourse.bass as bass
import concourse.mybir as mybir
from concourse.bass_utils import run_bass_kernel_spmd

F32 = mybir.dt.float32
BF16 = mybir.dt.bfloat16
I32 = mybir.dt.int32
U32 = mybir.dt.uint32
AF = mybir.ActivationFunctionType
ALU = mybir.AluOpType
AX = mybir.AxisListType

ENGS = ("pe", "act", "dve", "pool", "sp")
NT = 17
NTOK = NT * 128
EPS = 1e-6
GN_EPS = 64e-5


class FW:
    def __init__(self, nc, es, n_chan=48):
        self.nc = nc
        self.es = es
        self.ops = {e: [] for e in ENGS}
        self.sem = {e: es.enter_context(nc.semaphore("s_" + e)) for e in ENGS}
        self.cnt = {e: 0 for e in ENGS}
        self.known = {e: {} for e in ENGS}
        self.state = {}
        self.chans = [es.enter_context(nc.semaphore("c%d" % i)) for i in range(n_chan)]
        self.chan_cnt = [0] * n_chan
        self.chan_rr = 0
        self.stopped = False

    def sb(self, name, shape, dt):
        return self.es.enter_context(self.nc.sbuf_tensor(name, list(shape), dt))

    def ps(self, name, shape, dt):
        return self.es.enter_context(self.nc.psum_tensor(name, list(shape), dt))

    def _deps(self, eng, R, W):
        toks = []
        for r in R:
            st = self.state.get(r)
            if st and st[0]:
                toks.append(st[0])
        for w in W:
            st = self.state.get(w)
            if st:
                if st[0]:
                    toks.append(st[0])
                toks.extend(st[1])
        waits = {}
        kn = self.known[eng]
        for (sem, val, src) in toks:
            if eng == "pe" and src == "pe":
                continue
            if kn.get(id(sem), 0) >= val:
                continue
            if waits.get(id(sem), (None, 0))[1] < val:
                waits[id(sem)] = (sem, val)
        for k, (sem, val) in waits.items():
            kn[k] = val
        return list(waits.values())

    def _commit(self, tok, R, W):
        for r in R:
            st = self.state.setdefault(r, [None, []])
            st[1].append(tok)
        for w in W:
            self.state[w] = [tok, []]

    def op(self, eng, fn, R=(), W=()):
        if self.stopped:
            return
        R = list(R)
        W = list(W)
        W += [r for r in R if isinstance(r, tuple) and r[0] == "B" and r not in W]
        waits = self._deps(eng, R, W)
        self.cnt[eng] += 1
        tok = (self.sem[eng], self.cnt[eng], eng)
        self.ops[eng].append((waits, fn, (self.sem[eng], 1)))
        self._commit(tok, R, W)

    def dma(self, q, fn, R=(), W=()):
        if self.stopped:
            return
        R = list(R)
        W = list(W)
        waits = self._deps(q, R, W)
        chan = self.chan_rr
        self.chan_rr = (self.chan_rr + 1) % len(self.chans)
        prev = self.chan_cnt[chan]
        if prev and self.known[q].get(id(self.chans[chan]), 0) < prev:
            waits = [w for w in waits if w[0] is not self.chans[chan]] + [(self.chans[chan], prev)]
            self.known[q][id(self.chans[chan])] = prev
        self.chan_cnt[chan] += 16
        tok = (self.chans[chan], self.chan_cnt[chan], "dma")
        self.ops[q].append((waits, fn, (self.chans[chan], 16)))
        self._commit(tok, R, W)

    def barrier(self):
        for e in ENGS:
            waits = []
            for o in ENGS:
                if o != e and self.cnt[o] and self.known[e].get(id(self.sem[o]), 0) < self.cnt[o]:
                    waits.append((self.sem[o], self.cnt[o]))
                    self.known[e][id(self.sem[o])] = self.cnt[o]
            for i, c in enumerate(self.chans):
                if self.chan_cnt[i] and self.known[e].get(id(c), 0) < self.chan_cnt[i]:
                    waits.append((c, self.chan_cnt[i]))
                    self.known[e][id(c)] = self.chan_cnt[i]
            if waits:
                self.ops[e].append((waits, None, None))
        self.state = {}

    def finish(self, eng="sp"):
        waits = []
        for o in ENGS:
            if self.cnt[o] and o != eng:
                waits.append((self.sem[o], self.cnt[o]))
        for i, c in enumerate(self.chans):
            if self.chan_cnt[i]:
                waits.append((c, self.chan_cnt[i]))
        self.ops[eng].append((waits, None, None))

    def replay(self):
        ops = self.ops

        def run(engname, eng):
            for (waits, fn, inc) in ops[engname]:
                for (sem, val) in waits:
                    eng.wait_ge(sem, val)
                if fn is not None:
                    fn(eng).then_inc(inc[0], inc[1])

        with self.nc.Block() as block:
            @block.tensor
            def _(e):
                run("pe", e)

            @block.scalar
            def _(e):
                run("act", e)

            @block.vector
            def _(e):
                run("dve", e)

            @block.gpsimd
            def _(e):
                run("pool", e)

            @block.sync
            def _(e):
                run("sp", e)


WG = [(0, 512), (512, 512), (1024, 512), (1536, 256), (1792, 512), (2304, 512), (2816, 512),
      (3328, 512), (3840, 512), (4352, 512), (4864, 512), (5376, 512)]


class _Stop(Exception):
    pass


class Prog:
    def __init__(self, debug=False, tiles=None, do_peer=True, stop=None):
        self.debug = debug
        self.stop = stop
        self.tiles = list(range(NT)) if tiles is None else list(tiles)
        self.do_peer = do_peer
        self.held = set()
        self.dbg_names = []
        nc = self.nc = bass.Bass("TRN2", target_bir_lowering=False)
        di = lambda n, s, dt=F32: nc.dram_tensor(n, list(s), dt, kind="ExternalInput").ap()
        do = lambda n, s, dt=F32: nc.dram_tensor(n, list(s), dt, kind="ExternalOutput").ap()
        ds = lambda n, s, dt=F32: nc.dram_tensor(n, list(s), dt, kind="Internal").ap()
        self.xin = di("xin", [NTOK, 1024])
        self.shift0 = di("shift0", [16, 1024])
        self.wkv0 = di("wkv0", [16, 8, 64, 64])
        self.hg0 = di("hg0", [16, 4, 128, 128])
        self.g_mix = di("norm_mix_g", [1024])
        self.w_in = di("w_in", [1024, 5888])
        self.rw_mu = di("rw_mu", [1792])
        self.rw_w0 = di("rw_w0", [512])
        self.rw_w2 = di("rw_w2", [64, 512])
        self.rw_a0 = di("rw_a0", [512])
        self.rw_a2 = di("rw_a2", [64, 512])
        self.rw_g2 = di("rw_g2", [128, 512])
        self.rw_k_k = di("rw_k_k", [512])
        self.rw_k_a = di("rw_k_a", [512])
        self.rw_r_k = di("rw_r_k", [512])
        self.rw_ln_w = di("rw_ln_w", [512])
        self.rw_ln_b = di("rw_ln_b", [512])
        self.hg_lb = di("hg_lb_logits", [2, 512])
        self.hg_ng = di("hg_norm_g", [128])
        self.w_up_a = di("w_up_a", [512, 1024])
        self.w_up_b = di("w_up_b", [512, 1024])
        self.w_out = di("w_out", [1024, 1024])
        self.g_ffn = di("norm_ffn_g", [1024])
        self.w_q = di("peer_w_q", [1024, 2048])
        self.keys = di("peer_keys", [16, 128, 128])
        self.pu = di("peer_u", [16384, 1024])
        self.pv = di("peer_v", [16384, 1024])
        self.g_fin = di("norm_final_g", [1024])
        self.y = do("y", [NTOK, 1024])
        self.o_shp = do("o_shp", [1, 1024])
        self.o_wkvp = do("o_wkvp", [8, 64, 64])
        self.o_hgp = do("o_hgp", [4, 128, 128])
        self.o_shs = do("o_shs", [16, 1024])
        self.o_wkvs = do("o_wkvs", [16, 8, 64, 64])
        self.o_hgs = do("o_hgs", [16, 4, 128, 128])
        self.wsc_in = ds("wsc_in", [12, 128, 8, 512], BF16)
        self.wsc_up = ds("wsc_up", [2, 128, 8, 512], BF16)
        self.wsc_out = ds("wsc_out", [2, 128, 8, 512], BF16)
        self.wsc_q = ds("wsc_q", [4, 128, 8, 512], BF16)
        self.x1s = ds("x1s", [NTOK, 1024], F32)
        self.gd = ds("gd", [NT, 128, 128, 128], BF16)
        self.xn2Ts = ds("xn2Ts", [128, 8, NTOK], BF16)
        self.do = do
        with ExitStack() as es:
            self.es = es
            self.fw = FW(nc, es)
            try:
                self.build()
            except _Stop:
                pass
            self.fw.finish("sp")
            self.fw.replay()

    def mm(self, out, lhsT, rhs, start=True, stop=True, R=(), W=(), sgc=False):
        self.fw.op("pe", lambda e: e.matmul(out, lhsT=lhsT, rhs=rhs, start=start, stop=stop,
                                            skip_group_check=sgc), R=R, W=W)

    def tr(self, out, in_, R=(), W=(), k=128):
        ident = self.ident
        self.fw.op("pe", lambda e: e.transpose(out, in_, ident[0:k, 0:k]), R=list(R) + ["ident"], W=W)

    def act(self, out, in_, func, R=(), W=(), bias=None, scale=None, accum=None):
        kw = {}
        if bias is not None:
            kw["bias"] = bias
        if scale is not None:
            kw["scale"] = scale
        if accum is not None:
            kw["accum_out"] = accum
        self.fw.op("act", lambda e: e.activation(out=out, in_=in_, func=func, **kw), R=R, W=W)

    def tt(self, out, in0, in1, op, R=(), W=(), eng="dve"):
        self.fw.op(eng, lambda e: e.tensor_tensor(out=out, in0=in0, in1=in1, op=op), R=R, W=W)

    def ts(self, out, in0, s1, op0, R=(), W=(), s2=None, op1=None, eng="dve"):
        if op1 is None:
            self.fw.op(eng, lambda e: e.tensor_scalar(out=out, in0=in0, scalar1=s1, scalar2=None, op0=op0),
                       R=R, W=W)
        else:
            self.fw.op(eng, lambda e: e.tensor_scalar(out=out, in0=in0, scalar1=s1, scalar2=s2, op0=op0, op1=op1),
                       R=R, W=W)

    def stt(self, out, in0, scalar, in1, op0, op1, R=(), W=(), accum=None):
        if accum is None:
            self.fw.op("dve", lambda e: e.scalar_tensor_tensor(out=out, in0=in0, scalar=scalar, in1=in1,
                                                               op0=op0, op1=op1), R=R, W=W)
        else:
            self.fw.op("dve", lambda e: e.scalar_tensor_tensor(out=out, in0=in0, scalar=scalar, in1=in1,
                                                               op0=op0, op1=op1, accum_out=accum), R=R, W=W)

    def cp(self, out, in_, R=(), W=(), eng="dve"):
        if eng == "act":
            self.fw.op("act", lambda e: e.copy(out=out, in_=in_), R=R, W=W)
        else:
            self.fw.op(eng, lambda e: e.tensor_copy(out=out, in_=in_), R=R, W=W)

    def memset(self, ap, val, W=(), eng="pool"):
        self.fw.op(eng, lambda e: e.memset(ap, val), W=W)

    def dma(self, out, in_, R=(), W=(), q="sp", slow=False):
        if slow:
            self.fw.dma(q, lambda e: e.dma_start(out=out, in_=in_, allow_slow_non_contiguous=True), R=R, W=W)
        else:
            self.fw.dma(q, lambda e: e.dma_start(out=out, in_=in_), R=R, W=W)

    def dbg(self, name, ap, shape, R, dt=F32):
        if not self.debug or self.fw.stopped:
            return
        if ("dbg_" + name) in self.dbg_names:
            return
        if getattr(self, "cur_tile", None) is not None and self.cur_tile != self.tiles[-1]:
            return
        d = self.do("dbg_" + name, shape, dt)
        self.dbg_names.append("dbg_" + name)
        self.dma(d, ap, R=R)

    def pt(self, name):
        if self.stop == name:
            self.fw.stopped = True

    def bank(self, hold=False):
        while self.bank_rr in self.held:
            self.bank_rr = (self.bank_rr + 1) % 8
        b = self.bank_rr
        self.bank_rr = (self.bank_rr + 1) % 8
        if hold:
            self.held.add(b)
        return b

    def release(self, *bs):
        for b in bs:
            self.held.discard(b)

    @staticmethod
    def BK(b, qs=(0, 1, 2, 3)):
        return [("B", b)]

    def build(self):
        fw = self.fw
        self.bank_rr = 0
        self.pb = [fw.ps("pb%d" % i, [128, 512], F32) for i in range(8)]
        self.consts()
        self.pt("consts")
        self.prologue()
        self.pt("prologue")
        self.passA()
        fw.barrier()
        if self.do_peer:
            self.passB()

    def consts(self):
        fw = self.fw
        sb = fw.sb
        fi = sb("fidx_i", [128, 128], I32)
        pi = sb("pidx_i", [128, 1], I32)
        ti = sb("tmp_i", [128, 128], I32)
        tpi = sb("tmpp_i", [128, 1], I32)
        ff = self.ff = sb("fidx_f", [128, 128], F32)
        pf = sb("pidx_f", [128, 1], F32)
        fb = sb("fblk_f", [128, 128], F32)
        pbk = sb("pblk_f", [128, 1], F32)
        tmpf = sb("tmpc_f", [128, 128], F32)
        same = sb("same_f", [128, 128], F32)
        self.ident = sb("ident", [128, 128], F32)
        self.ones = sb("ones_f", [128, 128], F32)
        self.bm = sb("bm_f", [128, 128], F32)
        self.mU2 = sb("mU2", [128, 256], F32)
        self.mLs = sb("mLs", [128, 128], F32)
        self.mU2b = sb("mU2b", [128, 256], F32)
        self.mLsb = sb("mLsb", [128, 128], F32)
        self.rmask_s = sb("rmask_s", [128, 128], F32)
        self.rowmask = sb("rowmask", [128, 16], F32)
        self.epsc = sb("epsc", [128, 2], F32)
        self.iota16 = sb("iota16", [128, 16], F32)
        fw.op("pool", lambda e: e.iota(fi[:], pattern=[[1, 128]], base=0, channel_multiplier=0), W=["fi"])
        fw.op("pool", lambda e: e.iota(pi[:], pattern=[[0, 1]], base=0, channel_multiplier=1), W=["pi"])
        self.cp(ff[:], fi[:], R=["fi"], W=["ff"])
        self.cp(pf[:], pi[:], R=["pi"], W=["pf"])
        self.cp(self.iota16[:], fi[:, 0:16], R=["fi"], W=["iota16"])
        self.memset(self.ones[:], 1.0, W=["ones"])
        self.memset(self.epsc[:, 0:1], EPS, W=["epsc0"])
        self.memset(self.epsc[:, 1:2], GN_EPS, W=["epsc1"])
        self.ts(self.ident[:], ff[:], pf[:, 0:1], ALU.is_equal, R=["ff", "pf"], W=["ident"])
        self.ts(self.mU2[:, 0:128], ff[:], pf[:, 0:1], ALU.is_gt, R=["ff", "pf"], W=["mU2a"])
        self.ts(self.mU2[:, 128:256], ff[:], pf[:, 0:1], ALU.is_ge, R=["ff", "pf"], W=["mU2b_"])
        self.ts(self.mLs[:], ff[:], pf[:, 0:1], ALU.is_lt, R=["ff", "pf"], W=["mLs"])
        sh = lambda o, i, n, R, W: fw.op("dve", lambda e: e.tensor_single_scalar(out=o, in_=i, scalar=n,
                                                                                  op=ALU.arith_shift_right), R=R, W=W)
        sh(ti[:], fi[:], 3, ["fi"], ["ti"])
        self.cp(fb[:], ti[:], R=["ti"], W=["fb"])
        sh(tpi[:], pi[:], 3, ["pi"], ["tpi"])
        self.cp(pbk[:], tpi[:], R=["tpi"], W=["pbk"])
        self.ts(same[:], fb[:], pbk[:, 0:1], ALU.is_equal, R=["fb", "pbk"], W=["same"])
        self.tt(self.mU2b[:, 0:128], self.mU2[:, 0:128], same[:], ALU.mult, R=["mU2a", "same"], W=["mU2ba"])
        self.tt(self.mU2b[:, 128:256], self.mU2[:, 128:256], same[:], ALU.mult, R=["mU2b_", "same"], W=["mU2bb"])
        self.tt(self.mLsb[:], self.mLs[:], same[:], ALU.mult, R=["mLs", "same"], W=["mLsb"])
        self.ts(self.rowmask[:], ff[:, 0:16], pbk[:, 0:1], ALU.is_equal, R=["ff", "pbk"], W=["rowmask"])
        fw.op("dve", lambda e: e.tensor_single_scalar(out=ti[:], in_=fi[:], scalar=7, op=ALU.bitwise_and),
              R=["fi", "fb"], W=["ti"])
        self.cp(tmpf[:], ti[:], R=["ti"], W=["tmpf"])
        self.ts(self.rmask_s[:], tmpf[:], 0.5, ALU.is_gt, R=["tmpf"], W=["rmask_s"])
        sh(ti[:], fi[:], 6, ["fi", "tmpf"], ["ti"])
        self.cp(tmpf[:], ti[:], R=["ti"], W=["tmpf"])
        sh(tpi[:], pi[:], 6, ["pi", "pbk"], ["tpi"])
        self.cp(pbk[:], tpi[:], R=["tpi", "same", "rowmask"], W=["pbk2"])
        self.ts(self.bm[:], tmpf[:], pbk[:, 0:1], ALU.is_equal, R=["tmpf", "pbk2"], W=["bm"])
        for k in ["ident", "ones", "bm", "mU2a", "mU2b_", "mLs", "mU2ba", "mU2bb", "mLsb", "rmask_s", "rowmask",
                  "epsc0", "epsc1", "iota16"]:
            pass
        plist = [('muc', self.rw_mu, 14), ('w0c', self.rw_w0, 4), ('a0c', self.rw_a0, 4), ('kkc', self.rw_k_k, 4),
                 ('kac', self.rw_k_a, 4), ('rkc', self.rw_r_k, 4), ('lnwc', self.rw_ln_w, 4), ('lnbc', self.rw_ln_b, 4),
                 ('hgnc', self.hg_ng, 1), ('lb0c', self.hg_lb[0, :], 4), ('lb1c', self.hg_lb[1, :], 4)]
        pst = sb("pstage", [64, 128], F32)
        pcols = sb("pcols", [128, 64], F32)
        self.memset(pst[:], 0.0, W=["pstage"])
        r0 = 0
        offs = {}
        for (nm, ap, n) in plist:
            self.dma(pst[r0:r0 + n, :], ap.rearrange("(c p) -> c p", p=128), W=["pstage"])
            offs[nm] = (r0, n)
            r0 += n
        b = self.bank()
        self.tr(self.pb[b][:, 0:64], pst[:], R=["pstage"], W=self.BK(b), k=64)
        self.cp(pcols[:], self.pb[b][:, 0:64], R=self.BK(b), W=["pcols"])

        class _V:
            def __init__(s_, t, r, n):
                s_.t, s_.r, s_.n = t, r, n

            def __getitem__(s_, idx):
                if isinstance(idx, tuple):
                    a, c = idx
                    if isinstance(c, slice):
                        c = slice((c.start or 0) + s_.r, (s_.n if c.stop is None else c.stop) + s_.r)
                        return s_.t[a, c]
                    return s_.t[a, c + s_.r]
                return s_.t[idx, s_.r:s_.r + s_.n]
        for nm in offs:
            setattr(self, nm, _V(pcols, offs[nm][0], offs[nm][1]))
        self.pkeys = "pcols"
        l0, l1 = self.lb0c, self.lb1c
        self.ommc = sb("ommc", [128, 14], F32)
        self.ts(self.ommc[:], self.muc[:], -1.0, ALU.mult, R=["pcols"], W=["ommc"], s2=1.0, op1=ALU.add)
        self.lbc = sb("lbc", [128, 4], F32)
        self.omlbc = sb("omlbc", [128, 4], F32)
        lbt = sb("lbt", [128, 4], F32)
        self.tt(lbt[:], l0[:], l1[:], ALU.subtract, R=["pcols"], W=["lbt"])
        self.act(self.lbc[:], lbt[:], AF.Sigmoid, R=["lbt"], W=["lbc"])
        self.ts(self.omlbc[:], self.lbc[:], -1.0, ALU.mult, R=["lbc"], W=["omlbc"], s2=1.0, op1=ALU.add)
        self.gmix_bc = sb("gmix_bc", [128, 1024], F32)
        self.dma(self.gmix_bc[:], self.g_mix.partition_broadcast(128), W=["gmix_bc"])
        st = sb("lr_stage", [128, 1024], F32)
        self.w2a2 = sb("w2a2", [128, 512], BF16)
        self.g2b = sb("g2b", [128, 512], BF16)
        self.dma(st[0:64, 0:512], self.rw_w2, W=["lr_stage"])
        self.dma(st[64:128, 0:512], self.rw_a2, W=["lr_stage2"])
        self.dma(st[:, 512:1024], self.rw_g2, W=["lr_stage3"])
        self.cp(self.w2a2[:], st[:, 0:512], R=["lr_stage", "lr_stage2"], W=["w2a2"])
        self.cp(self.g2b[:], st[:, 512:1024], R=["lr_stage3"], W=["g2b"])

    def _col(self, name, ap, n):
        t = self.fw.sb(name, [128, n], F32)
        self.dma(t[:], ap.rearrange("(c p) -> p c", p=128), W=[name], slow=True)
        return t

    def prologue(self):
        with ExitStack() as es:
            old = self.fw.es
            self.fw.es = es
            self._prologue()
            self.fw.es = old

    def _prologue(self):
        sb = self.fw.sb
        stg = [sb("wstg%d" % i, [128, 8, 512], F32) for i in range(2)]
        cvt = [sb("wcvt%d" % i, [128, 8, 512], BF16) for i in range(2)]
        for i in range(2):
            self.memset(cvt[i][:], 0.0, W=["wcvt%d" % i])
        jobs = []
        for g, (c0, n) in enumerate(WG):
            jobs.append((self.wsc_in[g], [(slice(0, 8), self.w_in[:, c0:c0 + n], n)]))
        for i, w in enumerate([self.w_up_a, self.w_up_b]):
            jobs.append((self.wsc_up[i], [(slice(0, 4), w[:, 0:512], 512), (slice(4, 8), w[:, 512:1024], 512)]))
        for i in range(2):
            jobs.append((self.wsc_out[i], [(slice(0, 8), self.w_out[:, i * 512:(i + 1) * 512], 512)]))
        for i in range(4):
            jobs.append((self.wsc_q[i], [(slice(0, 8), self.w_q[:, i * 512:(i + 1) * 512], 512)]))
        engs = ["act", "dve", "pool"]
        for ji, (dst, parts) in enumerate(jobs):
            s = ji % 2
            sk, ck = "wstg%d" % s, "wcvt%d" % s
            ncol = parts[0][2]
            for (ks, src, n) in parts:
                self.dma(stg[s][:, ks, 0:n], src.rearrange("(kc p) c -> p kc c", p=128), W=[sk],
                         q="sp" if ji % 2 == 0 else "act")
            self.cp(cvt[s][:, :, 0:ncol], stg[s][:, :, 0:ncol], R=[sk], W=[ck], eng=engs[ji % 3])
            self.dma(dst[:, :, :], cvt[s][:, :, :], R=[ck], W=[("wsc", id(dst))], q="sp")
        self.fw.barrier()

    def wload(self, src):
        s = self.wslot_rr
        self.wslot_rr = (self.wslot_rr + 1) % len(self.wslots)
        self.dma(self.wslots[s][:], src, W=[("wslot", s)], q="sp")
        return self.wslots[s], ("wslot", s)

    def passA(self):
        fw = self.fw
        sb = fw.sb
        with ExitStack() as es:
            old_es = fw.es
            fw.es = es
            self.wslots = [sb("wslot%d" % i, [128, 8, 512], BF16) for i in range(4)]
            self.wslot_rr = 0
            F = lambda n: sb(n, [128, 4, 128], F32)
            self.x = sb("x_t", [128, 1024], F32)
            self.xn = sb("xn_t", [128, 1024], F32)
            self.ss = sb("ss", [128, 4], F32)
            self.xnT = sb("xnT", [128, 8, 128], BF16)
            self.xpT = sb("xpT", [128, 8, 128], BF16)
            self.lastcol = sb("lastcol", [128, 8, 1], BF16)
            self.s0 = sb("s0", [16, 1024], F32)
            self.s0T = sb("s0T", [128, 8, 16], BF16)
            self.zs0 = sb("zs0", [128, 128], F32)
            self.rT, self.kT, self.vT = F("rT"), F("kT"), F("vT")
            self.lowd = sb("lowd", [128, 128], F32)
            self.lowg = sb("lowg", [128, 128], F32)
            self.tl = sb("tl", [128, 128], BF16)
            self.tla = sb("tla", [128, 128], BF16)
            self.AR0 = sb("AR0", [128, 4, 256], F32)
            self.AR1 = sb("AR1", [128, 4, 256], F32)
            self.sg = sb("sg", [128, 128], BF16)
            self.wlog, self.cum, self.alr, self.gT = F("wlog"), F("cum"), F("alr"), F("gT")
            self.kk, self.kp, self.bb = F("kk"), F("kp"), F("bb")
            self.t1, self.t2, self.t3 = F("t1"), F("t2"), F("t3")
            self.AR = sb("AR", [128, 4, 256], F32)
            self.Kt, self.Bt, self.Kh, self.Bh = F("Kt"), F("Bt"), F("Kh"), F("Bh")
            self.bon = F("bon")
            self.Vtm = sb("Vtm", [128, 512], F32)
            self.Khtm = sb("Khtm", [128, 512], F32)
            self.Bhtm = sb("Bhtm", [128, 512], F32)
            self.Xtm = sb("Xtm", [128, 512], F32)
            self.Utm = sb("Utm", [128, 512], F32)
            self.XTs = sb("XTs", [128, 128], F32)
            self.NA3h = [sb("NA3h%d" % i, [128, 4, 256], F32) for i in range(2)]
            self.A24h = [sb("A24h%d" % i, [128, 4, 256], F32) for i in range(2)]
            self.Pmh = [[F("Pm0h%d" % i), F("Pm1h%d" % i)] for i in range(2)]
            self.Qmh = [F("Qmh%d" % i) for i in range(2)]
            self.Gmh = [[F("Gm0h%d" % i), F("Gm1h%d" % i)] for i in range(2)]
            self.A24 = self.A24h[0]
            self.Hst = F("Hst")
            self.WLc = sb("WLc", [128, 4], F32)
            self.WLs = sb("WLs", [128, 4, 16], F32)
            self.OTs = F("OTs")
            self.oaT = sb("oaT", [128, 4, 128], BF16)
            self.obT = sb("obT", [128, 4, 128], BF16)
            self.Hhg = F("Hhg")
            self.sga = sb("sga", [128, 8, 128], BF16)
            self.sgb = sb("sgb", [128, 8, 128], BF16)
            self.mergedT = sb("mergedT", [128, 8, 128], BF16)
            self.m1 = sb("m1", [128, 512], F32)
            self.m2 = sb("m2", [128, 512], F32)
            self.HsS = [sb("HsS%d" % i, [128, 128], F32) for i in range(8)]
            self.HsL = [sb("HsL%d" % i, [128, 128], F32) for i in range(4)]
            self.Hg4 = [sb("Hg4_%d" % i, [128, 4, 128], F32) for i in range(2)]
            self.Hg4_rr = 0
            self.Lcp = sb("Lcp", [128, 16, 64], F32)
            self.Sop = sb("Sop", [128, 16, 64], F32)
            self.HsS_rr = 0
            self.HsL_rr = 0
            self.BKm = sb("BKm", [128, 1024], F32)
            self.memset(self.lastcol[:], 0.0, W=["lastcol"])
            self.memset(self.tl[:], 0.0, W=["tl0"])
            self.memset(self.tla[:], 0.0, W=["tl1"])
            self.memset(self.AR0[:], 0.0, W=["AR0"])
            self.memset(self.AR1[:], 0.0, W=["AR1"])
            self.memset(self.Hst[:], 0.0, W=["Hst"])
            self.memset(self.Hhg[:], 0.0, W=["Hhg"])
            for i in range(len(self.HsL)):
                self.memset(self.HsL[i][:], 0.0, W=["HsL%d" % i])
            try:
                self.sbuf_left_A = self.nc.sbuf_bytes_remaining
            except Exception as ex:
                self.sbuf_left_A = str(ex)
            for ti in self.tiles:
                self.mixer_tile(ti)
            fw.barrier()
            fw.es = old_es

    def proj_fm(self, wsl, wk, off, out_ap, out_key, shift=False):
        for kc in range(8):
            self.mm(out_ap, lhsT=wsl[:, kc, off:off + 128], rhs=(self.xpT if shift else self.xnT)[:, kc, :],
                    start=(kc == 0), stop=(kc == 7), R=[wk, "xpT" if shift else "xnT"], W=[out_key])

    def mixer_tile(self, ti):
        self.cur_tile = ti
        sample = (ti == 16)
        pb = self.pb
        BK = self.BK
        x, xn = self.x, self.xn
        self.dma(x[:], self.xin[ti * 128:(ti + 1) * 128, :], W=["x"])
        self.act(xn[:], x[:], AF.Square, R=["x"], W=["xn", "ss0"], accum=self.ss[:, 0:1])
        self.act(self.ss[:, 1:2], self.ss[:, 0:1], AF.Sqrt, R=["ss0", "epsc0"], W=["ss1"], bias=self.epsc[:, 0:1],
                 scale=1.0 / 1024)
        self.fw.op("dve", lambda e: e.reciprocal(out=self.ss[:, 2:3], in_=self.ss[:, 1:2]), R=["ss1"], W=["ss2"])
        self.stt(xn[:], x[:], self.ss[:, 2:3], self.gmix_bc[:], ALU.mult, ALU.mult, R=["x", "ss2", "gmix_bc"], W=["xn"])
        if ti == 15:
            self.dma(self.o_shp[0:1, :], xn[127:128, :], R=["xn"], q="act")
        if sample:
            self.dma(self.o_shs[:, :], xn[7:128:8, :], R=["xn"], q="act")
        b0, b1 = self.bank(), self.bank()
        for kc in range(8):
            b = b0 if kc < 4 else b1
            q = kc % 4
            self.tr(pb[b][:, q * 128:(q + 1) * 128], xn[:, kc * 128:(kc + 1) * 128], R=["xn"], W=[("B", b)])
        self.cp(self.xnT[:, 0:4, :], pb[b0][:].rearrange("p (a b) -> p a b", a=4), R=BK(b0), W=["xnT"], eng="act")
        self.cp(self.xnT[:, 4:8, :], pb[b1][:].rearrange("p (a b) -> p a b", a=4), R=BK(b1), W=["xnT"], eng="act")
        self.cp(self.xpT[:, :, 1:128], self.xnT[:, :, 0:127], R=["xnT"], W=["xpT"], eng="pool")
        if not sample:
            self.cp(self.xpT[:, :, 0:1], self.lastcol[:], R=["lastcol"], W=["xpT"], eng="pool")
            self.cp(self.lastcol[:], self.xnT[:, :, 127:128], R=["xnT", "xpT"], W=["lastcol"], eng="pool")
        else:
            self.dma(self.s0[:], self.shift0, W=["s0"])
            b = self.bank()
            for kc in range(8):
                self.tr(pb[b][:, kc * 16:(kc + 1) * 16], self.s0[:, kc * 128:(kc + 1) * 128], R=["s0"], W=BK(b), k=16)
            self.cp(self.s0T[:], pb[b][:, 0:128].rearrange("p (a b) -> p a b", a=8), R=BK(b), W=["s0T"])
            self.cp(self.xpT[:, :, 0:128:8], self.s0T[:], R=["s0T"], W=["xpT"], eng="pool")
        self.dbg("xn", self.xn[:], [128, 1024], ["xn"])
        self.pt("t0")
        self.rwkv(ti, sample)
        self.pt("rwkv")
        self.hgrn(ti, sample)
        self.pt("hgrn")
        self.merge(ti)

    def shiftmix(self, wsl, wk, off, c, dst, dkey):
        pb = self.pb
        bz, bp = self.bank(), self.bank()
        self.proj_fm(wsl, wk, off, pb[bz][:, 0:128], ("B", bz), shift=False)
        self.proj_fm(wsl, wk, off, pb[bp][:, 0:128], ("B", bp), shift=True)
        self.act(self.zs0[:], pb[bz][:, 0:128], AF.Copy, R=[("B", bz), "ommc"], W=["zs0"], scale=self.ommc[:, c:c + 1])
        self.stt(dst, pb[bp][:, 0:128], self.muc[:, c:c + 1], self.zs0[:], ALU.mult, ALU.add,
                 R=[("B", bp), "zs0", "pcols"], W=[dkey])

    def rwkv(self, ti, sample):
        pb, BK = self.pb, self.BK
        segs = [(8 * j, 8) for j in range(16)] if sample else [(0, 128)]
        mU2 = self.mU2b if sample else self.mU2
        mLs = self.mLsb if sample else self.mLs
        mU2k = ["mU2ba", "mU2bb"] if sample else ["mU2a", "mU2b_"]
        mLsk = "mLsb" if sample else "mLs"
        nlev = 3 if sample else 7
        for g, dst, name in [(0, self.rT, "rT"), (1, self.kT, "kT"), (2, self.vT, "vT")]:
            wsl, wk = self.wload(self.wsc_in[g])
            for c4 in range(4):
                self.shiftmix(wsl, wk, c4 * 128, g * 4 + c4, dst[:, c4, :], (name, c4))
        wsl, wk = self.wload(self.wsc_in[3])
        self.shiftmix(wsl, wk, 0, 12, self.lowd[:], "lowd")
        self.shiftmix(wsl, wk, 128, 13, self.lowg[:], "lowg")
        self.dbg("rT", self.rT[:], [128, 4, 128], [("rT", c) for c in range(4)])
        self.pt("rwproj")
        for _ in self.merge_gates():
            pass
        self.act(self.tl[0:64, :], self.lowd[0:64, :], AF.Tanh, R=["lowd"], W=["tl0"])
        self.act(self.tla[64:128, :], self.lowd[64:128, :], AF.Copy, R=["lowd"], W=["tl1"])
        self.act(self.sg[:], self.lowg[:], AF.Sigmoid, R=["lowg"], W=["sg"])
        self.pt("lr1")
        bd, ba, bg = self.bank(), self.bank(), self.bank()
        for p in range(4):
            cs = slice(p * 128, (p + 1) * 128)
            self.mm(pb[bd][:, cs], lhsT=self.w2a2[:, cs], rhs=self.tl[:], R=["w2a2", "tl0"], W=[("B", bd)])
            self.mm(pb[ba][:, cs], lhsT=self.w2a2[:, cs], rhs=self.tla[:], R=["w2a2", "tl1"], W=[("B", ba)])
            self.mm(pb[bg][:, cs], lhsT=self.g2b[:, cs], rhs=self.sg[:], R=["g2b", "sg"], W=[("B", bg)])
        self.pt("lr2")
        for p in range(4):
            cs = slice(p * 128, (p + 1) * 128)
            self.ts(self.t1[:, p, :], pb[bd][:, cs], self.w0c[:, p:p + 1], ALU.add, R=[("B", bd), "pcols"], W=[("t1", p)])
            self.ts(self.alr[:, p, :], pb[ba][:, cs], self.a0c[:, p:p + 1], ALU.add, R=[("B", ba), "pcols"], W=[("alr", p)])
        self.act(self.t1[:], self.t1[:], AF.Sigmoid, R=[("t1", c) for c in range(4)], W=[("t1", c) for c in range(4)])
        self.act(self.alr[:], self.alr[:], AF.Sigmoid, R=[("alr", c) for c in range(4)], W=[("alr", c) for c in range(4)])
        for p in range(0):
            pass
        self.pt("lr3")
        self.cp(self.gT[:], pb[bg][:].rearrange("p (a b) -> p a b", a=4), R=BK(bg), W=["gT"], eng="act")
        self.pt("lr")
        K4 = lambda n: [(n, c) for c in range(4)]
        bc = lambda t: t[:][:, :, None].to_broadcast([128, 4, 128])
        self.ts(self.wlog[:], self.t1[:], -math.exp(-0.5), ALU.mult, R=K4("t1"), W=["wlog"])
        self.tt(self.kk[:], self.kT[:], bc(self.kkc), ALU.mult, R=K4("kT") + ["pcols"], W=["kk"])
        self.tt(self.t2[:], self.kk[:], self.kk[:], ALU.mult, R=["kk"], W=["t2"])
        b = self.bank()
        self.mm(pb[b][:], lhsT=self.bm[:], rhs=self.t2[:].rearrange("p a b -> p (a b)"), R=["bm", "t2"], W=BK(b))
        self.act(self.t3[:], pb[b][:].rearrange("p (a b) -> p a b", a=4), AF.Sqrt, R=BK(b), W=["t3"])
        self.ts(self.t3[:], self.t3[:], 1e-12, ALU.max, R=["t3"], W=["t3"])
        self.fw.op("dve", lambda e: e.reciprocal(out=self.t3[:], in_=self.t3[:]), R=["t3"], W=["t3"])
        self.tt(self.kk[:], self.kk[:], self.t3[:], ALU.mult, R=["kk", "t3"], W=["kk"])
        self.pt("kk")
        self.ts(self.t2[:], self.alr[:], -1.0, ALU.add, R=K4("alr") + ["t2"], W=["t2"])
        self.tt(self.t2[:], self.t2[:], bc(self.kac), ALU.mult, R=["t2", "pcols"], W=["t2"])
        self.stt(self.kp[:], self.t2[:], 1.0, self.kT[:], ALU.add, ALU.mult, R=["t2"] + K4("kT"), W=["kp"])
        self.tt(self.bb[:], self.kk[:], self.alr[:], ALU.mult, R=["kk"] + K4("alr"), W=["bb"])
        self.tt(self.t2[:], self.rT[:], self.kp[:], ALU.mult, R=K4("rT") + ["kp"], W=["t2"])
        self.tt(self.t2[:], self.t2[:], bc(self.rkc), ALU.mult, R=["t2", "pcols"], W=["t2"])
        b = self.bank()
        self.mm(pb[b][:], lhsT=self.bm[:], rhs=self.t2[:].rearrange("p a b -> p (a b)"), R=["bm", "t2"], W=BK(b))
        self.tt(self.bon[:], pb[b][:].rearrange("p (a b) -> p a b", a=4), self.vT[:], ALU.mult, R=BK(b) + K4("vT"), W=["bon"])
        self.pt("bon")
        rm = self.rmask_s if sample else self.ones
        rmk = "rmask_s" if sample else "ones"
        for p in range(4):
            self.fw.op("dve", (lambda p: lambda e: e.tensor_tensor_scan(out=self.cum[:, p, :], data0=rm[:], data1=self.wlog[:, p, :],
                                                                         initial=0.0, op0=ALU.mult, op1=ALU.add))(p),
                       R=["wlog", rmk], W=[("cum", p)])
        self.pt("scan")
        Kc = K4("cum")
        self.tt(self.t2[:], self.cum[:], self.wlog[:], ALU.subtract, R=Kc + ["wlog", "t2"], W=["t2"])
        self.act(self.t2[:], self.t2[:], AF.Exp, R=["t2"], W=["t2"])
        self.stt(self.AR[:, :, 0:128], self.kk[:], -1.0, self.t2[:], ALU.mult, ALU.mult, R=["kk", "t2"], W=["ARa"])
        self.act(self.t3[:], self.cum[:], AF.Exp, R=Kc + ["t3"], W=["t3"])
        self.tt(self.AR[:, :, 128:256], self.rT[:], self.t3[:], ALU.mult, R=K4("rT") + ["t3"], W=["ARr"])
        self.cp(self.AR0[0:64, :, :], self.AR[0:64, :, :], R=["ARa", "ARr"], W=["AR0"], eng="pool")
        self.cp(self.AR1[64:128, :, :], self.AR[64:128, :, :], R=["ARa", "ARr"], W=["AR1"], eng="pool")
        self.act(self.t1[:], self.cum[:], AF.Exp, R=Kc + K4("t1") + ["wlog"], W=K4("t1"), scale=-1.0)
        self.tt(self.Kt[:], self.kp[:], self.t1[:], ALU.mult, R=["kp"] + K4("t1"), W=["Kt"])
        self.tt(self.Bt[:], self.bb[:], self.t1[:], ALU.mult, R=["bb"] + K4("t1"), W=["Bt"])
        if not sample:
            for p in range(4):
                self.act(self.t2[:, p, :], self.cum[:, p, :], AF.Exp, R=[("cum", p), "t2", "ARa"], W=["t2"], scale=-1.0,
                         bias=self.cum[:, p, 127:128])
            self.act(self.WLc[:], self.cum[:, :, 127], AF.Exp, R=Kc, W=["WLc"])
        else:
            c4 = self.cum[:].rearrange("p a (j t) -> p (a j) t", t=8)
            self.tt(self.t2[:].rearrange("p a (j t) -> p (a j) t", t=8), c4[:, :, 7:8].to_broadcast([128, 64, 8]), c4,
                    ALU.subtract, R=Kc + ["t2", "ARa"], W=["t2"])
            self.act(self.t2[:], self.t2[:], AF.Exp, R=["t2"], W=["t2"])
            self.act(self.WLs[:], self.cum[:, :, 7:128:8], AF.Exp, R=Kc, W=["WLs"])
        self.tt(self.Kh[:], self.kp[:], self.t2[:], ALU.mult, R=["kp", "t2"], W=["Kh"])
        self.tt(self.Bh[:], self.bb[:], self.t2[:], ALU.mult, R=["bb", "t2"], W=["Bh"])
        self.pt("exp")
        for src, sk, dst, dk in [(self.vT, K4("vT"), self.Vtm, "Vtm"), (self.Kh, ["Kh"], self.Khtm, "Khtm"),
                                 (self.Bh, ["Bh"], self.Bhtm, "Bhtm")]:
            b = self.bank()
            for p in range(4):
                self.tr(pb[b][:, p * 128:(p + 1) * 128], src[:, p, :], R=sk, W=[("B", b)])
            self.cp(dst[:], pb[b][:], R=BK(b), W=[dk], eng="act")
        self.dbg("AR", self.AR[:], [128, 4, 256], ["ARa", "ARr"])
        self.dbg("Kt", self.Kt[:], [128, 4, 128], ["Kt"])
        self.dbg("Bt", self.Bt[:], [128, 4, 128], ["Bt"])
        self.pt("rwprep")
        bO = self.bank(hold=True)
        def hg_gen(hg):
            NA3, A24, Pm_, Qm_, Gm_ = self.NA3h[hg], self.A24h[hg], self.Pmh[hg], self.Qmh[hg], self.Gmh[hg]
            sfx = "h%d" % hg
            heads = [4 * hg + i for i in range(4)]
            bA = [self.bank(), self.bank()]
            bB = [self.bank(), self.bank()]
            bP = self.bank()
            for i, h in enumerate(heads):
                p, P = h // 2, slice((h % 2) * 64, (h % 2) * 64 + 64)
                hs = slice((i % 2) * 256, (i % 2) * 256 + 256)
                qa = [(i % 2) * 2, (i % 2) * 2 + 1]
                ARp, ARpk = (self.AR0, "AR0") if h % 2 == 0 else (self.AR1, "AR1")
                self.mm(pb[bA[i // 2]][:, hs], lhsT=self.Bt[:, p, :], rhs=ARp[:, p, :], R=["Bt", ARpk], W=BK(bA[i // 2], qa))
                self.mm(pb[bB[i // 2]][:, hs], lhsT=self.Kt[:, p, :], rhs=ARp[:, p, :], R=["Kt", ARpk], W=BK(bB[i // 2], qa))
                self.mm(pb[bP][:, i * 128:(i + 1) * 128], lhsT=ARp[:, p, 0:128], rhs=self.Bt[:, p, :], R=["Bt", ARpk], W=[("B", bP)])
            for i in range(4):
                hs = slice((i % 2) * 256, (i % 2) * 256 + 256)
                qa = [(i % 2) * 2, (i % 2) * 2 + 1]
                self.tt(NA3[:, i, :], pb[bA[i // 2]][:, hs], mU2[:], ALU.mult, R=BK(bA[i // 2], qa) + mU2k, W=[("NA3" + sfx, i)])
                self.tt(A24[:, i, :], pb[bB[i // 2]][:, hs], mU2[:], ALU.mult, R=BK(bB[i // 2], qa) + mU2k, W=[("A24" + sfx, i)])
            self.tt(Pm_[0][:], pb[bP][:].rearrange("p (a b) -> p a b", a=4), mLs[:, None, :].to_broadcast([128, 4, 128]),
                    ALU.mult, R=BK(bP) + [mLsk], W=["Pm0" + sfx])
            yield
            NA3k = [("NA3" + sfx, i) for i in range(4)]
            self.tt(Gm_[0][:], NA3[:, :, 0:128], self.ident[:, None, :].to_broadcast([128, 4, 128]), ALU.add,
                    R=NA3k + ["ident"], W=["Gm0" + sfx])
            Q, Qk = (lambda i: NA3[:, i, 0:128]), NA3k
            cur = 0
            for lvl in range(nlev - 1):
                nxt = 1 - cur
                Pc, Pk = Pm_[cur], "Pm%d" % cur + sfx
                Pn, Pnk = Pm_[nxt], "Pm%d" % nxt + sfx
                b1, b2, b3 = self.bank(), self.bank(), self.bank()
                for i in range(4):
                    cs = slice(i * 128, (i + 1) * 128)
                    self.mm(pb[b1][:, cs], lhsT=Q(i), rhs=Pc[:, i, :], R=Qk + [Pk], W=[("B", b1)])
                    self.mm(pb[b2][:, cs], lhsT=Pc[:, i, :], rhs=Q(i), R=Qk + [Pk], W=[("B", b2)])
                self.cp(Pn[:], pb[b1][:].rearrange("p (a b) -> p a b", a=4), R=BK(b1), W=[Pnk], eng="act")
                self.cp(Qm_[:], pb[b2][:].rearrange("p (a b) -> p a b", a=4), R=BK(b2), W=["Qm" + sfx])
                Q, Qk = (lambda i: Qm_[:, i, :]), ["Qm" + sfx]
                Gc, Gk = Gm_[cur], "Gm%d" % cur + sfx
                Gn, Gnk = Gm_[nxt], "Gm%d" % nxt + sfx
                for i in range(4):
                    cs = slice(i * 128, (i + 1) * 128)
                    self.mm(pb[b3][:, cs], lhsT=Pn[:, i, :], rhs=Gc[:, i, :], R=[Pnk, Gk], W=[("B", b3)])
                self.tt(Gn[:], pb[b3][:].rearrange("p (a b) -> p a b", a=4), Gc[:], ALU.add, R=BK(b3) + [Gk], W=[Gnk])
                cur = nxt
                yield
            G, Gk = Gm_[cur], "Gm%d" % cur + sfx
            if hg == 0 and ti == 0:
                self.dbg("G", G[:], [128, 4, 128], [Gk])
            for pp in range(2):
                p = 2 * hg + pp
                bX = self.bank(hold=True)
                for hh in range(2):
                    i = 2 * pp + hh
                    h = heads[i]
                    P = slice(hh * 64, hh * 64 + 64)
                    self.mm(pb[bX][P, 0:128], lhsT=self.Vtm[:, h * 64:(h + 1) * 64], rhs=A24[:, i, 0:128], start=True, stop=False,
                            R=["Vtm", ("A24" + sfx, i)], W=[("B", bX)], sgc=True)
                self.state_mm(pb[bX], 0, ("B", bX), self.AR, 0, ["ARa"], p, segs, sample, "rw")
                self.cp(self.XTs[:], pb[bX][:, 0:128], R=[("B", bX)], W=["XTs"], eng="act")
                self.release(bX)
                b = self.bank()
                self.tr(pb[b][:, 0:128], self.XTs[:], R=["XTs"], W=[("B", b)])
                self.cp(self.Xtm[:, p * 128:(p + 1) * 128], pb[b][:, 0:128], R=[("B", b)], W=[("Xtm", p)])
                yield
            bU = self.bank()
            for i, h in enumerate(heads):
                self.mm(pb[bU][:, i * 64:(i + 1) * 64], lhsT=G[:, i, :], rhs=self.Xtm[:, h * 64:(h + 1) * 64], R=[Gk, ("Xtm", h // 2)],
                        W=[("B", bU)])
            self.cp(self.Utm[:, hg * 256:(hg + 1) * 256], pb[bU][:, 0:256], R=[("B", bU)], W=[("Utm", hg)])
            yield
            for pp in range(2):
                p = 2 * hg + pp
                for hh in range(2):
                    i = 2 * pp + hh
                    h = heads[i]
                    P = slice(hh * 64, hh * 64 + 64)
                    self.mm(pb[bO][P, p * 128:(p + 1) * 128], lhsT=self.Utm[:, h * 64:(h + 1) * 64], rhs=NA3[:, i, 128:256],
                            start=True, stop=False, R=[("Utm", hg), ("NA3" + sfx, i)], W=[("B", bO)], sgc=True)
                    self.mm(pb[bO][P, p * 128:(p + 1) * 128], lhsT=self.Vtm[:, h * 64:(h + 1) * 64], rhs=A24[:, i, 128:256],
                            start=False, stop=False, R=["Vtm", ("A24" + sfx, i)], W=[("B", bO)], sgc=True)
                self.state_mm(pb[bO], p, ("B", bO), self.AR, 128, ["ARr"], p, segs, sample, "rw")
                yield
            self.rw_state_update(ti, hg, heads, segs, sample)
        gens = [hg_gen(0), hg_gen(1)]
        while gens:
            for g_ in list(gens):
                try:
                    next(g_)
                except StopIteration:
                    gens.remove(g_)
        self.cp(self.OTs[:], pb[bO][:].rearrange("p (a b) -> p a b", a=4), R=BK(bO), W=["OTs"], eng="act")
        self.release(bO)
        self.dbg("OTs", self.OTs[:], [128, 4, 128], ["OTs"])
        flat = lambda t: t[:].rearrange("p a b -> p (a b)")
        b = self.bank()
        self.mm(pb[b][:], lhsT=self.bm[:], rhs=flat(self.OTs), R=["bm", "OTs"], W=BK(b))
        self.stt(flat(self.t1), pb[b][:], -1.0 / 64, flat(self.OTs), ALU.mult, ALU.add, R=BK(b) + ["OTs"] + K4("t1"), W=K4("t1"))
        self.tt(self.t2[:], self.t1[:], self.t1[:], ALU.mult, R=K4("t1") + ["t2"], W=["t2"])
        b = self.bank()
        self.mm(pb[b][:], lhsT=self.bm[:], rhs=flat(self.t2), R=["bm", "t2"], W=BK(b))
        self.ts(flat(self.t3), pb[b][:], 1.0 / 64, ALU.mult, R=BK(b) + ["t3"], W=["t3"], s2=GN_EPS, op1=ALU.add)
        self.act(self.t3[:], self.t3[:], AF.Sqrt, R=["t3"], W=["t3"])
        self.fw.op("dve", lambda e: e.reciprocal(out=self.t3[:], in_=self.t3[:]), R=["t3"], W=["t3"])
        self.tt(self.t1[:], self.t1[:], self.t3[:], ALU.mult, R=K4("t1") + ["t3"], W=K4("t1"))
        self.tt(self.t1[:], self.t1[:], bc(self.lnwc), ALU.mult, R=K4("t1") + ["pcols"], W=K4("t1"))
        self.tt(self.t1[:], self.t1[:], bc(self.lnbc), ALU.add, R=K4("t1") + ["pcols"], W=K4("t1"))
        self.tt(self.t1[:], self.t1[:], self.bon[:], ALU.add, R=K4("t1") + ["bon"], W=K4("t1"))
        self.tt(self.t1[:], self.t1[:], self.gT[:], ALU.mult, R=K4("t1") + ["gT"], W=K4("t1"))
        self.cp(self.oaT[:], self.t1[:], R=K4("t1"), W=["oaT"], eng="pool")
        self.dbg("oaT", self.t1[:], [128, 4, 128], K4("t1"))

    def state_mm(self, bank_t, q, okey, src, off, skeys, p, segs, sample, kind):
        cs0 = q * 128
        if not sample:
            H, Hk = (self.Hst, "Hst") if kind == "rw" else (self.Hhg, "Hhg")
            self.mm(bank_t[:, cs0:cs0 + 128], lhsT=H[:, p, :], rhs=src[:, p, off:off + 128], start=False, stop=True,
                    R=[Hk] + skeys, W=[okey], sgc=True)
        else:
            if kind == "rw":
                self.load_pair_states(p)
            for j, (s0, ln) in enumerate(segs):
                Hs, Hsk = self.sample_state(j, p, kind)
                self.mm(bank_t[:, cs0 + s0:cs0 + s0 + ln], lhsT=Hs[:], rhs=src[:, p, off + s0:off + s0 + ln], start=False,
                        stop=(j == len(segs) - 1), R=[Hsk] + skeys, W=[okey], sgc=True)

    def load_hg_seq(self, j):
        i = self.Hg4_rr
        self.Hg4_rr = (i + 1) % len(self.Hg4)
        self.dma(self.Hg4[i][:], self.hg0[j].rearrange("h k v -> k h v"), W=["Hg4_%d" % i], q="sp")
        return self.Hg4[i], "Hg4_%d" % i

    def load_pair_states(self, p):
        for g4 in range(4):
            self.dma(self.Lcp[:, 4 * g4:4 * g4 + 4, :],
                     self.wkv0[4 * g4:4 * g4 + 4, 2 * p:2 * p + 2, :, :].rearrange("j hh v k -> (hh v) j k"), W=["Lcp"], q="sp")

    def sample_state(self, j, p, kind):
        s = self.HsS_rr
        self.HsS_rr = (s + 1) % len(self.HsS)
        dst, dk = self.HsS[s], "HsS%d" % s
        if kind == "hg":
            self.dma(dst[:], self.hg0[j, p, :, :], W=[dk], q="act")
            return dst, dk
        l = self.HsL_rr
        self.HsL_rr = (l + 1) % len(self.HsL)
        L, Lk = self.HsL[l], "HsL%d" % l
        for hh in range(2):
            self.cp(L[hh * 64:(hh + 1) * 64, hh * 64:(hh + 1) * 64], self.Lcp[hh * 64:(hh + 1) * 64, j, :], R=["Lcp"], W=[Lk], eng="pool")
        b = self.bank()
        self.tr(self.pb[b][:, 0:128], L[:], R=[Lk], W=[("B", b)])
        self.cp(dst[:], self.pb[b][:, 0:128], R=[("B", b)], W=[dk], eng="act")
        return dst, dk

    def rw_state_update(self, ti, hg, heads, segs, sample):
        pb = self.pb
        if not sample:
            for pp in range(2):
                p = 2 * hg + pp
                bH = self.bank()
                for hh in range(2):
                    h = heads[2 * pp + hh]
                    P = slice(hh * 64, hh * 64 + 64)
                    cs = slice(hh * 64, hh * 64 + 64)
                    hc = slice(h * 64, (h + 1) * 64)
                    self.mm(pb[bH][P, cs], lhsT=self.Bhtm[:, hc], rhs=self.Utm[:, hc], start=True, stop=False,
                            R=["Bhtm", ("Utm", hg)], W=[("B", bH)], sgc=True)
                    self.mm(pb[bH][P, cs], lhsT=self.Khtm[:, hc], rhs=self.Vtm[:, hc], start=False, stop=True,
                            R=["Khtm", "Vtm"], W=[("B", bH)], sgc=True)
                for hh in range(2):
                    P = slice(hh * 64, hh * 64 + 64)
                    cs = slice(hh * 64, hh * 64 + 64)
                    self.stt(self.Hst[P, p, cs], self.Hst[P, p, cs], self.WLc[P, p:p + 1], pb[bH][P, cs], ALU.mult, ALU.add,
                             R=["Hst", "WLc", ("B", bH)], W=["Hst"])
                if ti == 15:
                    self.store_rw_state(self.Hst[:, p, :], "Hst", lambda h: self.o_wkvp[h, :, :], p)
        else:
            for pp in range(2):
                p = 2 * hg + pp
                self.load_pair_states(p)
                pc = slice(p * 128, (p + 1) * 128)
                for j, (s0, ln) in enumerate(segs):
                    self.ts(self.BKm[:, 0:128], self.Bhtm[:, pc], self.rowmask[:, j:j + 1], ALU.mult, R=["Bhtm", "rowmask"], W=["BKm0"])
                    self.ts(self.BKm[:, 128:256], self.Khtm[:, pc], self.rowmask[:, j:j + 1], ALU.mult, R=["Khtm", "rowmask"], W=["BKm1"])
                    bH = self.bank(hold=True)
                    for hh in range(2):
                        h = heads[2 * pp + hh]
                        P = slice(hh * 64, hh * 64 + 64)
                        cs = slice(hh * 64, hh * 64 + 64)
                        hc = slice(h * 64, (h + 1) * 64)
                        self.mm(pb[bH][P, cs], lhsT=self.BKm[:, hh * 64:(hh + 1) * 64], rhs=self.Utm[:, hc], start=True, stop=False,
                                R=["BKm0", ("Utm", hg)], W=[("B", bH)], sgc=True)
                        self.mm(pb[bH][P, cs], lhsT=self.BKm[:, 128 + hh * 64:128 + (hh + 1) * 64], rhs=self.Vtm[:, hc],
                                start=False, stop=True, R=["BKm1", "Vtm"], W=[("B", bH)], sgc=True)
                    Hs, Hsk = self.sample_state(j, p, "rw")
                    for hh in range(2):
                        P = slice(hh * 64, hh * 64 + 64)
                        cs = slice(hh * 64, hh * 64 + 64)
                        self.stt(Hs[P, cs], Hs[P, cs], self.WLs[P, p, j:j + 1], pb[bH][P, cs], ALU.mult, ALU.add,
                                 R=[Hsk, "WLs", ("B", bH)], W=[Hsk])
                    self.release(bH)
                    b = self.bank()
                    self.tr(pb[b][:, 0:128], Hs[:], R=[Hsk], W=[("B", b)])
                    for hh in range(2):
                        P = slice(hh * 64, hh * 64 + 64)
                        self.cp(self.Sop[P, j, :], pb[b][P, hh * 64:(hh + 1) * 64], R=[("B", b)], W=["Sop"], eng="act")
                for g4 in range(4):
                    self.dma(self.o_wkvs[4 * g4:4 * g4 + 4, 2 * p:2 * p + 2, :, :].rearrange("j hh v k -> (hh v) j k"),
                             self.Sop[:, 4 * g4:4 * g4 + 4, :], R=["Sop"], q="act")

    def store_rw_state(self, H_ap, Hk, dst_of_head, p):
        b = self.bank()
        self.tr(self.pb[b][:, 0:128], H_ap, R=[Hk], W=[("B", b)])
        l = self.HsL_rr
        self.HsL_rr = (l + 1) % len(self.HsL)
        st, sk = self.HsL[l], "HsL%d" % l
        for hh in range(2):
            P = slice(hh * 64, hh * 64 + 64)
            self.cp(st[P, hh * 64:(hh + 1) * 64], self.pb[b][P, hh * 64:(hh + 1) * 64], R=[("B", b)], W=[sk], eng="act")
        for hh in range(2):
            P = slice(hh * 64, hh * 64 + 64)
            self.dma(dst_of_head(2 * p + hh), st[P, hh * 64:(hh + 1) * 64], R=[sk], q="act")

    def hgrn(self, ti, sample):
        pb, BK = self.pb, self.BK
        segs = [(8 * j, 8) for j in range(16)] if sample else [(0, 128)]
        mUi = (self.mU2b if sample else self.mU2)[:, 128:256]
        mUik = "mU2bb" if sample else "mU2b_"
        K4 = lambda n: [(n, c) for c in range(4)]
        f3 = lambda b: pb[b][:].rearrange("p (a b) -> p a b", a=4)
        bq, bf_, bgt, bi = self.bank(hold=True), self.bank(hold=True), self.bank(hold=True), self.bank(hold=True)
        wsl, wk = self.wload(self.wsc_in[4])
        for c in range(4):
            self.proj_fm(wsl, wk, c * 128, pb[bq][:, c * 128:(c + 1) * 128], ("B", bq))
        wsl, wk = self.wload(self.wsc_in[5])
        for c in range(4):
            self.proj_fm(wsl, wk, c * 128, pb[bf_][:, c * 128:(c + 1) * 128], ("B", bf_))
        wsl, wk = self.wload(self.wsc_in[6])
        for kc in range(8):
            self.mm(pb[bi][:], lhsT=self.xnT[:, kc, :], rhs=wsl[:, kc, :], start=(kc == 0), stop=(kc == 7), R=["xnT", wk], W=BK(bi))
        wsl, wk = self.wload(self.wsc_in[7])
        for c in range(4):
            self.proj_fm(wsl, wk, c * 128, pb[bgt][:, c * 128:(c + 1) * 128], ("B", bgt))
        self.cp(self.Vtm[:], pb[bi][:], R=BK(bi), W=["Vtm"], eng="act")
        self.release(bi)
        self.act(self.t1[:], f3(bf_), AF.Sigmoid, R=BK(bf_) + K4("t1"), W=K4("t1"))
        for h in range(4):
            self.ts(self.t1[:, h, :], self.t1[:, h, :], self.omlbc[:, h:h + 1], ALU.mult, R=[("t1", h), "omlbc", "lbc"], W=[("t1", h)],
                    s2=self.lbc[:, h:h + 1], op1=ALU.add)
        self.release(bf_)
        self.act(self.wlog[:], self.t1[:], AF.Ln, R=K4("t1") + ["wlog"], W=["wlog"])
        self.ts(self.kp[:], self.t1[:], -1.0, ALU.mult, R=K4("t1") + ["kp"], W=["kp"], s2=1.0, op1=ALU.add)
        rm = self.rmask_s if sample else self.ones
        rmk = "rmask_s" if sample else "ones"
        for h in range(4):
            self.fw.op("dve", (lambda h: lambda e: e.tensor_tensor_scan(out=self.cum[:, h, :], data0=rm[:], data1=self.wlog[:, h, :],
                                                                         initial=0.0, op0=ALU.mult, op1=ALU.add))(h),
                       R=["wlog", rmk], W=[("cum", h)])
        Kc = K4("cum")
        self.act(self.t3[:], self.cum[:], AF.Exp, R=Kc + ["t3"], W=["t3"])
        self.tt(self.AR[:, :, 128:256], f3(bq), self.t3[:], ALU.mult, R=BK(bq) + ["t3", "ARr"], W=["ARr"])
        self.release(bq)
        self.act(self.t1[:], self.cum[:], AF.Exp, R=Kc + K4("t1"), W=K4("t1"), scale=-1.0)
        self.tt(self.Kt[:], self.kp[:], self.t1[:], ALU.mult, R=["kp"] + K4("t1") + ["Kt"], W=["Kt"])
        if not sample:
            for h in range(4):
                self.act(self.t2[:, h, :], self.cum[:, h, :], AF.Exp, R=[("cum", h), "t2"], W=["t2"], scale=-1.0, bias=self.cum[:, h, 127:128])
            self.act(self.WLc[:], self.cum[:, :, 127], AF.Exp, R=Kc + ["WLc"], W=["WLc"])
        else:
            c4 = self.cum[:].rearrange("p a (j t) -> p (a j) t", t=8)
            self.tt(self.t2[:].rearrange("p a (j t) -> p (a j) t", t=8), c4[:, :, 7:8].to_broadcast([128, 64, 8]), c4,
                    ALU.subtract, R=Kc + ["t2"], W=["t2"])
            self.act(self.t2[:], self.t2[:], AF.Exp, R=["t2"], W=["t2"])
            self.act(self.WLs[:], self.cum[:, :, 7:128:8], AF.Exp, R=Kc + ["WLs"], W=["WLs"])
        self.tt(self.Kh[:], self.kp[:], self.t2[:], ALU.mult, R=["kp", "t2", "Kh"], W=["Kh"])
        b = self.bank()
        for h in range(4):
            self.tr(pb[b][:, h * 128:(h + 1) * 128], self.Kh[:, h, :], R=["Kh"], W=[("B", b)])
        self.cp(self.Khtm[:], pb[b][:], R=BK(b), W=["Khtm"], eng="act")
        b = self.bank()
        for h in range(4):
            self.mm(pb[b][:, h * 128:(h + 1) * 128], lhsT=self.Kt[:, h, :], rhs=self.AR[:, h, 128:256], R=["Kt", "ARr"], W=[("B", b)])
        self.tt(self.A24[:, :, 0:128], f3(b), mUi[:, None, :].to_broadcast([128, 4, 128]), ALU.mult,
                R=BK(b) + [mUik] + [("A24h0", i) for i in range(4)], W=[("A24h0", i) for i in range(4)])
        bO = self.bank(hold=True)
        if not sample:
            for h in range(4):
                self.mm(pb[bO][:, h * 128:(h + 1) * 128], lhsT=self.Vtm[:, h * 128:(h + 1) * 128], rhs=self.A24[:, h, 0:128], start=True, stop=False,
                        R=["Vtm", ("A24h0", h)], W=[("B", bO)], sgc=True)
                self.state_mm(pb[bO], h, ("B", bO), self.AR, 128, ["ARr"], h, segs, sample, "hg")
        else:
            for h in range(4):
                self.mm(pb[bO][:, h * 128:(h + 1) * 128], lhsT=self.Vtm[:, h * 128:(h + 1) * 128], rhs=self.A24[:, h, 0:128], start=(h == 0),
                        stop=False, R=["Vtm", ("A24h0", h)], W=[("B", bO)], sgc=True)
            for j, (s0, ln) in enumerate(segs):
                Hs4, Hs4k = self.load_hg_seq(j)
                for h in range(4):
                    self.mm(pb[bO][:, h * 128 + s0:h * 128 + s0 + ln], lhsT=Hs4[:, h, :], rhs=self.AR[:, h, 128 + s0:128 + s0 + ln], start=False,
                            stop=(j == len(segs) - 1 and h == 3), R=[Hs4k, "ARr"], W=[("B", bO)], sgc=True)
        self.cp(self.OTs[:], f3(bO), R=BK(bO), W=["OTs"], eng="act")
        self.release(bO)
        if not sample:
            bH = self.bank()
            for h in range(4):
                cs = slice(h * 128, (h + 1) * 128)
                self.mm(pb[bH][:, cs], lhsT=self.Khtm[:, cs], rhs=self.Vtm[:, cs], R=["Khtm", "Vtm"], W=[("B", bH)])
            for h in range(4):
                cs = slice(h * 128, (h + 1) * 128)
                self.stt(self.Hhg[:, h, :], self.Hhg[:, h, :], self.WLc[:, h:h + 1], pb[bH][:, cs], ALU.mult, ALU.add,
                         R=["Hhg", "WLc", ("B", bH)], W=["Hhg"])
            if ti == 15:
                self.dma(self.o_hgp.rearrange("h k v -> k h v"), self.Hhg[:], R=["Hhg"], q="act")
        else:
            for j, (s0, ln) in enumerate(segs):
                self.ts(self.BKm[:, 0:512], self.Khtm[:], self.rowmask[:, j:j + 1], ALU.mult, R=["Khtm", "rowmask", "BKm0", "BKm1"], W=["BKm0", "BKm1"])
                bH = self.bank(hold=True)
                for h in range(4):
                    cs = slice(h * 128, (h + 1) * 128)
                    self.mm(pb[bH][:, cs], lhsT=self.BKm[:, cs], rhs=self.Vtm[:, cs], R=["BKm0", "BKm1", "Vtm"], W=[("B", bH)])
                Hs4, Hs4k = self.load_hg_seq(j)
                for h in range(4):
                    cs = slice(h * 128, (h + 1) * 128)
                    self.stt(Hs4[:, h, :], Hs4[:, h, :], self.WLs[:, h, j:j + 1], pb[bH][:, cs], ALU.mult, ALU.add,
                             R=[Hs4k, "WLs", ("B", bH)], W=[Hs4k])
                self.dma(self.o_hgs[j].rearrange("h k v -> k h v"), Hs4[:], R=[Hs4k], q="act")
                self.release(bH)
        flat = lambda t: t[:].rearrange("p a b -> p (a b)")
        self.tt(self.t2[:], self.OTs[:], self.OTs[:], ALU.mult, R=["OTs", "t2"], W=["t2"])
        b = self.bank()
        self.mm(pb[b][:], lhsT=self.ones[:], rhs=flat(self.t2), R=["ones", "t2"], W=BK(b))
        self.ts(flat(self.t3), pb[b][:], 1.0 / 128, ALU.mult, R=BK(b) + ["t3"], W=["t3"], s2=EPS, op1=ALU.add)
        self.act(self.t3[:], self.t3[:], AF.Sqrt, R=["t3"], W=["t3"])
        self.fw.op("dve", lambda e: e.reciprocal(out=self.t3[:], in_=self.t3[:]), R=["t3"], W=["t3"])
        self.stt(self.t1[:], self.OTs[:], self.hgnc[:, 0:1], self.t3[:], ALU.mult, ALU.mult, R=["OTs", "pcols", "t3"] + K4("t1"), W=K4("t1"))
        self.act(self.t2[:], f3(bgt), AF.Silu, R=BK(bgt) + ["t2"], W=["t2"])
        self.release(bgt)
        self.tt(self.t1[:], self.t1[:], self.t2[:], ALU.mult, R=K4("t1") + ["t2"], W=K4("t1"))
        self.cp(self.obT[:], self.t1[:], R=K4("t1"), W=["obT"], eng="pool")
        self.dbg("obT", self.t1[:], [128, 4, 128], K4("t1"))

    def merge_gates(self):
        pb, BK = self.pb, self.BK
        f3 = lambda b: pb[b][:].rearrange("p (a b) -> p a b", a=4)
        for gi, (dst, dk) in enumerate([(self.sga, "sga"), (self.sga, "sga"), (self.sgb, "sgb"), (self.sgb, "sgb")]):
            wsl, wk = self.wload(self.wsc_in[8 + gi])
            b = self.bank()
            for c in range(4):
                self.proj_fm(wsl, wk, c * 128, pb[b][:, c * 128:(c + 1) * 128], ("B", b))
            half = gi % 2
            self.act(dst[:, half * 4:(half + 1) * 4, :], f3(b), AF.Sigmoid, R=BK(b), W=[(dk, half)])
            yield

    def merge(self, ti):
        pb, BK = self.pb, self.BK
        f3 = lambda b: pb[b][:].rearrange("p (a b) -> p a b", a=4)
        wa, wak = self.wload(self.wsc_up[0])
        wb, wbk = self.wload(self.wsc_up[1])
        for half in range(2):
            ba, bb_ = self.bank(), self.bank()
            for c in range(4):
                ec = half * 4 + c
                cs = slice(c * 128, (c + 1) * 128)
                for kc in range(4):
                    self.mm(pb[ba][:, cs], lhsT=wa[:, half * 4 + kc, cs], rhs=self.oaT[:, kc, :], start=(kc == 0), stop=(kc == 3),
                            R=[wak, "oaT"], W=[("B", ba)])
                for kc in range(4):
                    self.mm(pb[bb_][:, cs], lhsT=wb[:, half * 4 + kc, cs], rhs=self.obT[:, kc, :], start=(kc == 0), stop=(kc == 3),
                            R=[wbk, "obT"], W=[("B", bb_)])
            sl = slice(half * 4, (half + 1) * 4)
            self.tt(self.m1[:].rearrange("p (a b) -> p a b", a=4), f3(ba), self.sga[:, sl, :], ALU.mult, R=BK(ba) + [("sga", half)], W=["m1"])
            self.tt(self.m2[:].rearrange("p (a b) -> p a b", a=4), f3(bb_), self.sgb[:, sl, :], ALU.mult, R=BK(bb_) + [("sgb", half)], W=["m2"])
            self.tt(self.mergedT[:, sl, :], self.m1[:].rearrange("p (a b) -> p a b", a=4), self.m2[:].rearrange("p (a b) -> p a b", a=4),
                    ALU.add, R=["m1", "m2"], W=[("mergedT", half)])
        for dh in range(2):
            wo, wok = self.wload(self.wsc_out[dh])
            b = self.bank()
            for ec in range(8):
                self.mm(pb[b][:], lhsT=self.mergedT[:, ec, :], rhs=wo[:, ec, :], start=(ec == 0), stop=(ec == 7),
                        R=[wok, ("mergedT", 0), ("mergedT", 1)], W=BK(b))
            self.tt(self.x[:, dh * 512:(dh + 1) * 512], self.x[:, dh * 512:(dh + 1) * 512], pb[b][:], ALU.add, R=BK(b) + ["x"], W=["x"])
        self.dbg("x1_%d" % ti, self.x[:], [128, 1024], ["x"])
        self.dma(self.x1s[ti * 128:(ti + 1) * 128, :], self.x[:], R=["x"], W=["x1s"], q="sp")

    def passB(self):
        self.passB1()
        self.fw.barrier()
        self.passB2()

    def passB1(self):
        fw = self.fw
        sb = fw.sb
        pb, BK = self.pb, self.BK
        with ExitStack() as es:
            old_es = fw.es
            fw.es = es
            self.wslots = [sb("wslotB%d" % i, [128, 8, 512], BF16) for i in range(3)]
            self.wslot_rr = 0
            x1_ = [sb("x1_t%d" % i, [128, 1024], F32) for i in range(2)]
            xn2_ = [sb("xn2%d" % i, [128, 1024], F32) for i in range(2)]
            ssb = sb("ssB", [128, 8], F32)
            xn2T_ = [sb("xn2T%d" % i, [128, 8, 128], BF16) for i in range(2)]
            gffn = sb("gffn_bc", [128, 1024], F32)
            kst = sb("kstage", [128, 16, 128], F32)
            keysT = sb("keysT", [128, 16, 128], BF16)
            qT_ = [sb("qT%d" % i, [128, 16, 128], BF16) for i in range(2)]
            S_ = [sb("S_all%d" % i, [128, 16, 128], F32) for i in range(2)]
            Sw = sb("S_work", [128, 16, 128], F32)
            v16 = sb("v16", [128, 16, 16], F32)
            i16 = sb("i16", [128, 16, 16], U32)
            i16f = sb("i16f", [128, 16, 16], F32)
            cand = sb("cand", [128, 8, 256], F32)
            candw = sb("candw", [128, 8, 256], F32)
            cv = sb("cv", [128, 8, 16], F32)
            ci = sb("ci", [128, 8, 16], U32)
            cit = sb("cit", [128, 8, 16], U32)
            iif = sb("iif", [128, 128], F32)
            jjf = sb("jjf", [128, 128], F32)
            eq = sb("eq", [128, 128, 16], F32)
            aidx = sb("aidx", [128, 128], F32)
            bidx = sb("bidx", [128, 128], F32)
            gat = sb("gat", [128, 8, 16], F32)
            gsum = sb("gsum", [128, 8], F32)
            abg = sb("abg", [128, 3, 128], F32)
            NOH = 16
            At = [sb("At%d" % i, [128, 128], BF16) for i in range(NOH)]
            Bt = [sb("Bt%d" % i, [128, 128], BF16) for i in range(NOH)]
            GGt = sb("GGt", [128, 128, 128], BF16)
            ffb = sb("ffb", [128, 128], BF16)
            self.cp(ffb[:], self.ff[:], R=["ff"], W=["ffb"])
            self.dma(gffn[:], self.g_ffn.partition_broadcast(128), W=["gffn"])
            self.dma(kst[:], self.keys.rearrange("a k d -> k a d"), W=["kst"])
            for a4 in range(4):
                b = self.bank()
                for i in range(4):
                    self.tr(pb[b][:, i * 128:(i + 1) * 128], kst[:, a4 * 4 + i, :], R=["kst"], W=[("B", b)])
                self.cp(keysT[:, a4 * 4:(a4 + 1) * 4, :], pb[b][:].rearrange("p (a b) -> p a b", a=4), R=BK(b), W=["keysT"])
            bank6 = self.bank

            def front(ti, par):
                x1, xn2, xn2T, qT, S = x1_[par], xn2_[par], xn2T_[par], qT_[par], S_[par]
                kx = lambda n: n + str(par)
                self.dma(x1[:], self.x1s[ti * 128:(ti + 1) * 128, :], W=[kx("x1")])
                self.act(xn2[:], x1[:], AF.Square, R=[kx("x1")], W=[kx("xn2"), kx("ssB0")], accum=ssb[:, 4 * par:4 * par + 1])
                self.act(ssb[:, 4 * par + 1:4 * par + 2], ssb[:, 4 * par:4 * par + 1], AF.Sqrt, R=[kx("ssB0"), "epsc0"], W=[kx("ssB1")], bias=self.epsc[:, 0:1], scale=1.0 / 1024)
                fw.op("dve", lambda e: e.reciprocal(out=ssb[:, 4 * par + 2:4 * par + 3], in_=ssb[:, 4 * par + 1:4 * par + 2]), R=[kx("ssB1")], W=[kx("ssB2")])
                self.stt(xn2[:], x1[:], ssb[:, 4 * par + 2:4 * par + 3], gffn[:], ALU.mult, ALU.mult, R=[kx("x1"), kx("ssB2"), "gffn"], W=[kx("xn2")])
                for half in range(2):
                    b = bank6()
                    for q in range(4):
                        kc = half * 4 + q
                        self.tr(pb[b][:, q * 128:(q + 1) * 128], xn2[:, kc * 128:(kc + 1) * 128], R=[kx("xn2")], W=[("B", b)])
                    self.cp(xn2T[:, half * 4:(half + 1) * 4, :], pb[b][:].rearrange("p (a b) -> p a b", a=4), R=BK(b), W=[kx("xn2T")], eng="act")
                self.dma(self.xn2Ts[:, :, ti * 128:(ti + 1) * 128], xn2T[:], R=[kx("xn2T")], W=["xn2Ts"], q="act")
                for g in range(4):
                    wsl, wk = self.wload(self.wsc_q[g])
                    b = bank6()
                    for c in range(4):
                        for kc in range(8):
                            self.mm(pb[b][:, c * 128:(c + 1) * 128], lhsT=wsl[:, kc, c * 128:(c + 1) * 128], rhs=xn2T[:, kc, :],
                                    start=(kc == 0), stop=(kc == 7), R=[wk, kx("xn2T")], W=[("B", b)])
                    self.cp(qT[:, g * 4:(g + 1) * 4, :], pb[b][:].rearrange("p (a b) -> p a b", a=4), R=BK(b), W=[(kx("qT"), g)], eng="act")
                for g in range(4):
                    b = bank6()
                    for c in range(4):
                        a = g * 4 + c
                        self.mm(pb[b][:, c * 128:(c + 1) * 128], lhsT=qT[:, a, :], rhs=keysT[:, a, :], R=[(kx("qT"), g), "keysT"], W=[("B", b)])
                    self.cp(S[:, g * 4:(g + 1) * 4, :], pb[b][:].rearrange("p (a b) -> p a b", a=4), R=BK(b), W=[(kx("S"), g)], eng="act")

            def back(ti, par):
                S = S_[par]
                kx = lambda n: n + str(par)
                for a in range(16):
                    g = a // 4
                    fw.op("dve", (lambda a: lambda e: e.max(out=v16[:, a, 0:8], in_=S[:, a, :]))(a), R=[(kx("S"), g)], W=[("v16", a)])
                    fw.op("dve", (lambda a: lambda e: e.max_index(out=i16[:, a, 0:8], in_max=v16[:, a, 0:8], in_values=S[:, a, :]))(a),
                          R=[(kx("S"), g), ("v16", a)], W=[("i16", a)])
                    fw.op("dve", (lambda a: lambda e: e.match_replace(out=Sw[:, a, :], in_to_replace=v16[:, a, 0:8], in_values=S[:, a, :],
                                                                       imm_value=-1e30))(a), R=[(kx("S"), g), ("v16", a)], W=[("Sw", a)])
                    fw.op("dve", (lambda a: lambda e: e.max(out=v16[:, a, 8:16], in_=Sw[:, a, :]))(a), R=[("Sw", a)], W=[("v16", a)])
                    fw.op("dve", (lambda a: lambda e: e.max_index(out=i16[:, a, 8:16], in_max=v16[:, a, 8:16], in_values=Sw[:, a, :]))(a),
                          R=[("Sw", a), ("v16", a)], W=[("i16", a)])
                V16 = [("v16", a) for a in range(16)]
                I16 = [("i16", a) for a in range(16)]
                self.cp(i16f[:], i16[:], R=I16, W=["i16f"])
                for h in range(8):
                    self.tt(cand[:, h, :].rearrange("p (i j) -> p i j", i=16), v16[:, 2 * h, :, None].to_broadcast([128, 16, 16]),
                            v16[:, 2 * h + 1, None, :].to_broadcast([128, 16, 16]), ALU.add, R=V16, W=[("cand", h)])
                    fw.op("dve", (lambda h: lambda e: e.max(out=cv[:, h, 0:8], in_=cand[:, h, :]))(h), R=[("cand", h)], W=[("cv", h)])
                    fw.op("dve", (lambda h: lambda e: e.max_index(out=ci[:, h, 0:8], in_max=cv[:, h, 0:8], in_values=cand[:, h, :]))(h),
                          R=[("cand", h), ("cv", h)], W=[("ci", h)])
                    fw.op("dve", (lambda h: lambda e: e.match_replace(out=candw[:, h, :], in_to_replace=cv[:, h, 0:8], in_values=cand[:, h, :],
                                                                       imm_value=-1e30))(h), R=[("cand", h), ("cv", h)], W=[("candw", h)])
                    fw.op("dve", (lambda h: lambda e: e.max(out=cv[:, h, 8:16], in_=candw[:, h, :]))(h), R=[("candw", h)], W=[("cv", h)])
                    fw.op("dve", (lambda h: lambda e: e.max_index(out=ci[:, h, 8:16], in_max=cv[:, h, 8:16], in_values=candw[:, h, :]))(h),
                          R=[("candw", h), ("cv", h)], W=[("ci", h)])
                CV = [("cv", h) for h in range(8)]
                CI = [("ci", h) for h in range(8)]
                self.tt(gat[:], cv[:], cv[:, :, 0:1].to_broadcast([128, 8, 16]), ALU.subtract, R=CV, W=["gat"])
                self.act(gat[:], gat[:], AF.Exp, R=["gat"], W=["gat"])
                fw.op("dve", lambda e: e.tensor_reduce(out=gsum[:], in_=gat[:], axis=AX.X, op=ALU.add), R=["gat"], W=["gsum"])
                fw.op("dve", lambda e: e.reciprocal(out=gsum[:], in_=gsum[:]), R=["gsum"], W=["gsum"])
                self.tt(gat[:], gat[:], gsum[:, :, None].to_broadcast([128, 8, 16]), ALU.mult, R=["gat", "gsum"], W=["gat"])
                fw.op("dve", lambda e: e.tensor_single_scalar(out=cit[:], in_=ci[:], scalar=4, op=ALU.logical_shift_right), R=CI, W=["cit"])
                self.cp(iif[:], cit[:].rearrange("p a b -> p (a b)"), R=["cit"], W=["iif"])
                fw.op("dve", lambda e: e.tensor_single_scalar(out=cit[:], in_=ci[:], scalar=15, op=ALU.bitwise_and), R=CI + ["iif"], W=["cit"])
                self.cp(jjf[:], cit[:].rearrange("p a b -> p (a b)"), R=["cit"], W=["jjf"])
                for (src, sk, half, dst, dk) in [(iif, "iif", 0, aidx, "aidx"), (jjf, "jjf", 1, bidx, "bidx")]:
                    self.tt(eq[:], src[:, :, None].to_broadcast([128, 128, 16]), self.iota16[:, None, :].to_broadcast([128, 128, 16]),
                            ALU.is_equal, R=[sk, "iota16"], W=["eq"])
                    i1 = i16f[:].rearrange("p (h c) i -> p h c i", c=2)[:, :, half, :]
                    self.tt(eq[:].rearrange("p (h k) i -> p h k i", h=8), eq[:].rearrange("p (h k) i -> p h k i", h=8),
                            i1[:, :, None, :].to_broadcast([128, 8, 16, 16]), ALU.mult, R=["eq", "i16f"], W=["eq"])
                    fw.op("dve", (lambda dst: lambda e: e.tensor_reduce(out=dst[:], in_=eq[:], axis=AX.X, op=ALU.add))(dst), R=["eq"], W=[dk])
            def onehots(ti, par):
                b = bank6()
                self.tr(pb[b][:, 0:128], aidx[:], R=["aidx"], W=[("B", b)])
                self.tr(pb[b][:, 128:256], bidx[:], R=["bidx"], W=[("B", b)])
                self.tr(pb[b][:, 256:384], gat[:].rearrange("p a b -> p (a b)"), R=["gat"], W=[("B", b)])
                self.cp(abg[:], pb[b][:, 0:384].rearrange("p (a b) -> p a b", a=3), R=BK(b), W=["abg"], eng="act")
                for t0 in range(0, 128, 4):
                    b = bank6()
                    for tt_ in range(4):
                        t = t0 + tt_
                        o = t % NOH
                        self.ts(At[o][:], ffb[:], abg[:, 0, t:t + 1], ALU.is_equal, R=["ffb", "abg"], W=["At%d" % o],
                                s2=abg[:, 2, t:t + 1], op1=ALU.mult)
                        self.ts(Bt[o][:], ffb[:], abg[:, 1, t:t + 1], ALU.is_equal, R=["ffb", "abg"], W=["Bt%d" % o])
                        self.mm(pb[b][:, tt_ * 128:(tt_ + 1) * 128], lhsT=Bt[o][:], rhs=At[o][:], R=["At%d" % o, "Bt%d" % o], W=[("B", b)])
                    self.cp(GGt[:, :, t0:t0 + 4].rearrange("p a t -> p t a"), pb[b][:].rearrange("p (t a) -> p t a", t=4),
                            R=BK(b), W=["GGt"], eng="act")
                self.dma(self.gd[ti], GGt[:], R=["GGt"], W=["gd"], q="sp")

            tl = list(self.tiles)
            front(tl[0], 0)
            for k, ti in enumerate(tl):
                back(ti, k % 2)
                if k + 1 < len(tl):
                    front(tl[k + 1], (k + 1) % 2)
                onehots(ti, k % 2)
            fw.es = old_es

    def passB2(self):
        fw = self.fw
        sb = fw.sb
        pb, BK = self.pb, self.BK
        NTl = len(self.tiles)
        with ExitStack() as es:
            old_es = fw.es
            fw.es = es
            acc = sb("acc", [128, NT, 1024], F32)
            xT = sb("xn2T_all", [128, 8, NTOK], BF16)
            GTg = sb("GTg", [128, NT, 4, 128], BF16)
            Pm = sb("Pm", [128, 4, NTOK], BF16)
            UTg = [sb("UTg%d" % i, [128, 4, 8, 128], BF16) for i in range(2)]
            Vg = [sb("Vg%d" % i, [128, 4, 1024], BF16) for i in range(2)]
            Ust = [sb("Ust%d" % i, [128, 1024], F32) for i in range(2)]
            Vst = [sb("Vst0", [128, 1024], F32)] * 2
            Pc = [sb("Pc%d" % i, [128, 512], BF16) for i in range(2)]
            gfin = sb("gfin_bc", [128, 1024], F32)
            ssb = sb("ssB2", [128, 8], F32)
            junk = Ust[0]
            self.dma(gfin[:], self.g_fin.partition_broadcast(128), W=["gfin"])
            self.dma(acc[:], self.x1s.rearrange("(n p) d -> p n d", p=128), W=["acc"], q="act")
            self.dma(xT[:], self.xn2Ts, W=["xT"], q="sp")
            blocks = []
            tl = sorted(self.tiles)
            i = 0
            while i < len(tl):
                j = i
                while j + 1 < len(tl) and tl[j + 1] == tl[j] + 1 and (j + 1 - i) < 4:
                    j += 1
                blocks.append((tl[i], j - i + 1))
                i = j + 1
            hb = [0, 1]
            trb = [2, 3]
            accb = [[4, 5], [6, 7]]
            st_rr = 0
            pc_rr = 0
            hb_rr = 0
            for g in range(32):
                gb = g % 2
                UT, UTk = UTg[gb], "UTg%d" % gb
                V, Vk = Vg[gb], "Vg%d" % gb
                self.dma(GTg[:], self.gd[:, :, 4 * g:4 * g + 4, :].rearrange("n b a t -> b n a t"), W=["GTg"], q="sp")
                for a4 in range(4):
                    a = 4 * g + a4
                    s_ = st_rr % 2
                    st_rr += 1
                    self.dma(Ust[s_][:], self.pu[a * 128:(a + 1) * 128, :], W=["Ust%d" % s_], q="act")
                    self.dma(Vst[s_][:], self.pv[a * 128:(a + 1) * 128, :], W=["Vst0"], q="sp")
                    for half in range(2):
                        b = trb[half]
                        for q in range(4):
                            kc = half * 4 + q
                            self.tr(pb[b][:, q * 128:(q + 1) * 128], Ust[s_][:, kc * 128:(kc + 1) * 128], R=["Ust%d" % s_], W=[("B", b)])
                        self.cp(UT[:, a4, half * 4:(half + 1) * 4, :], pb[b][:].rearrange("p (a b) -> p a b", a=4), R=BK(b), W=[(UTk, a4)],
                                eng="act" if half == 0 else "dve")
                    self.cp(V[:, a4, :], Vst[s_][:], R=["Vst0"], W=[(Vk, a4)], eng="pool")
                for a4 in range(4):
                    for (t_first, nt) in blocks:
                        n = nt * 128
                        t0 = t_first * 128
                        b = hb[hb_rr % 2]
                        hb_rr += 1
                        for kc in range(8):
                            self.mm(pb[b][:, 0:n], lhsT=UT[:, a4, kc, :], rhs=xT[:, kc, t0:t0 + n], start=(kc == 0), stop=(kc == 7),
                                    R=[(UTk, a4), "xT"], W=[("B", b)])
                        pc, pck = Pc[pc_rr % 2], "Pc%d" % (pc_rr % 2)
                        pc_rr += 1
                        self.act(pc[:, 0:n], pb[b][:, 0:n], AF.Gelu, R=BK(b), W=[pck])
                        self.tt(Pm[:, a4, t0:t0 + n].rearrange("p (n t) -> p n t", t=128), pc[:, 0:n].rearrange("p (n t) -> p n t", t=128),
                                GTg[:, t_first:t_first + nt, a4, :], ALU.mult, R=[pck, "GTg"], W=[("Pm", a4)])
                for k, ti in enumerate(tl):
                    ab = accb[k % 2]
                    for dh in range(2):
                        for a4 in range(4):
                            self.mm(pb[ab[dh]][:], lhsT=Pm[:, a4, ti * 128:(ti + 1) * 128], rhs=V[:, a4, dh * 512:(dh + 1) * 512],
                                    start=(a4 == 0), stop=(a4 == 3), R=[("Pm", a4), (Vk, a4)], W=[("B", ab[dh])])
                        cs = slice(dh * 512, (dh + 1) * 512)
                        self.tt(acc[:, ti, cs], acc[:, ti, cs], pb[ab[dh]][:], ALU.add, R=BK(ab[dh]) + ["acc"], W=["acc"])
            for ti in tl:
                self.act(junk[:], acc[:, ti, :], AF.Square, R=["acc", "Ust0"], W=["Ust0", "ssB3"], accum=ssb[:, 3:4])
                self.act(ssb[:, 4:5], ssb[:, 3:4], AF.Sqrt, R=["ssB3", "epsc0"], W=["ssB4"], bias=self.epsc[:, 0:1], scale=1.0 / 1024)
                fw.op("dve", lambda e: e.reciprocal(out=ssb[:, 5:6], in_=ssb[:, 4:5]), R=["ssB4"], W=["ssB5"])
                self.stt(junk[:], acc[:, ti, :], ssb[:, 5:6], gfin[:], ALU.mult, ALU.mult, R=["acc", "ssB5", "gfin", "Ust0"], W=["Ust0"])
                self.dma(self.y[ti * 128:(ti + 1) * 128, :], junk[:], R=["Ust0"], W=["ydram"], q="sp")
            fw.es = old_es

    def _bank6(self):
        b = self.bank_rr % 6
        self.bank_rr = (b + 1) % 6
        return b


_PROG = {}


def _get_prog():
    if "p" not in _PROG:
        _PROG["p"] = Prog()
    return _PROG["p"]


def _in_maps(inputs):
    f = lambda a: np.ascontiguousarray(np.asarray(a, dtype=np.float32))
    xp = f(inputs["x_prompt"])
    xs = f(inputs["x_sample"])
    sh = f(inputs["state_rwkv_shift"])[0]
    wkv = f(inputs["state_rwkv_wkv"])[0]
    hg = f(inputs["state_hgrn"])[0]
    shared = {
        "norm_mix_g": f(inputs["norm_mix_g"])[0], "w_in": f(inputs["w_in"])[0], "rw_mu": f(inputs["rw_mu"])[0],
        "rw_w0": f(inputs["rw_w0"])[0], "rw_w2": f(inputs["rw_w2"])[0], "rw_a0": f(inputs["rw_a0"])[0],
        "rw_a2": f(inputs["rw_a2"])[0], "rw_g2": f(inputs["rw_g2"])[0], "rw_k_k": f(inputs["rw_k_k"])[0],
        "rw_k_a": f(inputs["rw_k_a"])[0], "rw_r_k": f(inputs["rw_r_k"])[0].reshape(512),
        "rw_ln_w": f(inputs["rw_ln_w"])[0], "rw_ln_b": f(inputs["rw_ln_b"])[0],
        "hg_lb_logits": f(inputs["hg_lb_logits"]), "hg_norm_g": f(inputs["hg_norm_g"])[0],
        "w_up_a": f(inputs["w_up_a"])[0], "w_up_b": f(inputs["w_up_b"])[0], "w_out": f(inputs["w_out"])[0],
        "norm_ffn_g": f(inputs["norm_ffn_g"])[0], "peer_w_q": f(inputs["peer_w_q"])[0],
        "peer_keys": f(inputs["peer_keys"])[0].reshape(16, 128, 128), "peer_u": f(inputs["peer_u"])[0],
        "peer_v": f(inputs["peer_v"])[0], "norm_final_g": f(inputs["norm_final_g"]),
    }
    maps = []
    for c in range(8):
        m = dict(shared)
        m["xin"] = np.concatenate([xp[c], xs[16 * c:16 * (c + 1)].reshape(128, 1024)], axis=0)
        m["shift0"] = sh[16 * c:16 * (c + 1)]
        m["wkv0"] = wkv[16 * c:16 * (c + 1)]
        m["hg0"] = hg[16 * c:16 * (c + 1)]
        maps.append(m)
    return maps


def kernel(**inputs):
    prog = _get_prog()
    maps = _in_maps(inputs)
    res = run_bass_kernel_spmd(prog.nc, maps, core_ids=list(range(8)))
    r = res.results
    y_p = np.stack([r[c]["y"][:2048] for c in range(8)], 0)
    y_s = np.concatenate([r[c]["y"][2048:].reshape(16, 8, 1024) for c in range(8)], 0)
    sh_p = np.stack([r[c]["o_shp"][0] for c in range(8)], 0)[None]
    wkv_p = np.stack([r[c]["o_wkvp"] for c in range(8)], 0)[None]
    hg_p = np.stack([r[c]["o_hgp"] for c in range(8)], 0)[None]
    sh_s = np.concatenate([r[c]["o_shs"] for c in range(8)], 0)[None]
    wkv_s = np.concatenate([r[c]["o_wkvs"] for c in range(8)], 0)[None]
    hg_s = np.concatenate([r[c]["o_hgs"] for c in range(8)], 0)[None]
    return tuple(np.ascontiguousarray(a, dtype=np.float32) for a in (y_p, y_s, sh_p, wkv_p, hg_p, sh_s, wkv_s, hg_s))
```

```python
import math
from contextlib import ExitStack

import numpy as np
import concourse.bass as bass
import concourse.mybir as mybir
from concourse.bass_utils import run_bass_kernel_spmd

F32 = mybir.dt.float32
BF16 = mybir.dt.bfloat16
I32 = mybir.dt.int32
U32 = mybir.dt.uint32
AF = mybir.ActivationFunctionType
ALU = mybir.AluOpType
AX = mybir.AxisListType

ENGS = ("pe", "act", "dve", "pool", "sp")
NT = 17
NTOK = NT * 128
EPS = 1e-6
GN_EPS = 64e-5


class FW:
    def __init__(self, nc, es, n_chan=48):
        self.nc = nc
        self.es = es
        self.ops = {e: [] for e in ENGS}
        self.sem = {e: es.enter_context(nc.semaphore("s_" + e)) for e in ENGS}
        self.cnt = {e: 0 for e in ENGS}
        self.known = {e: {} for e in ENGS}
        self.state = {}
        self.chans = [es.enter_context(nc.semaphore("c%d" % i)) for i in range(n_chan)]
        self.chan_cnt = [0] * n_chan
        self.chan_rr = 0
        self.stopped = False

    def sb(self, name, shape, dt):
        return self.es.enter_context(self.nc.sbuf_tensor(name, list(shape), dt))

    def ps(self, name, shape, dt):
        return self.es.enter_context(self.nc.psum_tensor(name, list(shape), dt))

    def _deps(self, eng, R, W):
        toks = []
        for r in R:
            st = self.state.get(r)
            if st and st[0]:
                toks.append(st[0])
        for w in W:
            st = self.state.get(w)
            if st:
                if st[0]:
                    toks.append(st[0])
                toks.extend(st[1])
        waits = {}
        kn = self.known[eng]
        for (sem, val, src) in toks:
            if eng == "pe" and src == "pe":
                continue
            if kn.get(id(sem), 0) >= val:
                continue
            if waits.get(id(sem), (None, 0))[1] < val:
                waits[id(sem)] = (sem, val)
        for k, (sem, val) in waits.items():
            kn[k] = val
        return list(waits.values())

    def _commit(self, tok, R, W):
        for r in R:
            st = self.state.setdefault(r, [None, []])
            st[1].append(tok)
        for w in W:
            self.state[w] = [tok, []]

    def op(self, eng, fn, R=(), W=()):
        if self.stopped:
            return
        R = list(R)
        W = list(W)
        W += [r for r in R if isinstance(r, tuple) and r[0] == "B" and r not in W]
        waits = self._deps(eng, R, W)
        self.cnt[eng] += 1
        tok = (self.sem[eng], self.cnt[eng], eng)
        self.ops[eng].append((waits, fn, (self.sem[eng], 1)))
        self._commit(tok, R, W)

    def dma(self, q, fn, R=(), W=()):
        if self.stopped:
            return
        R = list(R)
        W = list(W)
        waits = self._deps(q, R, W)
        chan = self.chan_rr
        self.chan_rr = (self.chan_rr + 1) % len(self.chans)
        prev = self.chan_cnt[chan]
        if prev and self.known[q].get(id(self.chans[chan]), 0) < prev:
            waits = [w for w in waits if w[0] is not self.chans[chan]] + [(self.chans[chan], prev)]
            self.known[q][id(self.chans[chan])] = prev
        self.chan_cnt[chan] += 16
        tok = (self.chans[chan], self.chan_cnt[chan], "dma")
        self.ops[q].append((waits, fn, (self.chans[chan], 16)))
        self._commit(tok, R, W)

    def barrier(self):
        for e in ENGS:
            waits = []
            for o in ENGS:
                if o != e and self.cnt[o] and self.known[e].get(id(self.sem[o]), 0) < self.cnt[o]:
                    waits.append((self.sem[o], self.cnt[o]))
                    self.known[e][id(self.sem[o])] = self.cnt[o]
            for i, c in enumerate(self.chans):
                if self.chan_cnt[i] and self.known[e].get(id(c), 0) < self.chan_cnt[i]:
                    waits.append((c, self.chan_cnt[i]))
                    self.known[e][id(c)] = self.chan_cnt[i]
            if waits:
                self.ops[e].append((waits, None, None))
        self.state = {}

    def finish(self, eng="sp"):
        waits = []
        for o in ENGS:
            if self.cnt[o] and o != eng:
                waits.append((self.sem[o], self.cnt[o]))
        for i, c in enumerate(self.chans):
            if self.chan_cnt[i]:
                waits.append((c, self.chan_cnt[i]))
        self.ops[eng].append((waits, None, None))

    def replay(self):
        ops = self.ops

        def run(engname, eng):
            for (waits, fn, inc) in ops[engname]:
                for (sem, val) in waits:
                    eng.wait_ge(sem, val)
                if fn is not None:
                    fn(eng).then_inc(inc[0], inc[1])

        with self.nc.Block() as block:
            @block.tensor
            def _(e):
                run("pe", e)

            @block.scalar
            def _(e):
                run("act", e)

            @block.vector
            def _(e):
                run("dve", e)

            @block.gpsimd
            def _(e):
                run("pool", e)

            @block.sync
            def _(e):
                run("sp", e)


WG = [(0, 512), (512, 512), (1024, 512), (1536, 256), (1792, 512), (2304, 512), (2816, 512),
      (3328, 512), (3840, 512), (4352, 512), (4864, 512), (5376, 512)]


class _Stop(Exception):
    pass


class Prog:
    def __init__(self, debug=False, tiles=None, do_peer=True, stop=None):
        self.debug = debug
        self.stop = stop
        self.tiles = list(range(NT)) if tiles is None else list(tiles)
        self.do_peer = do_peer
        self.held = set()
        self.dbg_names = []
        nc = self.nc = bass.Bass("TRN2", target_bir_lowering=False)
        di = lambda n, s, dt=F32: nc.dram_tensor(n, list(s), dt, kind="ExternalInput").ap()
        do = lambda n, s, dt=F32: nc.dram_tensor(n, list(s), dt, kind="ExternalOutput").ap()
        ds = lambda n, s, dt=F32: nc.dram_tensor(n, list(s), dt, kind="Internal").ap()
        self.xin = di("xin", [NTOK, 1024])
        self.shift0 = di("shift0", [16, 1024])
        self.wkv0 = di("wkv0", [16, 8, 64, 64])
        self.hg0 = di("hg0", [16, 4, 128, 128])
        self.g_mix = di("norm_mix_g", [1024])
        self.w_in = di("w_in", [1024, 5888])
        self.rw_mu = di("rw_mu", [1792])
        self.rw_w0 = di("rw_w0", [512])
        self.rw_w2 = di("rw_w2", [64, 512])
        self.rw_a0 = di("rw_a0", [512])
        self.rw_a2 = di("rw_a2", [64, 512])
        self.rw_g2 = di("rw_g2", [128, 512])
        self.rw_k_k = di("rw_k_k", [512])
        self.rw_k_a = di("rw_k_a", [512])
        self.rw_r_k = di("rw_r_k", [512])
        self.rw_ln_w = di("rw_ln_w", [512])
        self.rw_ln_b = di("rw_ln_b", [512])
        self.hg_lb = di("hg_lb_logits", [2, 512])
        self.hg_ng = di("hg_norm_g", [128])
        self.w_up_a = di("w_up_a", [512, 1024])
        self.w_up_b = di("w_up_b", [512, 1024])
        self.w_out = di("w_out", [1024, 1024])
        self.g_ffn = di("norm_ffn_g", [1024])
        self.w_q = di("peer_w_q", [1024, 2048])
        self.keys = di("peer_keys", [16, 128, 128])
        self.pu = di("peer_u", [16384, 1024])
        self.pv = di("peer_v", [16384, 1024])
        self.g_fin = di("norm_final_g", [1024])
        self.y = do("y", [NTOK, 1024])
        self.o_shp = do("o_shp", [1, 1024])
        self.o_wkvp = do("o_wkvp", [8, 64, 64])
        self.o_hgp = do("o_hgp", [4, 128, 128])
        self.o_shs = do("o_shs", [16, 1024])
        self.o_wkvs = do("o_wkvs", [16, 8, 64, 64])
        self.o_hgs = do("o_hgs", [16, 4, 128, 128])
        self.wsc_in = ds("wsc_in", [12, 128, 8, 512], BF16)
        self.wsc_up = ds("wsc_up", [2, 128, 8, 512], BF16)
        self.wsc_out = ds("wsc_out", [2, 128, 8, 512], BF16)
        self.wsc_q = ds("wsc_q", [4, 128, 8, 512], BF16)
        self.x1s = ds("x1s", [NTOK, 1024], F32)
        self.gd = ds("gd", [NT, 128, 128, 128], BF16)
        self.xn2Ts = ds("xn2Ts", [128, 8, NTOK], BF16)
        self.do = do
        with ExitStack() as es:
            self.es = es
            self.fw = FW(nc, es)
            try:
                self.build()
            except _Stop:
                pass
            self.fw.finish("sp")
            self.fw.replay()

    def mm(self, out, lhsT, rhs, start=True, stop=True, R=(), W=(), sgc=False):
        self.fw.op("pe", lambda e: e.matmul(out, lhsT=lhsT, rhs=rhs, start=start, stop=stop,
                                            skip_group_check=sgc), R=R, W=W)

    def tr(self, out, in_, R=(), W=(), k=128):
        ident = self.ident
        self.fw.op("pe", lambda e: e.transpose(out, in_, ident[0:k, 0:k]), R=list(R) + ["ident"], W=W)

    def act(self, out, in_, func, R=(), W=(), bias=None, scale=None, accum=None):
        kw = {}
        if bias is not None:
            kw["bias"] = bias
        if scale is not None:
            kw["scale"] = scale
        if accum is not None:
            kw["accum_out"] = accum
        self.fw.op("act", lambda e: e.activation(out=out, in_=in_, func=func, **kw), R=R, W=W)

    def tt(self, out, in0, in1, op, R=(), W=(), eng="dve"):
        self.fw.op(eng, lambda e: e.tensor_tensor(out=out, in0=in0, in1=in1, op=op), R=R, W=W)

    def ts(self, out, in0, s1, op0, R=(), W=(), s2=None, op1=None, eng="dve"):
        if op1 is None:
            self.fw.op(eng, lambda e: e.tensor_scalar(out=out, in0=in0, scalar1=s1, scalar2=None, op0=op0),
                       R=R, W=W)
        else:
            self.fw.op(eng, lambda e: e.tensor_scalar(out=out, in0=in0, scalar1=s1, scalar2=s2, op0=op0, op1=op1),
                       R=R, W=W)

    def stt(self, out, in0, scalar, in1, op0, op1, R=(), W=(), accum=None):
        if accum is None:
            self.fw.op("dve", lambda e: e.scalar_tensor_tensor(out=out, in0=in0, scalar=scalar, in1=in1,
                                                               op0=op0, op1=op1), R=R, W=W)
        else:
            self.fw.op("dve", lambda e: e.scalar_tensor_tensor(out=out, in0=in0, scalar=scalar, in1=in1,
                                                               op0=op0, op1=op1, accum_out=accum), R=R, W=W)

    def cp(self, out, in_, R=(), W=(), eng="dve"):
        if eng == "act":
            self.fw.op("act", lambda e: e.copy(out=out, in_=in_), R=R, W=W)
        else:
            self.fw.op(eng, lambda e: e.tensor_copy(out=out, in_=in_), R=R, W=W)

    def memset(self, ap, val, W=(), eng="pool"):
        self.fw.op(eng, lambda e: e.memset(ap, val), W=W)

    def dma(self, out, in_, R=(), W=(), q="sp", slow=False):
        if slow:
            self.fw.dma(q, lambda e: e.dma_start(out=out, in_=in_, allow_slow_non_contiguous=True), R=R, W=W)
        else:
            self.fw.dma(q, lambda e: e.dma_start(out=out, in_=in_), R=R, W=W)

    def dbg(self, name, ap, shape, R, dt=F32):
        if not self.debug or self.fw.stopped:
            return
        if ("dbg_" + name) in self.dbg_names:
            return
        if getattr(self, "cur_tile", None) is not None and self.cur_tile != self.tiles[-1]:
            return
        d = self.do("dbg_" + name, shape, dt)
        self.dbg_names.append("dbg_" + name)
        self.dma(d, ap, R=R)

    def pt(self, name):
        if self.stop == name:
            self.fw.stopped = True

    def bank(self, hold=False):
        while self.bank_rr in self.held:
            self.bank_rr = (self.bank_rr + 1) % 8
        b = self.bank_rr
        self.bank_rr = (self.bank_rr + 1) % 8
        if hold:
            self.held.add(b)
        return b

    def release(self, *bs):
        for b in bs:
            self.held.discard(b)

    @staticmethod
    def BK(b, qs=(0, 1, 2, 3)):
        return [("B", b)]

    def build(self):
        fw = self.fw
        self.bank_rr = 0
        self.pb = [fw.ps("pb%d" % i, [128, 512], F32) for i in range(8)]
        self.consts()
        self.pt("consts")
        self.prologue()
        self.pt("prologue")
        self.passA()
        fw.barrier()
        if self.do_peer:
            self.passB()

    def consts(self):
        fw = self.fw
        sb = fw.sb
        fi = sb("fidx_i", [128, 128], I32)
        pi = sb("pidx_i", [128, 1], I32)
        ti = sb("tmp_i", [128, 128], I32)
        tpi = sb("tmpp_i", [128, 1], I32)
        ff = self.ff = sb("fidx_f", [128, 128], F32)
        pf = sb("pidx_f", [128, 1], F32)
        fb = sb("fblk_f", [128, 128], F32)
        pbk = sb("pblk_f", [128, 1], F32)
        tmpf = sb("tmpc_f", [128, 128], F32)
        same = sb("same_f", [128, 128], F32)
        self.ident = sb("ident", [128, 128], F32)
        self.ones = sb("ones_f", [128, 128], F32)
        self.bm = sb("bm_f", [128, 128], F32)
        self.mU2 = sb("mU2", [128, 256], F32)
        self.mLs = sb("mLs", [128, 128], F32)
        self.mU2b = sb("mU2b", [128, 256], F32)
        self.mLsb = sb("mLsb", [128, 128], F32)
        self.rmask_s = sb("rmask_s", [128, 128], F32)
        self.rowmask = sb("rowmask", [128, 16], F32)
        self.epsc = sb("epsc", [128, 2], F32)
        self.iota16 = sb("iota16", [128, 16], F32)
        fw.op("pool", lambda e: e.iota(fi[:], pattern=[[1, 128]], base=0, channel_multiplier=0), W=["fi"])
        fw.op("pool", lambda e: e.iota(pi[:], pattern=[[0, 1]], base=0, channel_multiplier=1), W=["pi"])
        self.cp(ff[:], fi[:], R=["fi"], W=["ff"])
        self.cp(pf[:], pi[:], R=["pi"], W=["pf"])
        self.cp(self.iota16[:], fi[:, 0:16], R=["fi"], W=["iota16"])
        self.memset(self.ones[:], 1.0, W=["ones"])
        self.memset(self.epsc[:, 0:1], EPS, W=["epsc0"])
        self.memset(self.epsc[:, 1:2], GN_EPS, W=["epsc1"])
        self.ts(self.ident[:], ff[:], pf[:, 0:1], ALU.is_equal, R=["ff", "pf"], W=["ident"])
        self.ts(self.mU2[:, 0:128], ff[:], pf[:, 0:1], ALU.is_gt, R=["ff", "pf"], W=["mU2a"])
        self.ts(self.mU2[:, 128:256], ff[:], pf[:, 0:1], ALU.is_ge, R=["ff", "pf"], W=["mU2b_"])
        self.ts(self.mLs[:], ff[:], pf[:, 0:1], ALU.is_lt, R=["ff", "pf"], W=["mLs"])
        sh = lambda o, i, n, R, W: fw.op("dve", lambda e: e.tensor_single_scalar(out=o, in_=i, scalar=n,
                                                                                  op=ALU.arith_shift_right), R=R, W=W)
        sh(ti[:], fi[:], 3, ["fi"], ["ti"])
        self.cp(fb[:], ti[:], R=["ti"], W=["fb"])
        sh(tpi[:], pi[:], 3, ["pi"], ["tpi"])
        self.cp(pbk[:], tpi[:], R=["tpi"], W=["pbk"])
        self.ts(same[:], fb[:], pbk[:, 0:1], ALU.is_equal, R=["fb", "pbk"], W=["same"])
        self.tt(self.mU2b[:, 0:128], self.mU2[:, 0:128], same[:], ALU.mult, R=["mU2a", "same"], W=["mU2ba"])
        self.tt(self.mU2b[:, 128:256], self.mU2[:, 128:256], same[:], ALU.mult, R=["mU2b_", "same"], W=["mU2bb"])
        self.tt(self.mLsb[:], self.mLs[:], same[:], ALU.mult, R=["mLs", "same"], W=["mLsb"])
        self.ts(self.rowmask[:], ff[:, 0:16], pbk[:, 0:1], ALU.is_equal, R=["ff", "pbk"], W=["rowmask"])
        fw.op("dve", lambda e: e.tensor_single_scalar(out=ti[:], in_=fi[:], scalar=7, op=ALU.bitwise_and),
              R=["fi", "fb"], W=["ti"])
        self.cp(tmpf[:], ti[:], R=["ti"], W=["tmpf"])
        self.ts(self.rmask_s[:], tmpf[:], 0.5, ALU.is_gt, R=["tmpf"], W=["rmask_s"])
        sh(ti[:], fi[:], 6, ["fi", "tmpf"], ["ti"])
        self.cp(tmpf[:], ti[:], R=["ti"], W=["tmpf"])
        sh(tpi[:], pi[:], 6, ["pi", "pbk"], ["tpi"])
        self.cp(pbk[:], tpi[:], R=["tpi", "same", "rowmask"], W=["pbk2"])
        self.ts(self.bm[:], tmpf[:], pbk[:, 0:1], ALU.is_equal, R=["tmpf", "pbk2"], W=["bm"])
        for k in ["ident", "ones", "bm", "mU2a", "mU2b_", "mLs", "mU2ba", "mU2bb", "mLsb", "rmask_s", "rowmask",
                  "epsc0", "epsc1", "iota16"]:
            pass
        plist = [('muc', self.rw_mu, 14), ('w0c', self.rw_w0, 4), ('a0c', self.rw_a0, 4), ('kkc', self.rw_k_k, 4),
                 ('kac', self.rw_k_a, 4), ('rkc', self.rw_r_k, 4), ('lnwc', self.rw_ln_w, 4), ('lnbc', self.rw_ln_b, 4),
                 ('hgnc', self.hg_ng, 1), ('lb0c', self.hg_lb[0, :], 4), ('lb1c', self.hg_lb[1, :], 4)]
        pst = sb("pstage", [64, 128], F32)
        pcols = sb("pcols", [128, 64], F32)
        self.memset(pst[:], 0.0, W=["pstage"])
        r0 = 0
        offs = {}
        for (nm, ap, n) in plist:
            self.dma(pst[r0:r0 + n, :], ap.rearrange("(c p) -> c p", p=128), W=["pstage"])
            offs[nm] = (r0, n)
            r0 += n
        b = self.bank()
        self.tr(self.pb[b][:, 0:64], pst[:], R=["pstage"], W=self.BK(b), k=64)
        self.cp(pcols[:], self.pb[b][:, 0:64], R=self.BK(b), W=["pcols"])

        class _V:
            def __init__(s_, t, r, n):
                s_.t, s_.r, s_.n = t, r, n

            def __getitem__(s_, idx):
                if isinstance(idx, tuple):
                    a, c = idx
                    if isinstance(c, slice):
                        c = slice((c.start or 0) + s_.r, (s_.n if c.stop is None else c.stop) + s_.r)
                        return s_.t[a, c]
                    return s_.t[a, c + s_.r]
                return s_.t[idx, s_.r:s_.r + s_.n]
        for nm in offs:
            setattr(self, nm, _V(pcols, offs[nm][0], offs[nm][1]))
        self.pkeys = "pcols"
        l0, l1 = self.lb0c, self.lb1c
        self.ommc = sb("ommc", [128, 14], F32)
        self.ts(self.ommc[:], self.muc[:], -1.0, ALU.mult, R=["pcols"], W=["ommc"], s2=1.0, op1=ALU.add)
        self.lbc = sb("lbc", [128, 4], F32)
        self.omlbc = sb("omlbc", [128, 4], F32)
        lbt = sb("lbt", [128, 4], F32)
        self.tt(lbt[:], l0[:], l1[:], ALU.subtract, R=["pcols"], W=["lbt"])
        self.act(self.lbc[:], lbt[:], AF.Sigmoid, R=["lbt"], W=["lbc"])
        self.ts(self.omlbc[:], self.lbc[:], -1.0, ALU.mult, R=["lbc"], W=["omlbc"], s2=1.0, op1=ALU.add)
        self.gmix_bc = sb("gmix_bc", [128, 1024], F32)
        self.dma(self.gmix_bc[:], self.g_mix.partition_broadcast(128), W=["gmix_bc"])
        st = sb("lr_stage", [128, 1024], F32)
        self.w2a2 = sb("w2a2", [128, 512], BF16)
        self.g2b = sb("g2b", [128, 512], BF16)
        self.dma(st[0:64, 0:512], self.rw_w2, W=["lr_stage"])
        self.dma(st[64:128, 0:512], self.rw_a2, W=["lr_stage2"])
        self.dma(st[:, 512:1024], self.rw_g2, W=["lr_stage3"])
        self.cp(self.w2a2[:], st[:, 0:512], R=["lr_stage", "lr_stage2"], W=["w2a2"])
        self.cp(self.g2b[:], st[:, 512:1024], R=["lr_stage3"], W=["g2b"])

    def _col(self, name, ap, n):
        t = self.fw.sb(name, [128, n], F32)
        self.dma(t[:], ap.rearrange("(c p) -> p c", p=128), W=[name], slow=True)
        return t

    def prologue(self):
        with ExitStack() as es:
            old = self.fw.es
            self.fw.es = es
            self._prologue()
            self.fw.es = old

    def _prologue(self):
        sb = self.fw.sb
        stg = [sb("wstg%d" % i, [128, 8, 512], F32) for i in range(2)]
        cvt = [sb("wcvt%d" % i, [128, 8, 512], BF16) for i in range(2)]
        for i in range(2):
            self.memset(cvt[i][:], 0.0, W=["wcvt%d" % i])
        jobs = []
        for g, (c0, n) in enumerate(WG):
            jobs.append((self.wsc_in[g], [(slice(0, 8), self.w_in[:, c0:c0 + n], n)]))
        for i, w in enumerate([self.w_up_a, self.w_up_b]):
            jobs.append((self.wsc_up[i], [(slice(0, 4), w[:, 0:512], 512), (slice(4, 8), w[:, 512:1024], 512)]))
        for i in range(2):
            jobs.append((self.wsc_out[i], [(slice(0, 8), self.w_out[:, i * 512:(i + 1) * 512], 512)]))
        for i in range(4):
            jobs.append((self.wsc_q[i], [(slice(0, 8), self.w_q[:, i * 512:(i + 1) * 512], 512)]))
        engs = ["act", "dve", "pool"]
        for ji, (dst, parts) in enumerate(jobs):
            s = ji % 2
            sk, ck = "wstg%d" % s, "wcvt%d" % s
            ncol = parts[0][2]
            for (ks, src, n) in parts:
                self.dma(stg[s][:, ks, 0:n], src.rearrange("(kc p) c -> p kc c", p=128), W=[sk],
                         q="sp" if ji % 2 == 0 else "act")
            self.cp(cvt[s][:, :, 0:ncol], stg[s][:, :, 0:ncol], R=[sk], W=[ck], eng=engs[ji % 3])
            self.dma(dst[:, :, :], cvt[s][:, :, :], R=[ck], W=[("wsc", id(dst))], q="sp")
        self.fw.barrier()

    def wload(self, src):
        s = self.wslot_rr
        self.wslot_rr = (self.wslot_rr + 1) % len(self.wslots)
        self.dma(self.wslots[s][:], src, W=[("wslot", s)], q="sp")
        return self.wslots[s], ("wslot", s)

    def passA(self):
        fw = self.fw
        sb = fw.sb
        with ExitStack() as es:
            old_es = fw.es
            fw.es = es
            self.wslots = [sb("wslot%d" % i, [128, 8, 512], BF16) for i in range(4)]
            self.wslot_rr = 0
            F = lambda n: sb(n, [128, 4, 128], F32)
            self.x = sb("x_t", [128, 1024], F32)
            self.xn = sb("xn_t", [128, 1024], F32)
            self.ss = sb("ss", [128, 4], F32)
            self.xnT = sb("xnT", [128, 8, 128], BF16)
            self.xpT = sb("xpT", [128, 8, 128], BF16)
            self.lastcol = sb("lastcol", [128, 8, 1], BF16)
            self.s0 = sb("s0", [16, 1024], F32)
            self.s0T = sb("s0T", [128, 8, 16], BF16)
            self.zs0 = sb("zs0", [128, 128], F32)
            self.rT, self.kT, self.vT = F("rT"), F("kT"), F("vT")
            self.lowd = sb("lowd", [128, 128], F32)
            self.lowg = sb("lowg", [128, 128], F32)
            self.tl = sb("tl", [128, 128], BF16)
            self.tla = sb("tla", [128, 128], BF16)
            self.AR0 = sb("AR0", [128, 4, 256], F32)
            self.AR1 = sb("AR1", [128, 4, 256], F32)
            self.sg = sb("sg", [128, 128], BF16)
            self.wlog, self.cum, self.alr, self.gT = F("wlog"), F("cum"), F("alr"), F("gT")
            self.kk, self.kp, self.bb = F("kk"), F("kp"), F("bb")
            self.t1, self.t2, self.t3 = F("t1"), F("t2"), F("t3")
            self.AR = sb("AR", [128, 4, 256], F32)
            self.Kt, self.Bt, self.Kh, self.Bh = F("Kt"), F("Bt"), F("Kh"), F("Bh")
            self.bon = F("bon")
            self.Vtm = sb("Vtm", [128, 512], F32)
            self.Khtm = sb("Khtm", [128, 512], F32)
            self.Bhtm = sb("Bhtm", [128, 512], F32)
            self.Xtm = sb("Xtm", [128, 512], F32)
            self.Utm = sb("Utm", [128, 512], F32)
            self.XTs = sb("XTs", [128, 128], F32)
            self.NA3h = [sb("NA3h%d" % i, [128, 4, 256], F32) for i in range(2)]
            self.A24h = [sb("A24h%d" % i, [128, 4, 256], F32) for i in range(2)]
            self.Pmh = [[F("Pm0h%d" % i), F("Pm1h%d" % i)] for i in range(2)]
            self.Qmh = [F("Qmh%d" % i) for i in range(2)]
            self.Gmh = [[F("Gm0h%d" % i), F("Gm1h%d" % i)] for i in range(2)]
            self.A24 = self.A24h[0]
            self.Hst = F("Hst")
            self.WLc = sb("WLc", [128, 4], F32)
            self.WLs = sb("WLs", [128, 4, 16], F32)
            self.OTs = F("OTs")
            self.oaT = sb("oaT", [128, 4, 128], BF16)
            self.obT = sb("obT", [128, 4, 128], BF16)
            self.Hhg = F("Hhg")
            self.sga = sb("sga", [128, 8, 128], BF16)
            self.sgb = sb("sgb", [128, 8, 128], BF16)
            self.mergedT = sb("mergedT", [128, 8, 128], BF16)
            self.m1 = sb("m1", [128, 512], F32)
            self.m2 = sb("m2", [128, 512], F32)
            self.HsS = [sb("HsS%d" % i, [128, 128], F32) for i in range(8)]
            self.HsL = [sb("HsL%d" % i, [128, 128], F32) for i in range(4)]
            self.Hg4 = [sb("Hg4_%d" % i, [128, 4, 128], F32) for i in range(2)]
            self.Hg4_rr = 0
            self.Lcp = sb("Lcp", [128, 16, 64], F32)
            self.Sop = sb("Sop", [128, 16, 64], F32)
            self.HsS_rr = 0
            self.HsL_rr = 0
            self.BKm = sb("BKm", [128, 1024], F32)
            self.memset(self.lastcol[:], 0.0, W=["lastcol"])
            self.memset(self.tl[:], 0.0, W=["tl0"])
            self.memset(self.tla[:], 0.0, W=["tl1"])
            self.memset(self.AR0[:], 0.0, W=["AR0"])
            self.memset(self.AR1[:], 0.0, W=["AR1"])
            self.memset(self.Hst[:], 0.0, W=["Hst"])
            self.memset(self.Hhg[:], 0.0, W=["Hhg"])
            for i in range(len(self.HsL)):
                self.memset(self.HsL[i][:], 0.0, W=["HsL%d" % i])
            try:
                self.sbuf_left_A = self.nc.sbuf_bytes_remaining
            except Exception as ex:
                self.sbuf_left_A = str(ex)
            for ti in self.tiles:
                self.mixer_tile(ti)
            fw.barrier()
            fw.es = old_es

    def proj_fm(self, wsl, wk, off, out_ap, out_key, shift=False):
        for kc in range(8):
            self.mm(out_ap, lhsT=wsl[:, kc, off:off + 128], rhs=(self.xpT if shift else self.xnT)[:, kc, :],
                    start=(kc == 0), stop=(kc == 7), R=[wk, "xpT" if shift else "xnT"], W=[out_key])

    def mixer_tile(self, ti):
        self.cur_tile = ti
        sample = (ti == 16)
        pb = self.pb
        BK = self.BK
        x, xn = self.x, self.xn
        self.dma(x[:], self.xin[ti * 128:(ti + 1) * 128, :], W=["x"])
        self.act(xn[:], x[:], AF.Square, R=["x"], W=["xn", "ss0"], accum=self.ss[:, 0:1])
        self.act(self.ss[:, 1:2], self.ss[:, 0:1], AF.Sqrt, R=["ss0", "epsc0"], W=["ss1"], bias=self.epsc[:, 0:1],
                 scale=1.0 / 1024)
        self.fw.op("dve", lambda e: e.reciprocal(out=self.ss[:, 2:3], in_=self.ss[:, 1:2]), R=["ss1"], W=["ss2"])
        self.stt(xn[:], x[:], self.ss[:, 2:3], self.gmix_bc[:], ALU.mult, ALU.mult, R=["x", "ss2", "gmix_bc"], W=["xn"])
        if ti == 15:
            self.dma(self.o_shp[0:1, :], xn[127:128, :], R=["xn"], q="act")
        if sample:
            self.dma(self.o_shs[:, :], xn[7:128:8, :], R=["xn"], q="act")
        b0, b1 = self.bank(), self.bank()
        for kc in range(8):
            b = b0 if kc < 4 else b1
            q = kc % 4
            self.tr(pb[b][:, q * 128:(q + 1) * 128], xn[:, kc * 128:(kc + 1) * 128], R=["xn"], W=[("B", b)])
        self.cp(self.xnT[:, 0:4, :], pb[b0][:].rearrange("p (a b) -> p a b", a=4), R=BK(b0), W=["xnT"], eng="act")
        self.cp(self.xnT[:, 4:8, :], pb[b1][:].rearrange("p (a b) -> p a b", a=4), R=BK(b1), W=["xnT"], eng="act")
        self.cp(self.xpT[:, :, 1:128], self.xnT[:, :, 0:127], R=["xnT"], W=["xpT"], eng="pool")
        if not sample:
            self.cp(self.xpT[:, :, 0:1], self.lastcol[:], R=["lastcol"], W=["xpT"], eng="pool")
            self.cp(self.lastcol[:], self.xnT[:, :, 127:128], R=["xnT", "xpT"], W=["lastcol"], eng="pool")
        else:
            self.dma(self.s0[:], self.shift0, W=["s0"])
            b = self.bank()
            for kc in range(8):
                self.tr(pb[b][:, kc * 16:(kc + 1) * 16], self.s0[:, kc * 128:(kc + 1) * 128], R=["s0"], W=BK(b), k=16)
            self.cp(self.s0T[:], pb[b][:, 0:128].rearrange("p (a b) -> p a b", a=8), R=BK(b), W=["s0T"])
            self.cp(self.xpT[:, :, 0:128:8], self.s0T[:], R=["s0T"], W=["xpT"], eng="pool")
        self.dbg("xn", self.xn[:], [128, 1024], ["xn"])
        self.pt("t0")
        self.rwkv(ti, sample)
        self.pt("rwkv")
        self.hgrn(ti, sample)
        self.pt("hgrn")
        self.merge(ti)

    def shiftmix(self, wsl, wk, off, c, dst, dkey):
        pb = self.pb
        bz, bp = self.bank(), self.bank()
        self.proj_fm(wsl, wk, off, pb[bz][:, 0:128], ("B", bz), shift=False)
        self.proj_fm(wsl, wk, off, pb[bp][:, 0:128], ("B", bp), shift=True)
        self.act(self.zs0[:], pb[bz][:, 0:128], AF.Copy, R=[("B", bz), "ommc"], W=["zs0"], scale=self.ommc[:, c:c + 1])
        self.stt(dst, pb[bp][:, 0:128], self.muc[:, c:c + 1], self.zs0[:], ALU.mult, ALU.add,
                 R=[("B", bp), "zs0", "pcols"], W=[dkey])

    def rwkv(self, ti, sample):
        pb, BK = self.pb, self.BK
        segs = [(8 * j, 8) for j in range(16)] if sample else [(0, 128)]
        mU2 = self.mU2b if sample else self.mU2
        mLs = self.mLsb if sample else self.mLs
        mU2k = ["mU2ba", "mU2bb"] if sample else ["mU2a", "mU2b_"]
        mLsk = "mLsb" if sample else "mLs"
        nlev = 3 if sample else 7
        for g, dst, name in [(0, self.rT, "rT"), (1, self.kT, "kT"), (2, self.vT, "vT")]:
            wsl, wk = self.wload(self.wsc_in[g])
            for c4 in range(4):
                self.shiftmix(wsl, wk, c4 * 128, g * 4 + c4, dst[:, c4, :], (name, c4))
        wsl, wk = self.wload(self.wsc_in[3])
        self.shiftmix(wsl, wk, 0, 12, self.lowd[:], "lowd")
        self.shiftmix(wsl, wk, 128, 13, self.lowg[:], "lowg")
        self.dbg("rT", self.rT[:], [128, 4, 128], [("rT", c) for c in range(4)])
        self.pt("rwproj")
        for _ in self.merge_gates():
            pass
        self.act(self.tl[0:64, :], self.lowd[0:64, :], AF.Tanh, R=["lowd"], W=["tl0"])
        self.act(self.tla[64:128, :], self.lowd[64:128, :], AF.Copy, R=["lowd"], W=["tl1"])
        self.act(self.sg[:], self.lowg[:], AF.Sigmoid, R=["lowg"], W=["sg"])
        self.pt("lr1")
        bd, ba, bg = self.bank(), self.bank(), self.bank()
        for p in range(4):
            cs = slice(p * 128, (p + 1) * 128)
            self.mm(pb[bd][:, cs], lhsT=self.w2a2[:, cs], rhs=self.tl[:], R=["w2a2", "tl0"], W=[("B", bd)])
            self.mm(pb[ba][:, cs], lhsT=self.w2a2[:, cs], rhs=self.tla[:], R=["w2a2", "tl1"], W=[("B", ba)])
            self.mm(pb[bg][:, cs], lhsT=self.g2b[:, cs], rhs=self.sg[:], R=["g2b", "sg"], W=[("B", bg)])
        self.pt("lr2")
        for p in range(4):
            cs = slice(p * 128, (p + 1) * 128)
            self.ts(self.t1[:, p, :], pb[bd][:, cs], self.w0c[:, p:p + 1], ALU.add, R=[("B", bd), "pcols"], W=[("t1", p)])
            self.ts(self.alr[:, p, :], pb[ba][:, cs], self.a0c[:, p:p + 1], ALU.add, R=[("B", ba), "pcols"], W=[("alr", p)])
        self.act(self.t1[:], self.t1[:], AF.Sigmoid, R=[("t1", c) for c in range(4)], W=[("t1", c) for c in range(4)])
        self.act(self.alr[:], self.alr[:], AF.Sigmoid, R=[("alr", c) for c in range(4)], W=[("alr", c) for c in range(4)])
        for p in range(0):
            pass
        self.pt("lr3")
        self.cp(self.gT[:], pb[bg][:].rearrange("p (a b) -> p a b", a=4), R=BK(bg), W=["gT"], eng="act")
        self.pt("lr")
        K4 = lambda n: [(n, c) for c in range(4)]
        bc = lambda t: t[:][:, :, None].to_broadcast([128, 4, 128])
        self.ts(self.wlog[:], self.t1[:], -math.exp(-0.5), ALU.mult, R=K4("t1"), W=["wlog"])
        self.tt(self.kk[:], self.kT[:], bc(self.kkc), ALU.mult, R=K4("kT") + ["pcols"], W=["kk"])
        self.tt(self.t2[:], self.kk[:], self.kk[:], ALU.mult, R=["kk"], W=["t2"])
        b = self.bank()
        self.mm(pb[b][:], lhsT=self.bm[:], rhs=self.t2[:].rearrange("p a b -> p (a b)"), R=["bm", "t2"], W=BK(b))
        self.act(self.t3[:], pb[b][:].rearrange("p (a b) -> p a b", a=4), AF.Sqrt, R=BK(b), W=["t3"])
        self.ts(self.t3[:], self.t3[:], 1e-12, ALU.max, R=["t3"], W=["t3"])
        self.fw.op("dve", lambda e: e.reciprocal(out=self.t3[:], in_=self.t3[:]), R=["t3"], W=["t3"])
        self.tt(self.kk[:], self.kk[:], self.t3[:], ALU.mult, R=["kk", "t3"], W=["kk"])
        self.pt("kk")
        self.ts(self.t2[:], self.alr[:], -1.0, ALU.add, R=K4("alr") + ["t2"], W=["t2"])
        self.tt(self.t2[:], self.t2[:], bc(self.kac), ALU.mult, R=["t2", "pcols"], W=["t2"])
        self.stt(self.kp[:], self.t2[:], 1.0, self.kT[:], ALU.add, ALU.mult, R=["t2"] + K4("kT"), W=["kp"])
        self.tt(self.bb[:], self.kk[:], self.alr[:], ALU.mult, R=["kk"] + K4("alr"), W=["bb"])
        self.tt(self.t2[:], self.rT[:], self.kp[:], ALU.mult, R=K4("rT") + ["kp"], W=["t2"])
        self.tt(self.t2[:], self.t2[:], bc(self.rkc), ALU.mult, R=["t2", "pcols"], W=["t2"])
        b = self.bank()
        self.mm(pb[b][:], lhsT=self.bm[:], rhs=self.t2[:].rearrange("p a b -> p (a b)"), R=["bm", "t2"], W=BK(b))
        self.tt(self.bon[:], pb[b][:].rearrange("p (a b) -> p a b", a=4), self.vT[:], ALU.mult, R=BK(b) + K4("vT"), W=["bon"])
        self.pt("bon")
        rm = self.rmask_s if sample else self.ones
        rmk = "rmask_s" if sample else "ones"
        for p in range(4):
            self.fw.op("dve", (lambda p: lambda e: e.tensor_tensor_scan(out=self.cum[:, p, :], data0=rm[:], data1=self.wlog[:, p, :],
                                                                         initial=0.0, op0=ALU.mult, op1=ALU.add))(p),
                       R=["wlog", rmk], W=[("cum", p)])
        self.pt("scan")
        Kc = K4("cum")
        self.tt(self.t2[:], self.cum[:], self.wlog[:], ALU.subtract, R=Kc + ["wlog", "t2"], W=["t2"])
        self.act(self.t2[:], self.t2[:], AF.Exp, R=["t2"], W=["t2"])
        self.stt(self.AR[:, :, 0:128], self.kk[:], -1.0, self.t2[:], ALU.mult, ALU.mult, R=["kk", "t2"], W=["ARa"])
        self.act(self.t3[:], self.cum[:], AF.Exp, R=Kc + ["t3"], W=["t3"])
        self.tt(self.AR[:, :, 128:256], self.rT[:], self.t3[:], ALU.mult, R=K4("rT") + ["t3"], W=["ARr"])
        self.cp(self.AR0[0:64, :, :], self.AR[0:64, :, :], R=["ARa", "ARr"], W=["AR0"], eng="pool")
        self.cp(self.AR1[64:128, :, :], self.AR[64:128, :, :], R=["ARa", "ARr"], W=["AR1"], eng="pool")
        self.act(self.t1[:], self.cum[:], AF.Exp, R=Kc + K4("t1") + ["wlog"], W=K4("t1"), scale=-1.0)
        self.tt(self.Kt[:], self.kp[:], self.t1[:], ALU.mult, R=["kp"] + K4("t1"), W=["Kt"])
        self.tt(self.Bt[:], self.bb[:], self.t1[:], ALU.mult, R=["bb"] + K4("t1"), W=["Bt"])
        if not sample:
            for p in range(4):
                self.act(self.t2[:, p, :], self.cum[:, p, :], AF.Exp, R=[("cum", p), "t2", "ARa"], W=["t2"], scale=-1.0,
                         bias=self.cum[:, p, 127:128])
            self.act(self.WLc[:], self.cum[:, :, 127], AF.Exp, R=Kc, W=["WLc"])
        else:
            c4 = self.cum[:].rearrange("p a (j t) -> p (a j) t", t=8)
            self.tt(self.t2[:].rearrange("p a (j t) -> p (a j) t", t=8), c4[:, :, 7:8].to_broadcast([128, 64, 8]), c4,
                    ALU.subtract, R=Kc + ["t2", "ARa"], W=["t2"])
            self.act(self.t2[:], self.t2[:], AF.Exp, R=["t2"], W=["t2"])
            self.act(self.WLs[:], self.cum[:, :, 7:128:8], AF.Exp, R=Kc, W=["WLs"])
        self.tt(self.Kh[:], self.kp[:], self.t2[:], ALU.mult, R=["kp", "t2"], W=["Kh"])
        self.tt(self.Bh[:], self.bb[:], self.t2[:], ALU.mult, R=["bb", "t2"], W=["Bh"])
        self.pt("exp")
        for src, sk, dst, dk in [(self.vT, K4("vT"), self.Vtm, "Vtm"), (self.Kh, ["Kh"], self.Khtm, "Khtm"),
                                 (self.Bh, ["Bh"], self.Bhtm, "Bhtm")]:
            b = self.bank()
            for p in range(4):
                self.tr(pb[b][:, p * 128:(p + 1) * 128], src[:, p, :], R=sk, W=[("B", b)])
            self.cp(dst[:], pb[b][:], R=BK(b), W=[dk], eng="act")
        self.dbg("AR", self.AR[:], [128, 4, 256], ["ARa", "ARr"])
        self.dbg("Kt", self.Kt[:], [128, 4, 128], ["Kt"])
        self.dbg("Bt", self.Bt[:], [128, 4, 128], ["Bt"])
        self.pt("rwprep")
        bO = self.bank(hold=True)
        def hg_gen(hg):
            NA3, A24, Pm_, Qm_, Gm_ = self.NA3h[hg], self.A24h[hg], self.Pmh[hg], self.Qmh[hg], self.Gmh[hg]
            sfx = "h%d" % hg
            heads = [4 * hg + i for i in range(4)]
            bA = [self.bank(), self.bank()]
            bB = [self.bank(), self.bank()]
            bP = self.bank()
            for i, h in enumerate(heads):
                p, P = h // 2, slice((h % 2) * 64, (h % 2) * 64 + 64)
                hs = slice((i % 2) * 256, (i % 2) * 256 + 256)
                qa = [(i % 2) * 2, (i % 2) * 2 + 1]
                ARp, ARpk = (self.AR0, "AR0") if h % 2 == 0 else (self.AR1, "AR1")
                self.mm(pb[bA[i // 2]][:, hs], lhsT=self.Bt[:, p, :], rhs=ARp[:, p, :], R=["Bt", ARpk], W=BK(bA[i // 2], qa))
                self.mm(pb[bB[i // 2]][:, hs], lhsT=self.Kt[:, p, :], rhs=ARp[:, p, :], R=["Kt", ARpk], W=BK(bB[i // 2], qa))
                self.mm(pb[bP][:, i * 128:(i + 1) * 128], lhsT=ARp[:, p, 0:128], rhs=self.Bt[:, p, :], R=["Bt", ARpk], W=[("B", bP)])
            for i in range(4):
                hs = slice((i % 2) * 256, (i % 2) * 256 + 256)
                qa = [(i % 2) * 2, (i % 2) * 2 + 1]
                self.tt(NA3[:, i, :], pb[bA[i // 2]][:, hs], mU2[:], ALU.mult, R=BK(bA[i // 2], qa) + mU2k, W=[("NA3" + sfx, i)])
                self.tt(A24[:, i, :], pb[bB[i // 2]][:, hs], mU2[:], ALU.mult, R=BK(bB[i // 2], qa) + mU2k, W=[("A24" + sfx, i)])
            self.tt(Pm_[0][:], pb[bP][:].rearrange("p (a b) -> p a b", a=4), mLs[:, None, :].to_broadcast([128, 4, 128]),
                    ALU.mult, R=BK(bP) + [mLsk], W=["Pm0" + sfx])
            yield
            NA3k = [("NA3" + sfx, i) for i in range(4)]
            self.tt(Gm_[0][:], NA3[:, :, 0:128], self.ident[:, None, :].to_broadcast([128, 4, 128]), ALU.add,
                    R=NA3k + ["ident"], W=["Gm0" + sfx])
            Q, Qk = (lambda i: NA3[:, i, 0:128]), NA3k
            cur = 0
            for lvl in range(nlev - 1):
                nxt = 1 - cur
                Pc, Pk = Pm_[cur], "Pm%d" % cur + sfx
                Pn, Pnk = Pm_[nxt], "Pm%d" % nxt + sfx
                b1, b2, b3 = self.bank(), self.bank(), self.bank()
                for i in range(4):
                    cs = slice(i * 128, (i + 1) * 128)
                    self.mm(pb[b1][:, cs], lhsT=Q(i), rhs=Pc[:, i, :], R=Qk + [Pk], W=[("B", b1)])
                    self.mm(pb[b2][:, cs], lhsT=Pc[:, i, :], rhs=Q(i), R=Qk + [Pk], W=[("B", b2)])
                self.cp(Pn[:], pb[b1][:].rearrange("p (a b) -> p a b", a=4), R=BK(b1), W=[Pnk], eng="act")
                self.cp(Qm_[:], pb[b2][:].rearrange("p (a b) -> p a b", a=4), R=BK(b2), W=["Qm" + sfx])
                Q, Qk = (lambda i: Qm_[:, i, :]), ["Qm" + sfx]
                Gc, Gk = Gm_[cur], "Gm%d" % cur + sfx
                Gn, Gnk = Gm_[nxt], "Gm%d" % nxt + sfx
                for i in range(4):
                    cs = slice(i * 128, (i + 1) * 128)
                    self.mm(pb[b3][:, cs], lhsT=Pn[:, i, :], rhs=Gc[:, i, :], R=[Pnk, Gk], W=[("B", b3)])
                self.tt(Gn[:], pb[b3][:].rearrange("p (a b) -> p a b", a=4), Gc[:], ALU.add, R=BK(b3) + [Gk], W=[Gnk])
                cur = nxt
                yield
            G, Gk = Gm_[cur], "Gm%d" % cur + sfx
            if hg == 0 and ti == 0:
                self.dbg("G", G[:], [128, 4, 128], [Gk])
            for pp in range(2):
                p = 2 * hg + pp
                bX = self.bank(hold=True)
                for hh in range(2):
                    i = 2 * pp + hh
                    h = heads[i]
                    P = slice(hh * 64, hh * 64 + 64)
                    self.mm(pb[bX][P, 0:128], lhsT=self.Vtm[:, h * 64:(h + 1) * 64], rhs=A24[:, i, 0:128], start=True, stop=False,
                            R=["Vtm", ("A24" + sfx, i)], W=[("B", bX)], sgc=True)
                self.state_mm(pb[bX], 0, ("B", bX), self.AR, 0, ["ARa"], p, segs, sample, "rw")
                self.cp(self.XTs[:], pb[bX][:, 0:128], R=[("B", bX)], W=["XTs"], eng="act")
                self.release(bX)
                b = self.bank()
                self.tr(pb[b][:, 0:128], self.XTs[:], R=["XTs"], W=[("B", b)])
                self.cp(self.Xtm[:, p * 128:(p + 1) * 128], pb[b][:, 0:128], R=[("B", b)], W=[("Xtm", p)])
                yield
            bU = self.bank()
            for i, h in enumerate(heads):
                self.mm(pb[bU][:, i * 64:(i + 1) * 64], lhsT=G[:, i, :], rhs=self.Xtm[:, h * 64:(h + 1) * 64], R=[Gk, ("Xtm", h // 2)],
                        W=[("B", bU)])
            self.cp(self.Utm[:, hg * 256:(hg + 1) * 256], pb[bU][:, 0:256], R=[("B", bU)], W=[("Utm", hg)])
            yield
            for pp in range(2):
                p = 2 * hg + pp
                for hh in range(2):
                    i = 2 * pp + hh
                    h = heads[i]
                    P = slice(hh * 64, hh * 64 + 64)
                    self.mm(pb[bO][P, p * 128:(p + 1) * 128], lhsT=self.Utm[:, h * 64:(h + 1) * 64], rhs=NA3[:, i, 128:256],
                            start=True, stop=False, R=[("Utm", hg), ("NA3" + sfx, i)], W=[("B", bO)], sgc=True)
                    self.mm(pb[bO][P, p * 128:(p + 1) * 128], lhsT=self.Vtm[:, h * 64:(h + 1) * 64], rhs=A24[:, i, 128:256],
                            start=False, stop=False, R=["Vtm", ("A24" + sfx, i)], W=[("B", bO)], sgc=True)
                self.state_mm(pb[bO], p, ("B", bO), self.AR, 128, ["ARr"], p, segs, sample, "rw")
                yield
            self.rw_state_update(ti, hg, heads, segs, sample)
        gens = [hg_gen(0), hg_gen(1)]
        while gens:
            for g_ in list(gens):
                try:
                    next(g_)
                except StopIteration:
                    gens.remove(g_)
        self.cp(self.OTs[:], pb[bO][:].rearrange("p (a b) -> p a b", a=4), R=BK(bO), W=["OTs"], eng="act")
        self.release(bO)
        self.dbg("OTs", self.OTs[:], [128, 4, 128], ["OTs"])
        flat = lambda t: t[:].rearrange("p a b -> p (a b)")
        b = self.bank()
        self.mm(pb[b][:], lhsT=self.bm[:], rhs=flat(self.OTs), R=["bm", "OTs"], W=BK(b))
        self.stt(flat(self.t1), pb[b][:], -1.0 / 64, flat(self.OTs), ALU.mult, ALU.add, R=BK(b) + ["OTs"] + K4("t1"), W=K4("t1"))
        self.tt(self.t2[:], self.t1[:], self.t1[:], ALU.mult, R=K4("t1") + ["t2"], W=["t2"])
        b = self.bank()
        self.mm(pb[b][:], lhsT=self.bm[:], rhs=flat(self.t2), R=["bm", "t2"], W=BK(b))
        self.ts(flat(self.t3), pb[b][:], 1.0 / 64, ALU.mult, R=BK(b) + ["t3"], W=["t3"], s2=GN_EPS, op1=ALU.add)
        self.act(self.t3[:], self.t3[:], AF.Sqrt, R=["t3"], W=["t3"])
        self.fw.op("dve", lambda e: e.reciprocal(out=self.t3[:], in_=self.t3[:]), R=["t3"], W=["t3"])
        self.tt(self.t1[:], self.t1[:], self.t3[:], ALU.mult, R=K4("t1") + ["t3"], W=K4("t1"))
        self.tt(self.t1[:], self.t1[:], bc(self.lnwc), ALU.mult, R=K4("t1") + ["pcols"], W=K4("t1"))
        self.tt(self.t1[:], self.t1[:], bc(self.lnbc), ALU.add, R=K4("t1") + ["pcols"], W=K4("t1"))
        self.tt(self.t1[:], self.t1[:], self.bon[:], ALU.add, R=K4("t1") + ["bon"], W=K4("t1"))
        self.tt(self.t1[:], self.t1[:], self.gT[:], ALU.mult, R=K4("t1") + ["gT"], W=K4("t1"))
        self.cp(self.oaT[:], self.t1[:], R=K4("t1"), W=["oaT"], eng="pool")
        self.dbg("oaT", self.t1[:], [128, 4, 128], K4("t1"))

    def state_mm(self, bank_t, q, okey, src, off, skeys, p, segs, sample, kind):
        cs0 = q * 128
        if not sample:
            H, Hk = (self.Hst, "Hst") if kind == "rw" else (self.Hhg, "Hhg")
            self.mm(bank_t[:, cs0:cs0 + 128], lhsT=H[:, p, :], rhs=src[:, p, off:off + 128], start=False, stop=True,
                    R=[Hk] + skeys, W=[okey], sgc=True)
        else:
            if kind == "rw":
                self.load_pair_states(p)
            for j, (s0, ln) in enumerate(segs):
                Hs, Hsk = self.sample_state(j, p, kind)
                self.mm(bank_t[:, cs0 + s0:cs0 + s0 + ln], lhsT=Hs[:], rhs=src[:, p, off + s0:off + s0 + ln], start=False,
                        stop=(j == len(segs) - 1), R=[Hsk] + skeys, W=[okey], sgc=True)

    def load_hg_seq(self, j):
        i = self.Hg4_rr
        self.Hg4_rr = (i + 1) % len(self.Hg4)
        self.dma(self.Hg4[i][:], self.hg0[j].rearrange("h k v -> k h v"), W=["Hg4_%d" % i], q="sp")
        return self.Hg4[i], "Hg4_%d" % i

    def load_pair_states(self, p):
        for g4 in range(4):
            self.dma(self.Lcp[:, 4 * g4:4 * g4 + 4, :],
                     self.wkv0[4 * g4:4 * g4 + 4, 2 * p:2 * p + 2, :, :].rearrange("j hh v k -> (hh v) j k"), W=["Lcp"], q="sp")

    def sample_state(self, j, p, kind):
        s = self.HsS_rr
        self.HsS_rr = (s + 1) % len(self.HsS)
        dst, dk = self.HsS[s], "HsS%d" % s
        if kind == "hg":
            self.dma(dst[:], self.hg0[j, p, :, :], W=[dk], q="act")
            return dst, dk
        l = self.HsL_rr
        self.HsL_rr = (l + 1) % len(self.HsL)
        L, Lk = self.HsL[l], "HsL%d" % l
        for hh in range(2):
            self.cp(L[hh * 64:(hh + 1) * 64, hh * 64:(hh + 1) * 64], self.Lcp[hh * 64:(hh + 1) * 64, j, :], R=["Lcp"], W=[Lk], eng="pool")
        b = self.bank()
        self.tr(self.pb[b][:, 0:128], L[:], R=[Lk], W=[("B", b)])
        self.cp(dst[:], self.pb[b][:, 0:128], R=[("B", b)], W=[dk], eng="act")
        return dst, dk

    def rw_state_update(self, ti, hg, heads, segs, sample):
        pb = self.pb
        if not sample:
            for pp in range(2):
                p = 2 * hg + pp
                bH = self.bank()
                for hh in range(2):
                    h = heads[2 * pp + hh]
                    P = slice(hh * 64, hh * 64 + 64)
                    cs = slice(hh * 64, hh * 64 + 64)
                    hc = slice(h * 64, (h + 1) * 64)
                    self.mm(pb[bH][P, cs], lhsT=self.Bhtm[:, hc], rhs=self.Utm[:, hc], start=True, stop=False,
                            R=["Bhtm", ("Utm", hg)], W=[("B", bH)], sgc=True)
                    self.mm(pb[bH][P, cs], lhsT=self.Khtm[:, hc], rhs=self.Vtm[:, hc], start=False, stop=True,
                            R=["Khtm", "Vtm"], W=[("B", bH)], sgc=True)
                for hh in range(2):
                    P = slice(hh * 64, hh * 64 + 64)
                    cs = slice(hh * 64, hh * 64 + 64)
                    self.stt(self.Hst[P, p, cs], self.Hst[P, p, cs], self.WLc[P, p:p + 1], pb[bH][P, cs], ALU.mult, ALU.add,
                             R=["Hst", "WLc", ("B", bH)], W=["Hst"])
                if ti == 15:
                    self.store_rw_state(self.Hst[:, p, :], "Hst", lambda h: self.o_wkvp[h, :, :], p)
        else:
            for pp in range(2):
                p = 2 * hg + pp
                self.load_pair_states(p)
                pc = slice(p * 128, (p + 1) * 128)
                for j, (s0, ln) in enumerate(segs):
                    self.ts(self.BKm[:, 0:128], self.Bhtm[:, pc], self.rowmask[:, j:j + 1], ALU.mult, R=["Bhtm", "rowmask"], W=["BKm0"])
                    self.ts(self.BKm[:, 128:256], self.Khtm[:, pc], self.rowmask[:, j:j + 1], ALU.mult, R=["Khtm", "rowmask"], W=["BKm1"])
                    bH = self.bank(hold=True)
                    for hh in range(2):
                        h = heads[2 * pp + hh]
                        P = slice(hh * 64, hh * 64 + 64)
                        cs = slice(hh * 64, hh * 64 + 64)
                        hc = slice(h * 64, (h + 1) * 64)
                        self.mm(pb[bH][P, cs], lhsT=self.BKm[:, hh * 64:(hh + 1) * 64], rhs=self.Utm[:, hc], start=True, stop=False,
                                R=["BKm0", ("Utm", hg)], W=[("B", bH)], sgc=True)
                        self.mm(pb[bH][P, cs], lhsT=self.BKm[:, 128 + hh * 64:128 + (hh + 1) * 64], rhs=self.Vtm[:, hc],
                                start=False, stop=True, R=["BKm1", "Vtm"], W=[("B", bH)], sgc=True)
                    Hs, Hsk = self.sample_state(j, p, "rw")
                    for hh in range(2):
                        P = slice(hh * 64, hh * 64 + 64)
                        cs = slice(hh * 64, hh * 64 + 64)
                        self.stt(Hs[P, cs], Hs[P, cs], self.WLs[P, p, j:j + 1], pb[bH][P, cs], ALU.mult, ALU.add,
                                 R=[Hsk, "WLs", ("B", bH)], W=[Hsk])
                    self.release(bH)
                    b = self.bank()
                    self.tr(pb[b][:, 0:128], Hs[:], R=[Hsk], W=[("B", b)])
                    for hh in range(2):
                        P = slice(hh * 64, hh * 64 + 64)
                        self.cp(self.Sop[P, j, :], pb[b][P, hh * 64:(hh + 1) * 64], R=[("B", b)], W=["Sop"], eng="act")
                for g4 in range(4):
                    self.dma(self.o_wkvs[4 * g4:4 * g4 + 4, 2 * p:2 * p + 2, :, :].rearrange("j hh v k -> (hh v) j k"),
                             self.Sop[:, 4 * g4:4 * g4 + 4, :], R=["Sop"], q="act")

    def store_rw_state(self, H_ap, Hk, dst_of_head, p):
        b = self.bank()
        self.tr(self.pb[b][:, 0:128], H_ap, R=[Hk], W=[("B", b)])
        l = self.HsL_rr
        self.HsL_rr = (l + 1) % len(self.HsL)
        st, sk = self.HsL[l], "HsL%d" % l
        for hh in range(2):
            P = slice(hh * 64, hh * 64 + 64)
            self.cp(st[P, hh * 64:(hh + 1) * 64], self.pb[b][P, hh * 64:(hh + 1) * 64], R=[("B", b)], W=[sk], eng="act")
        for hh in range(2):
            P = slice(hh * 64, hh * 64 + 64)
            self.dma(dst_of_head(2 * p + hh), st[P, hh * 64:(hh + 1) * 64], R=[sk], q="act")

    def hgrn(self, ti, sample):
        pb, BK = self.pb, self.BK
        segs = [(8 * j, 8) for j in range(16)] if sample else [(0, 128)]
        mUi = (self.mU2b if sample else self.mU2)[:, 128:256]
        mUik = "mU2bb" if sample else "mU2b_"
        K4 = lambda n: [(n, c) for c in range(4)]
        f3 = lambda b: pb[b][:].rearrange("p (a b) -> p a b", a=4)
        bq, bf_, bgt, bi = self.bank(hold=True), self.bank(hold=True), self.bank(hold=True), self.bank(hold=True)
        wsl, wk = self.wload(self.wsc_in[4])
        for c in range(4):
            self.proj_fm(wsl, wk, c * 128, pb[bq][:, c * 128:(c + 1) * 128], ("B", bq))
        wsl, wk = self.wload(self.wsc_in[5])
        for c in range(4):
            self.proj_fm(wsl, wk, c * 128, pb[bf_][:, c * 128:(c + 1) * 128], ("B", bf_))
        wsl, wk = self.wload(self.wsc_in[6])
        for kc in range(8):
            self.mm(pb[bi][:], lhsT=self.xnT[:, kc, :], rhs=wsl[:, kc, :], start=(kc == 0), stop=(kc == 7), R=["xnT", wk], W=BK(bi))
        wsl, wk = self.wload(self.wsc_in[7])
        for c in range(4):
            self.proj_fm(wsl, wk, c * 128, pb[bgt][:, c * 128:(c + 1) * 128], ("B", bgt))
        self.cp(self.Vtm[:], pb[bi][:], R=BK(bi), W=["Vtm"], eng="act")
        self.release(bi)
        self.act(self.t1[:], f3(bf_), AF.Sigmoid, R=BK(bf_) + K4("t1"), W=K4("t1"))
        for h in range(4):
            self.ts(self.t1[:, h, :], self.t1[:, h, :], self.omlbc[:, h:h + 1], ALU.mult, R=[("t1", h), "omlbc", "lbc"], W=[("t1", h)],
                    s2=self.lbc[:, h:h + 1], op1=ALU.add)
        self.release(bf_)
        self.act(self.wlog[:], self.t1[:], AF.Ln, R=K4("t1") + ["wlog"], W=["wlog"])
        self.ts(self.kp[:], self.t1[:], -1.0, ALU.mult, R=K4("t1") + ["kp"], W=["kp"], s2=1.0, op1=ALU.add)
        rm = self.rmask_s if sample else self.ones
        rmk = "rmask_s" if sample else "ones"
        for h in range(4):
            self.fw.op("dve", (lambda h: lambda e: e.tensor_tensor_scan(out=self.cum[:, h, :], data0=rm[:], data1=self.wlog[:, h, :],
                                                                         initial=0.0, op0=ALU.mult, op1=ALU.add))(h),
                       R=["wlog", rmk], W=[("cum", h)])
        Kc = K4("cum")
        self.act(self.t3[:], self.cum[:], AF.Exp, R=Kc + ["t3"], W=["t3"])
        self.tt(self.AR[:, :, 128:256], f3(bq), self.t3[:], ALU.mult, R=BK(bq) + ["t3", "ARr"], W=["ARr"])
        self.release(bq)
        self.act(self.t1[:], self.cum[:], AF.Exp, R=Kc + K4("t1"), W=K4("t1"), scale=-1.0)
        self.tt(self.Kt[:], self.kp[:], self.t1[:], ALU.mult, R=["kp"] + K4("t1") + ["Kt"], W=["Kt"])
        if not sample:
            for h in range(4):
                self.act(self.t2[:, h, :], self.cum[:, h, :], AF.Exp, R=[("cum", h), "t2"], W=["t2"], scale=-1.0, bias=self.cum[:, h, 127:128])
            self.act(self.WLc[:], self.cum[:, :, 127], AF.Exp, R=Kc + ["WLc"], W=["WLc"])
        else:
            c4 = self.cum[:].rearrange("p a (j t) -> p (a j) t", t=8)
            self.tt(self.t2[:].rearrange("p a (j t) -> p (a j) t", t=8), c4[:, :, 7:8].to_broadcast([128, 64, 8]), c4,
                    ALU.subtract, R=Kc + ["t2"], W=["t2"])
            self.act(self.t2[:], self.t2[:], AF.Exp, R=["t2"], W=["t2"])
            self.act(self.WLs[:], self.cum[:, :, 7:128:8], AF.Exp, R=Kc + ["WLs"], W=["WLs"])
        self.tt(self.Kh[:], self.kp[:], self.t2[:], ALU.mult, R=["kp", "t2", "Kh"], W=["Kh"])
        b = self.bank()
        for h in range(4):
            self.tr(pb[b][:, h * 128:(h + 1) * 128], self.Kh[:, h, :], R=["Kh"], W=[("B", b)])
        self.cp(self.Khtm[:], pb[b][:], R=BK(b), W=["Khtm"], eng="act")
        b = self.bank()
        for h in range(4):
            self.mm(pb[b][:, h * 128:(h + 1) * 128], lhsT=self.Kt[:, h, :], rhs=self.AR[:, h, 128:256], R=["Kt", "ARr"], W=[("B", b)])
        self.tt(self.A24[:, :, 0:128], f3(b), mUi[:, None, :].to_broadcast([128, 4, 128]), ALU.mult,
                R=BK(b) + [mUik] + [("A24h0", i) for i in range(4)], W=[("A24h0", i) for i in range(4)])
        bO = self.bank(hold=True)
        if not sample:
            for h in range(4):
                self.mm(pb[bO][:, h * 128:(h + 1) * 128], lhsT=self.Vtm[:, h * 128:(h + 1) * 128], rhs=self.A24[:, h, 0:128], start=True, stop=False,
                        R=["Vtm", ("A24h0", h)], W=[("B", bO)], sgc=True)
                self.state_mm(pb[bO], h, ("B", bO), self.AR, 128, ["ARr"], h, segs, sample, "hg")
        else:
            for h in range(4):
                self.mm(pb[bO][:, h * 128:(h + 1) * 128], lhsT=self.Vtm[:, h * 128:(h + 1) * 128], rhs=self.A24[:, h, 0:128], start=(h == 0),
                        stop=False, R=["Vtm", ("A24h0", h)], W=[("B", bO)], sgc=True)
            for j, (s0, ln) in enumerate(segs):
                Hs4, Hs4k = self.load_hg_seq(j)
                for h in range(4):
                    self.mm(pb[bO][:, h * 128 + s0:h * 128 + s0 + ln], lhsT=Hs4[:, h, :], rhs=self.AR[:, h, 128 + s0:128 + s0 + ln], start=False,
                            stop=(j == len(segs) - 1 and h == 3), R=[Hs4k, "ARr"], W=[("B", bO)], sgc=True)
        self.cp(self.OTs[:], f3(bO), R=BK(bO), W=["OTs"], eng="act")
        self.release(bO)
        if not sample:
            bH = self.bank()
            for h in range(4):
                cs = slice(h * 128, (h + 1) * 128)
                self.mm(pb[bH][:, cs], lhsT=self.Khtm[:, cs], rhs=self.Vtm[:, cs], R=["Khtm", "Vtm"], W=[("B", bH)])
            for h in range(4):
                cs = slice(h * 128, (h + 1) * 128)
                self.stt(self.Hhg[:, h, :], self.Hhg[:, h, :], self.WLc[:, h:h + 1], pb[bH][:, cs], ALU.mult, ALU.add,
                         R=["Hhg", "WLc", ("B", bH)], W=["Hhg"])
            if ti == 15:
                self.dma(self.o_hgp.rearrange("h k v -> k h v"), self.Hhg[:], R=["Hhg"], q="act")
        else:
            for j, (s0, ln) in enumerate(segs):
                self.ts(self.BKm[:, 0:512], self.Khtm[:], self.rowmask[:, j:j + 1], ALU.mult, R=["Khtm", "rowmask", "BKm0", "BKm1"], W=["BKm0", "BKm1"])
                bH = self.bank(hold=True)
                for h in range(4):
                    cs = slice(h * 128, (h + 1) * 128)
                    self.mm(pb[bH][:, cs], lhsT=self.BKm[:, cs], rhs=self.Vtm[:, cs], R=["BKm0", "BKm1", "Vtm"], W=[("B", bH)])
                Hs4, Hs4k = self.load_hg_seq(j)
                for h in range(4):
                    cs = slice(h * 128, (h + 1) * 128)
                    self.stt(Hs4[:, h, :], Hs4[:, h, :], self.WLs[:, h, j:j + 1], pb[bH][:, cs], ALU.mult, ALU.add,
                             R=[Hs4k, "WLs", ("B", bH)], W=[Hs4k])
                self.dma(self.o_hgs[j].rearrange("h k v -> k h v"), Hs4[:], R=[Hs4k], q="act")
                self.release(bH)
        flat = lambda t: t[:].rearrange("p a b -> p (a b)")
        self.tt(self.t2[:], self.OTs[:], self.OTs[:], ALU.mult, R=["OTs", "t2"], W=["t2"])
        b = self.bank()
        self.mm(pb[b][:], lhsT=self.ones[:], rhs=flat(self.t2), R=["ones", "t2"], W=BK(b))
        self.ts(flat(self.t3), pb[b][:], 1.0 / 128, ALU.mult, R=BK(b) + ["t3"], W=["t3"], s2=EPS, op1=ALU.add)
        self.act(self.t3[:], self.t3[:], AF.Sqrt, R=["t3"], W=["t3"])
        self.fw.op("dve", lambda e: e.reciprocal(out=self.t3[:], in_=self.t3[:]), R=["t3"], W=["t3"])
        self.stt(self.t1[:], self.OTs[:], self.hgnc[:, 0:1], self.t3[:], ALU.mult, ALU.mult, R=["OTs", "pcols", "t3"] + K4("t1"), W=K4("t1"))
        self.act(self.t2[:], f3(bgt), AF.Silu, R=BK(bgt) + ["t2"], W=["t2"])
        self.release(bgt)
        self.tt(self.t1[:], self.t1[:], self.t2[:], ALU.mult, R=K4("t1") + ["t2"], W=K4("t1"))
        self.cp(self.obT[:], self.t1[:], R=K4("t1"), W=["obT"], eng="pool")
        self.dbg("obT", self.t1[:], [128, 4, 128], K4("t1"))

    def merge_gates(self):
        pb, BK = self.pb, self.BK
        f3 = lambda b: pb[b][:].rearrange("p (a b) -> p a b", a=4)
        for gi, (dst, dk) in enumerate([(self.sga, "sga"), (self.sga, "sga"), (self.sgb, "sgb"), (self.sgb, "sgb")]):
            wsl, wk = self.wload(self.wsc_in[8 + gi])
            b = self.bank()
            for c in range(4):
                self.proj_fm(wsl, wk, c * 128, pb[b][:, c * 128:(c + 1) * 128], ("B", b))
            half = gi % 2
            self.act(dst[:, half * 4:(half + 1) * 4, :], f3(b), AF.Sigmoid, R=BK(b), W=[(dk, half)])
            yield

    def merge(self, ti):
        pb, BK = self.pb, self.BK
        f3 = lambda b: pb[b][:].rearrange("p (a b) -> p a b", a=4)
        wa, wak = self.wload(self.wsc_up[0])
        wb, wbk = self.wload(self.wsc_up[1])
        for half in range(2):
            ba, bb_ = self.bank(), self.bank()
            for c in range(4):
                ec = half * 4 + c
                cs = slice(c * 128, (c + 1) * 128)
                for kc in range(4):
                    self.mm(pb[ba][:, cs], lhsT=wa[:, half * 4 + kc, cs], rhs=self.oaT[:, kc, :], start=(kc == 0), stop=(kc == 3),
                            R=[wak, "oaT"], W=[("B", ba)])
                for kc in range(4):
                    self.mm(pb[bb_][:, cs], lhsT=wb[:, half * 4 + kc, cs], rhs=self.obT[:, kc, :], start=(kc == 0), stop=(kc == 3),
                            R=[wbk, "obT"], W=[("B", bb_)])
            sl = slice(half * 4, (half + 1) * 4)
            self.tt(self.m1[:].rearrange("p (a b) -> p a b", a=4), f3(ba), self.sga[:, sl, :], ALU.mult, R=BK(ba) + [("sga", half)], W=["m1"])
            self.tt(self.m2[:].rearrange("p (a b) -> p a b", a=4), f3(bb_), self.sgb[:, sl, :], ALU.mult, R=BK(bb_) + [("sgb", half)], W=["m2"])
            self.tt(self.mergedT[:, sl, :], self.m1[:].rearrange("p (a b) -> p a b", a=4), self.m2[:].rearrange("p (a b) -> p a b", a=4),
                    ALU.add, R=["m1", "m2"], W=[("mergedT", half)])
        for dh in range(2):
            wo, wok = self.wload(self.wsc_out[dh])
            b = self.bank()
            for ec in range(8):
                self.mm(pb[b][:], lhsT=self.mergedT[:, ec, :], rhs=wo[:, ec, :], start=(ec == 0), stop=(ec == 7),
                        R=[wok, ("mergedT", 0), ("mergedT", 1)], W=BK(b))
            self.tt(self.x[:, dh * 512:(dh + 1) * 512], self.x[:, dh * 512:(dh + 1) * 512], pb[b][:], ALU.add, R=BK(b) + ["x"], W=["x"])
        self.dbg("x1_%d" % ti, self.x[:], [128, 1024], ["x"])
        self.dma(self.x1s[ti * 128:(ti + 1) * 128, :], self.x[:], R=["x"], W=["x1s"], q="sp")

    def passB(self):
        self.passB1()
        self.fw.barrier()
        self.passB2()

    def passB1(self):
        fw = self.fw
        sb = fw.sb
        pb, BK = self.pb, self.BK
        with ExitStack() as es:
            old_es = fw.es
            fw.es = es
            self.wslots = [sb("wslotB%d" % i, [128, 8, 512], BF16) for i in range(3)]
            self.wslot_rr = 0
            x1_ = [sb("x1_t%d" % i, [128, 1024], F32) for i in range(2)]
            xn2_ = [sb("xn2%d" % i, [128, 1024], F32) for i in range(2)]
            ssb = sb("ssB", [128, 8], F32)
            xn2T_ = [sb("xn2T%d" % i, [128, 8, 128], BF16) for i in range(2)]
            gffn = sb("gffn_bc", [128, 1024], F32)
            kst = sb("kstage", [128, 16, 128], F32)
            keysT = sb("keysT", [128, 16, 128], BF16)
            qT_ = [sb("qT%d" % i, [128, 16, 128], BF16) for i in range(2)]
            S_ = [sb("S_all%d" % i, [128, 16, 128], F32) for i in range(2)]
            Sw = sb("S_work", [128, 16, 128], F32)
            v16 = sb("v16", [128, 16, 16], F32)
            i16 = sb("i16", [128, 16, 16], U32)
            i16f = sb("i16f", [128, 16, 16], F32)
            cand = sb("cand", [128, 8, 256], F32)
            candw = sb("candw", [128, 8, 256], F32)
            cv = sb("cv", [128, 8, 16], F32)
            ci = sb("ci", [128, 8, 16], U32)
            cit = sb("cit", [128, 8, 16], U32)
            iif = sb("iif", [128, 128], F32)
            jjf = sb("jjf", [128, 128], F32)
            eq = sb("eq", [128, 128, 16], F32)
            aidx = sb("aidx", [128, 128], F32)
            bidx = sb("bidx", [128, 128], F32)
            gat = sb("gat", [128, 8, 16], F32)
            gsum = sb("gsum", [128, 8], F32)
            abg = sb("abg", [128, 3, 128], F32)
            NOH = 16
            At = [sb("At%d" % i, [128, 128], BF16) for i in range(NOH)]
            Bt = [sb("Bt%d" % i, [128, 128], BF16) for i in range(NOH)]
            GGt = sb("GGt", [128, 128, 128], BF16)
            ffb = sb("ffb", [128, 128], BF16)
            self.cp(ffb[:], self.ff[:], R=["ff"], W=["ffb"])
            self.dma(gffn[:], self.g_ffn.partition_broadcast(128), W=["gffn"])
            self.dma(kst[:], self.keys.rearrange("a k d -> k a d"), W=["kst"])
            for a4 in range(4):
                b = self.bank()
                for i in range(4):
                    self.tr(pb[b][:, i * 128:(i + 1) * 128], kst[:, a4 * 4 + i, :], R=["kst"], W=[("B", b)])
                self.cp(keysT[:, a4 * 4:(a4 + 1) * 4, :], pb[b][:].rearrange("p (a b) -> p a b", a=4), R=BK(b), W=["keysT"])
            bank6 = self.bank

            def front(ti, par):
                x1, xn2, xn2T, qT, S = x1_[par], xn2_[par], xn2T_[par], qT_[par], S_[par]
                kx = lambda n: n + str(par)
                self.dma(x1[:], self.x1s[ti * 128:(ti + 1) * 128, :], W=[kx("x1")])
                self.act(xn2[:], x1[:], AF.Square, R=[kx("x1")], W=[kx("xn2"), kx("ssB0")], accum=ssb[:, 4 * par:4 * par + 1])
                self.act(ssb[:, 4 * par + 1:4 * par + 2], ssb[:, 4 * par:4 * par + 1], AF.Sqrt, R=[kx("ssB0"), "epsc0"], W=[kx("ssB1")], bias=self.epsc[:, 0:1], scale=1.0 / 1024)
                fw.op("dve", lambda e: e.reciprocal(out=ssb[:, 4 * par + 2:4 * par + 3], in_=ssb[:, 4 * par + 1:4 * par + 2]), R=[kx("ssB1")], W=[kx("ssB2")])
                self.stt(xn2[:], x1[:], ssb[:, 4 * par + 2:4 * par + 3], gffn[:], ALU.mult, ALU.mult, R=[kx("x1"), kx("ssB2"), "gffn"], W=[kx("xn2")])
                for half in range(2):
                    b = bank6()
                    for q in range(4):
                        kc = half * 4 + q
                        self.tr(pb[b][:, q * 128:(q + 1) * 128], xn2[:, kc * 128:(kc + 1) * 128], R=[kx("xn2")], W=[("B", b)])
                    self.cp(xn2T[:, half * 4:(half + 1) * 4, :], pb[b][:].rearrange("p (a b) -> p a b", a=4), R=BK(b), W=[kx("xn2T")], eng="act")
                self.dma(self.xn2Ts[:, :, ti * 128:(ti + 1) * 128], xn2T[:], R=[kx("xn2T")], W=["xn2Ts"], q="act")
                for g in range(4):
                    wsl, wk = self.wload(self.wsc_q[g])
                    b = bank6()
                    for c in range(4):
                        for kc in range(8):
                            self.mm(pb[b][:, c * 128:(c + 1) * 128], lhsT=wsl[:, kc, c * 128:(c + 1) * 128], rhs=xn2T[:, kc, :],
                                    start=(kc == 0), stop=(kc == 7), R=[wk, kx("xn2T")], W=[("B", b)])
                    self.cp(qT[:, g * 4:(g + 1) * 4, :], pb[b][:].rearrange("p (a b) -> p a b", a=4), R=BK(b), W=[(kx("qT"), g)], eng="act")
                for g in range(4):
                    b = bank6()
                    for c in range(4):
                        a = g * 4 + c
                        self.mm(pb[b][:, c * 128:(c + 1) * 128], lhsT=qT[:, a, :], rhs=keysT[:, a, :], R=[(kx("qT"), g), "keysT"], W=[("B", b)])
                    self.cp(S[:, g * 4:(g + 1) * 4, :], pb[b][:].rearrange("p (a b) -> p a b", a=4), R=BK(b), W=[(kx("S"), g)], eng="act")

            def back(ti, par):
                S = S_[par]
                kx = lambda n: n + str(par)
                for a in range(16):
                    g = a // 4
                    fw.op("dve", (lambda a: lambda e: e.max(out=v16[:, a, 0:8], in_=S[:, a, :]))(a), R=[(kx("S"), g)], W=[("v16", a)])
                    fw.op("dve", (lambda a: lambda e: e.max_index(out=i16[:, a, 0:8], in_max=v16[:, a, 0:8], in_values=S[:, a, :]))(a),
                          R=[(kx("S"), g), ("v16", a)], W=[("i16", a)])
                    fw.op("dve", (lambda a: lambda e: e.match_replace(out=Sw[:, a, :], in_to_replace=v16[:, a, 0:8], in_values=S[:, a, :],
                                                                       imm_value=-1e30))(a), R=[(kx("S"), g), ("v16", a)], W=[("Sw", a)])
                    fw.op("dve", (lambda a: lambda e: e.max(out=v16[:, a, 8:16], in_=Sw[:, a, :]))(a), R=[("Sw", a)], W=[("v16", a)])
                    fw.op("dve", (lambda a: lambda e: e.max_index(out=i16[:, a, 8:16], in_max=v16[:, a, 8:16], in_values=Sw[:, a, :]))(a),
                          R=[("Sw", a), ("v16", a)], W=[("i16", a)])
                V16 = [("v16", a) for a in range(16)]
                I16 = [("i16", a) for a in range(16)]
                self.cp(i16f[:], i16[:], R=I16, W=["i16f"])
                for h in range(8):
                    self.tt(cand[:, h, :].rearrange("p (i j) -> p i j", i=16), v16[:, 2 * h, :, None].to_broadcast([128, 16, 16]),
                            v16[:, 2 * h + 1, None, :].to_broadcast([128, 16, 16]), ALU.add, R=V16, W=[("cand", h)])
                    fw.op("dve", (lambda h: lambda e: e.max(out=cv[:, h, 0:8], in_=cand[:, h, :]))(h), R=[("cand", h)], W=[("cv", h)])
                    fw.op("dve", (lambda h: lambda e: e.max_index(out=ci[:, h, 0:8], in_max=cv[:, h, 0:8], in_values=cand[:, h, :]))(h),
                          R=[("cand", h), ("cv", h)], W=[("ci", h)])
                    fw.op("dve", (lambda h: lambda e: e.match_replace(out=candw[:, h, :], in_to_replace=cv[:, h, 0:8], in_values=cand[:, h, :],
                                                                       imm_value=-1e30))(h), R=[("cand", h), ("cv", h)], W=[("candw", h)])
                    fw.op("dve", (lambda h: lambda e: e.max(out=cv[:, h, 8:16], in_=candw[:, h, :]))(h), R=[("candw", h)], W=[("cv", h)])
                    fw.op("dve", (lambda h: lambda e: e.max_index(out=ci[:, h, 8:16], in_max=cv[:, h, 8:16], in_values=candw[:, h, :]))(h),
                          R=[("candw", h), ("cv", h)], W=[("ci", h)])
                CV = [("cv", h) for h in range(8)]
                CI = [("ci", h) for h in range(8)]
                self.tt(gat[:], cv[:], cv[:, :, 0:1].to_broadcast([128, 8, 16]), ALU.subtract, R=CV, W=["gat"])
                self.act(gat[:], gat[:], AF.Exp, R=["gat"], W=["gat"])
                fw.op("dve", lambda e: e.tensor_reduce(out=gsum[:], in_=gat[:], axis=AX.X, op=ALU.add), R=["gat"], W=["gsum"])
                fw.op("dve", lambda e: e.reciprocal(out=gsum[:], in_=gsum[:]), R=["gsum"], W=["gsum"])
                self.tt(gat[:], gat[:], gsum[:, :, None].to_broadcast([128, 8, 16]), ALU.mult, R=["gat", "gsum"], W=["gat"])
                fw.op("dve", lambda e: e.tensor_single_scalar(out=cit[:], in_=ci[:], scalar=4, op=ALU.logical_shift_right), R=CI, W=["cit"])
                self.cp(iif[:], cit[:].rearrange("p a b -> p (a b)"), R=["cit"], W=["iif"])
                fw.op("dve", lambda e: e.tensor_single_scalar(out=cit[:], in_=ci[:], scalar=15, op=ALU.bitwise_and), R=CI + ["iif"], W=["cit"])
                self.cp(jjf[:], cit[:].rearrange("p a b -> p (a b)"), R=["cit"], W=["jjf"])
                for (src, sk, half, dst, dk) in [(iif, "iif", 0, aidx, "aidx"), (jjf, "jjf", 1, bidx, "bidx")]:
                    self.tt(eq[:], src[:, :, None].to_broadcast([128, 128, 16]), self.iota16[:, None, :].to_broadcast([128, 128, 16]),
                            ALU.is_equal, R=[sk, "iota16"], W=["eq"])
                    i1 = i16f[:].rearrange("p (h c) i -> p h c i", c=2)[:, :, half, :]
                    self.tt(eq[:].rearrange("p (h k) i -> p h k i", h=8), eq[:].rearrange("p (h k) i -> p h k i", h=8),
                            i1[:, :, None, :].to_broadcast([128, 8, 16, 16]), ALU.mult, R=["eq", "i16f"], W=["eq"])
                    fw.op("dve", (lambda dst: lambda e: e.tensor_reduce(out=dst[:], in_=eq[:], axis=AX.X, op=ALU.add))(dst), R=["eq"], W=[dk])
            def onehots(ti, par):
                b = bank6()
                self.tr(pb[b][:, 0:128], aidx[:], R=["aidx"], W=[("B", b)])
                self.tr(pb[b][:, 128:256], bidx[:], R=["bidx"], W=[("B", b)])
                self.tr(pb[b][:, 256:384], gat[:].rearrange("p a b -> p (a b)"), R=["gat"], W=[("B", b)])
                self.cp(abg[:], pb[b][:, 0:384].rearrange("p (a b) -> p a b", a=3), R=BK(b), W=["abg"], eng="act")
                for t0 in range(0, 128, 4):
                    b = bank6()
                    for tt_ in range(4):
                        t = t0 + tt_
                        o = t % NOH
                        self.ts(At[o][:], ffb[:], abg[:, 0, t:t + 1], ALU.is_equal, R=["ffb", "abg"], W=["At%d" % o],
                                s2=abg[:, 2, t:t + 1], op1=ALU.mult)
                        self.ts(Bt[o][:], ffb[:], abg[:, 1, t:t + 1], ALU.is_equal, R=["ffb", "abg"], W=["Bt%d" % o])
                        self.mm(pb[b][:, tt_ * 128:(tt_ + 1) * 128], lhsT=Bt[o][:], rhs=At[o][:], R=["At%d" % o, "Bt%d" % o], W=[("B", b)])
                    self.cp(GGt[:, :, t0:t0 + 4].rearrange("p a t -> p t a"), pb[b][:].rearrange("p (t a) -> p t a", t=4),
                            R=BK(b), W=["GGt"], eng="act")
                self.dma(self.gd[ti], GGt[:], R=["GGt"], W=["gd"], q="sp")

            tl = list(self.tiles)
            front(tl[0], 0)
            for k, ti in enumerate(tl):
                back(ti, k % 2)
                if k + 1 < len(tl):
                    front(tl[k + 1], (k + 1) % 2)
                onehots(ti, k % 2)
            fw.es = old_es

    def passB2(self):
        fw = self.fw
        sb = fw.sb
        pb, BK = self.pb, self.BK
        NTl = len(self.tiles)
        with ExitStack() as es:
            old_es = fw.es
            fw.es = es
            acc = sb("acc", [128, NT, 1024], F32)
            xT = sb("xn2T_all", [128, 8, NTOK], BF16)
            GTg = sb("GTg", [128, NT, 4, 128], BF16)
            Pm = sb("Pm", [128, 4, NTOK], BF16)
            UTg = [sb("UTg%d" % i, [128, 4, 8, 128], BF16) for i in range(2)]
            Vg = [sb("Vg%d" % i, [128, 4, 1024], BF16) for i in range(2)]
            Ust = [sb("Ust%d" % i, [128, 1024], F32) for i in range(2)]
            Vst = [sb("Vst0", [128, 1024], F32)] * 2
            Pc = [sb("Pc%d" % i, [128, 512], BF16) for i in range(2)]
            gfin = sb("gfin_bc", [128, 1024], F32)
            ssb = sb("ssB2", [128, 8], F32)
            junk = Ust[0]
            self.dma(gfin[:], self.g_fin.partition_broadcast(128), W=["gfin"])
            self.dma(acc[:], self.x1s.rearrange("(n p) d -> p n d", p=128), W=["acc"], q="act")
            self.dma(xT[:], self.xn2Ts, W=["xT"], q="sp")
            blocks = []
            tl = sorted(self.tiles)
            i = 0
            nblk = -(-len(tl) // 4)
            sizes = [len(tl) // nblk + (1 if k < len(tl) % nblk else 0) for k in range(nblk)]
            for sz in sizes:
                j = i
                while j + 1 < len(tl) and tl[j + 1] == tl[j] + 1 and (j + 1 - i) < sz:
                    j += 1
                blocks.append((tl[i], j - i + 1))
                i = j + 1
            while i < len(tl):
                j = i
                while j + 1 < len(tl) and tl[j + 1] == tl[j] + 1 and (j + 1 - i) < 4:
                    j += 1
                blocks.append((tl[i], j - i + 1))
                i = j + 1
            hb = [0, 1]
            trb = [2, 3]
            accb = [[4, 5], [6, 7]]
            st_rr = 0
            pc_rr = 0
            hb_rr = 0
            for g in range(32):
                gb = g % 2
                UT, UTk = UTg[gb], "UTg%d" % gb
                V, Vk = Vg[gb], "Vg%d" % gb
                self.dma(GTg[:], self.gd[:, :, 4 * g:4 * g + 4, :].rearrange("n b a t -> b n a t"), W=["GTg"], q="sp")
                for a4 in range(4):
                    a = 4 * g + a4
                    s_ = st_rr % 2
                    st_rr += 1
                    self.dma(Ust[s_][:], self.pu[a * 128:(a + 1) * 128, :], W=["Ust%d" % s_], q="act")
                    self.dma(Vst[s_][:], self.pv[a * 128:(a + 1) * 128, :], W=["Vst0"], q="sp")
                    for half in range(2):
                        b = trb[half]
                        for q in range(4):
                            kc = half * 4 + q
                            self.tr(pb[b][:, q * 128:(q + 1) * 128], Ust[s_][:, kc * 128:(kc + 1) * 128], R=["Ust%d" % s_], W=[("B", b)])
                        self.cp(UT[:, a4, half * 4:(half + 1) * 4, :], pb[b][:].rearrange("p (a b) -> p a b", a=4), R=BK(b), W=[(UTk, a4)],
                                eng="act" if half == 0 else "dve")
                    self.cp(V[:, a4, :], Vst[s_][:], R=["Vst0"], W=[(Vk, a4)], eng="pool")
                for a4 in range(4):
                    for (t_first, nt) in blocks:
                        n = nt * 128
                        t0 = t_first * 128
                        b = hb[hb_rr % 2]
                        hb_rr += 1
                        for kc in range(8):
                            self.mm(pb[b][:, 0:n], lhsT=UT[:, a4, kc, :], rhs=xT[:, kc, t0:t0 + n], start=(kc == 0), stop=(kc == 7),
                                    R=[(UTk, a4), "xT"], W=[("B", b)])
                        pc, pck = Pc[pc_rr % 2], "Pc%d" % (pc_rr % 2)
                        pc_rr += 1
                        self.act(pc[:, 0:n], pb[b][:, 0:n], AF.Gelu, R=BK(b), W=[pck])
                        self.tt(Pm[:, a4, t0:t0 + n].rearrange("p (n t) -> p n t", t=128), pc[:, 0:n].rearrange("p (n t) -> p n t", t=128),
                                GTg[:, t_first:t_first + nt, a4, :], ALU.mult, R=[pck, "GTg"], W=[("Pm", a4)])
                for k, ti in enumerate(tl):
                    ab = accb[k % 2]
                    for dh in range(2):
                        for a4 in range(4):
                            self.mm(pb[ab[dh]][:], lhsT=Pm[:, a4, ti * 128:(ti + 1) * 128], rhs=V[:, a4, dh * 512:(dh + 1) * 512],
                                    start=(a4 == 0), stop=(a4 == 3), R=[("Pm", a4), (Vk, a4)], W=[("B", ab[dh])])
                        cs = slice(dh * 512, (dh + 1) * 512)
                        self.tt(acc[:, ti, cs], acc[:, ti, cs], pb[ab[dh]][:], ALU.add, R=BK(ab[dh]) + ["acc"], W=["acc"])
            for ti in tl:
                self.act(junk[:], acc[:, ti, :], AF.Square, R=["acc", "Ust0"], W=["Ust0", "ssB3"], accum=ssb[:, 3:4])
                self.act(ssb[:, 4:5], ssb[:, 3:4], AF.Sqrt, R=["ssB3", "epsc0"], W=["ssB4"], bias=self.epsc[:, 0:1], scale=1.0 / 1024)
                fw.op("dve", lambda e: e.reciprocal(out=ssb[:, 5:6], in_=ssb[:, 4:5]), R=["ssB4"], W=["ssB5"])
                self.stt(junk[:], acc[:, ti, :], ssb[:, 5:6], gfin[:], ALU.mult, ALU.mult, R=["acc", "ssB5", "gfin", "Ust0"], W=["Ust0"])
                self.dma(self.y[ti * 128:(ti + 1) * 128, :], junk[:], R=["Ust0"], W=["ydram"], q="sp")
            fw.es = old_es

    def _bank6(self):
        b = self.bank_rr % 6
        self.bank_rr = (b + 1) % 6
        return b


_PROG = {}


def _get_prog():
    if "p" not in _PROG:
        _PROG["p"] = Prog()
    return _PROG["p"]


def _in_maps(inputs):
    f = lambda a: np.ascontiguousarray(np.asarray(a, dtype=np.float32))
    xp = f(inputs["x_prompt"])
    xs = f(inputs["x_sample"])
    sh = f(inputs["state_rwkv_shift"])[0]
    wkv = f(inputs["state_rwkv_wkv"])[0]
    hg = f(inputs["state_hgrn"])[0]
    shared = {
        "norm_mix_g": f(inputs["norm_mix_g"])[0], "w_in": f(inputs["w_in"])[0], "rw_mu": f(inputs["rw_mu"])[0],
        "rw_w0": f(inputs["rw_w0"])[0], "rw_w2": f(inputs["rw_w2"])[0], "rw_a0": f(inputs["rw_a0"])[0],
        "rw_a2": f(inputs["rw_a2"])[0], "rw_g2": f(inputs["rw_g2"])[0], "rw_k_k": f(inputs["rw_k_k"])[0],
        "rw_k_a": f(inputs["rw_k_a"])[0], "rw_r_k": f(inputs["rw_r_k"])[0].reshape(512),
        "rw_ln_w": f(inputs["rw_ln_w"])[0], "rw_ln_b": f(inputs["rw_ln_b"])[0],
        "hg_lb_logits": f(inputs["hg_lb_logits"]), "hg_norm_g": f(inputs["hg_norm_g"])[0],
        "w_up_a": f(inputs["w_up_a"])[0], "w_up_b": f(inputs["w_up_b"])[0], "w_out": f(inputs["w_out"])[0],
        "norm_ffn_g": f(inputs["norm_ffn_g"])[0], "peer_w_q": f(inputs["peer_w_q"])[0],
        "peer_keys": f(inputs["peer_keys"])[0].reshape(16, 128, 128), "peer_u": f(inputs["peer_u"])[0],
        "peer_v": f(inputs["peer_v"])[0], "norm_final_g": f(inputs["norm_final_g"]),
    }
    maps = []
    for c in range(8):
        m = dict(shared)
        m["xin"] = np.concatenate([xp[c], xs[16 * c:16 * (c + 1)].reshape(128, 1024)], axis=0)
        m["shift0"] = sh[16 * c:16 * (c + 1)]
        m["wkv0"] = wkv[16 * c:16 * (c + 1)]
        m["hg0"] = hg[16 * c:16 * (c + 1)]
        maps.append(m)
    return maps


def kernel(**inputs):
    prog = _get_prog()
    maps = _in_maps(inputs)
    res = run_bass_kernel_spmd(prog.nc, maps, core_ids=list(range(8)))
    r = res.results
    y_p = np.stack([r[c]["y"][:2048] for c in range(8)], 0)
    y_s = np.concatenate([r[c]["y"][2048:].reshape(16, 8, 1024) for c in range(8)], 0)
    sh_p = np.stack([r[c]["o_shp"][0] for c in range(8)], 0)[None]
    wkv_p = np.stack([r[c]["o_wkvp"] for c in range(8)], 0)[None]
    hg_p = np.stack([r[c]["o_hgp"] for c in range(8)], 0)[None]
    sh_s = np.concatenate([r[c]["o_shs"] for c in range(8)], 0)[None]
    wkv_s = np.concatenate([r[c]["o_wkvs"] for c in range(8)], 0)[None]
    hg_s = np.concatenate([r[c]["o_hgs"] for c in range(8)], 0)[None]
    return tuple(np.ascontiguousarray(a, dtype=np.float32) for a in (y_p, y_s, sh_p, wkv_p, hg_p, sh_s, wkv_s, hg_s))
```
